# Optimizing a Trainium2 kernel written in Bass

```python
import math
import jax, jax.numpy as jnp
from jax import lax
import numpy as np

D_MODEL = 1024
BATCH = 2
SEQ = 8192
DEPTH = 2

GRID_W = 64
CTX_LEN = 256
N_EVEN = (DEPTH + 1) // 2
N_ODD = DEPTH // 2
ALPHA = (2 * DEPTH) ** 0.25
BETA = (8 * DEPTH) ** -0.25
ROPE_BASE = 10000.0
Q_BLOCK = 128
LN_EPS = 1e-5
RMS_EPS = 1e-6
N_MOD = 6

HEAD_DIM = 64
DIFF_HEADS = 4
DIFF_V = 2 * HEAD_DIM
NA_HEADS = 8
NA_ROWS = 8
NA_COLS = 16
MLA_HEADS = 8
MLA_Q_RANK = 384
MLA_KV_RANK = 256
MLA_NOPE = 64
MLA_ROPE = 32
MLA_V = 64
GMLP_GROUPS = 4
GMLP_GROUP_CH = 128
GMLP_CHUNK = 128
GMLP_WIDTH = GMLP_GROUPS * GMLP_GROUP_CH
FFN_HIDDEN = -(-8 * D_MODEL // (3 * 256)) * 256

EVEN_SPLITS = [DIFF_HEADS * 2 * HEAD_DIM, DIFF_HEADS * 2 * HEAD_DIM, DIFF_HEADS * DIFF_V,
               NA_HEADS * HEAD_DIM, NA_HEADS * HEAD_DIM, NA_HEADS * HEAD_DIM]
EVEN_IN = sum(EVEN_SPLITS)
EVEN_MIX = DIFF_HEADS * DIFF_V + NA_HEADS * HEAD_DIM
ODD_SPLITS = [MLA_Q_RANK, MLA_KV_RANK, MLA_ROPE, GMLP_WIDTH, GMLP_WIDTH]
ODD_IN = sum(ODD_SPLITS)
ODD_MIX = MLA_HEADS * MLA_V + GMLP_WIDTH

kernel_name = "hybrid_diffnat_mla_gmlp_prefix_dit"


def layer_norm(x, g, b):
    xf = x.astype(jnp.float32)
    mu = jnp.mean(xf, -1, keepdims=True)
    var = jnp.mean(jnp.square(xf - mu), -1, keepdims=True)
    return ((xf - mu) * lax.rsqrt(var + LN_EPS) * g.astype(jnp.float32) + b.astype(jnp.float32)).astype(x.dtype)


def rms_norm(x, g):
    xf = x.astype(jnp.float32)
    ms = jnp.mean(jnp.square(xf), -1, keepdims=True)
    return (xf * lax.rsqrt(ms + RMS_EPS) * g.astype(jnp.float32)).astype(x.dtype)


def rope_1d(x, pos):
    d = x.shape[-1]
    inv = jnp.power(ROPE_BASE, -jnp.arange(0, d, 2, dtype=jnp.float32) / d)
    ang = pos.astype(jnp.float32)[:, None] * inv[None, :]
    cos, sin = jnp.cos(ang).astype(x.dtype), jnp.sin(ang).astype(x.dtype)
    x1, x2 = x[..., : d // 2], x[..., d // 2:]
    return jnp.concatenate([x1 * cos - x2 * sin, x2 * cos + x1 * sin], -1)


def rope_2d(x, row, col):
    h = x.shape[-1] // 2
    return jnp.concatenate([rope_1d(x[..., :h], row), rope_1d(x[..., h:], col)], -1)


def split_cols(p, sizes):
    return jnp.split(p, np.cumsum(sizes)[:-1].tolist(), axis=-1)


def heads(t, n):
    b, s, _ = t.shape
    return t.reshape(b, s, n, -1).transpose(0, 2, 1, 3)


def diff_heads(t):
    b, s, _ = t.shape
    return t.reshape(b, s, DIFF_HEADS, 2, HEAD_DIM).transpose(0, 2, 3, 1, 4)


def merge(o):
    b, h, s, d = o.shape
    return o.transpose(0, 2, 1, 3).reshape(b, s, h * d)


def softmax_attention(q, k, v, scale):
    b, h, t, dq = q.shape
    nb = t // Q_BLOCK
    qb = q.reshape(b, h, nb, Q_BLOCK, dq).transpose(2, 0, 1, 3, 4)

    def block(qi):
        s = jnp.einsum('bhqd,bhkd->bhqk', qi, k).astype(jnp.float32) * scale
        p = jax.nn.softmax(s, axis=-1).astype(v.dtype)
        return jnp.einsum('bhqk,bhkd->bhqd', p, v)

    o = lax.map(block, qb)
    return o.transpose(1, 2, 0, 3, 4).reshape(b, h, t, v.shape[-1])


def diff_attention(q, k, v, lam):
    b, h, _, t, dh = q.shape
    nb = t // Q_BLOCK
    scale = dh ** -0.5
    qb = q.reshape(b, h, 2, nb, Q_BLOCK, dh).transpose(3, 0, 1, 2, 4, 5)

    def block(qi):
        s = jnp.einsum('bhmqd,bhmkd->bhmqk', qi, k).astype(jnp.float32) * scale
        p = jax.nn.softmax(s, axis=-1)
        a = (p[:, :, 0] - lam * p[:, :, 1]).astype(v.dtype)
        return jnp.einsum('bhqk,bhkd->bhqd', a, v)

    o = lax.map(block, qb)
    return o.transpose(1, 2, 0, 3, 4).reshape(b, h, t, v.shape[-1])


def neighbourhood_attention(q, k, v, k_ctx, v_ctx, rpb):
    b, h, s, dh = q.shape
    rows = s // GRID_W
    wr = min(NA_ROWS, rows)
    wc = NA_COLS
    nw = wr * wc
    scale = dh ** -0.5
    kg = k.reshape(b, h, rows, GRID_W, dh)
    vg = v.reshape(b, h, rows, GRID_W, dh)
    r = jnp.arange(rows)
    cidx = jnp.arange(GRID_W)
    row_start = jnp.clip(r - wr // 2, 0, rows - wr)
    col_start = jnp.clip(cidx - wc // 2, 0, GRID_W - wc)
    col_keys = col_start[:, None] + jnp.arange(wc)[None, :]
    row_off = row_start[:, None] + jnp.arange(wr)[None, :] - r[:, None] + (NA_ROWS - 1)
    col_off = col_keys - cidx[:, None] + (NA_COLS - 1)
    qr = q.reshape(b, h, rows, GRID_W, dh).transpose(2, 0, 1, 3, 4)

    def row_block(args):
        q_row, start, roff = args
        k_rows = lax.dynamic_slice_in_dim(kg, start, wr, axis=2)
        v_rows = lax.dynamic_slice_in_dim(vg, start, wr, axis=2)
        k_win = k_rows[:, :, :, col_keys].transpose(0, 1, 3, 2, 4, 5).reshape(b, h, GRID_W, nw, dh)
        v_win = v_rows[:, :, :, col_keys].transpose(0, 1, 3, 2, 4, 5).reshape(b, h, GRID_W, nw, dh)
        bias = rpb[:, roff[:, None, None], col_off[None, :, :]]
        bias = bias.transpose(0, 2, 1, 3).reshape(h, GRID_W, nw).astype(jnp.float32)
        s_win = jnp.einsum('bhqd,bhqkd->bhqk', q_row, k_win).astype(jnp.float32) * scale + bias
        s_ctx = jnp.einsum('bhqd,bhkd->bhqk', q_row, k_ctx).astype(jnp.float32) * scale
        p = jax.nn.softmax(jnp.concatenate([s_win, s_ctx], -1), axis=-1).astype(v.dtype)
        return (jnp.einsum('bhqk,bhqkd->bhqd', p[..., :nw], v_win)
                + jnp.einsum('bhqk,bhkd->bhqd', p[..., nw:], v_ctx))

    o = lax.map(row_block, (qr, row_start, row_off))
    return o.transpose(1, 2, 0, 3, 4).reshape(b, h, s, dh)


def even_mixer(p_x, p_c, w_out, diff_lam, diff_subln_g, na_rpb, layer_idx, need_ctx):
    s = p_x.shape[1]
    t = jnp.arange(s)
    row, col = t // GRID_W, t % GRID_W
    aq_x, ak_x, av_x, bq_x, bk_x, bv_x = split_cols(p_x, EVEN_SPLITS)
    aq_c, ak_c, av_c, bq_c, bk_c, bv_c = split_cols(p_c, EVEN_SPLITS)
    lambda_init = 0.8 - 0.6 * math.exp(-0.3 * layer_idx)
    lf = diff_lam.astype(jnp.float32)
    lam = jnp.exp(jnp.sum(lf[0] * lf[1])) - jnp.exp(jnp.sum(lf[2] * lf[3])) + lambda_init

    def diff_out(o):
        return merge(rms_norm(o, diff_subln_g) * (1.0 - lambda_init))

    ka_c, va_c = diff_heads(ak_c), heads(av_c, DIFF_HEADS)
    kb_c, vb_c = heads(bk_c, NA_HEADS), heads(bv_c, NA_HEADS)
    qa_x = rope_2d(diff_heads(aq_x), row, col)
    ka_x = rope_2d(diff_heads(ak_x), row, col)
    oa_x = diff_attention(qa_x, jnp.concatenate([ka_x, ka_c], axis=3),
                          jnp.concatenate([heads(av_x, DIFF_HEADS), va_c], axis=2), lam)
    ob_x = neighbourhood_attention(heads(bq_x, NA_HEADS), heads(bk_x, NA_HEADS), heads(bv_x, NA_HEADS),
                                   kb_c, vb_c, na_rpb)
    y_x = jnp.concatenate([diff_out(oa_x), merge(ob_x)], -1) @ w_out
    if not need_ctx:
        return y_x, None
    oa_c = diff_attention(diff_heads(aq_c), ka_c, va_c, lam)
    ob_c = softmax_attention(heads(bq_c, NA_HEADS), kb_c, vb_c, HEAD_DIM ** -0.5)
    y_c = jnp.concatenate([diff_out(oa_c), merge(ob_c)], -1) @ w_out
    return y_x, y_c


def odd_mixer(p_x, p_c, w_out, mla_q_norm_g, mla_w_uq, mla_kv_norm_g, mla_w_ukv,
              gmlp_ln_g, gmlp_ln_b, gmlp_ws, gmlp_b, need_ctx):
    s = p_x.shape[1]
    t = jnp.arange(s)
    pos = (t // GRID_W, t % GRID_W)
    cq_x, ckv_x, kr_x, gu_x, gv_x = split_cols(p_x, ODD_SPLITS)
    cq_c, ckv_c, kr_c, gu_c, gv_c = split_cols(p_c, ODD_SPLITS)
    scale = (MLA_NOPE + MLA_ROPE) ** -0.5

    def mla_q(cq, rpos):
        q = heads(rms_norm(cq, mla_q_norm_g) @ mla_w_uq, MLA_HEADS)
        q_nope, q_rope = q[..., :MLA_NOPE], q[..., MLA_NOPE:]
        if rpos is not None:
            q_rope = rope_2d(q_rope, *rpos)
        return jnp.concatenate([q_nope, q_rope], -1)

    def mla_kv(ckv, kr, rpos):
        kv = heads(rms_norm(ckv, mla_kv_norm_g) @ mla_w_ukv, MLA_HEADS)
        k_nope, v = kv[..., :MLA_NOPE], kv[..., MLA_NOPE:]
        k_rope = kr[:, None]
        if rpos is not None:
            k_rope = rope_2d(k_rope, *rpos)
        k = jnp.concatenate([k_nope, jnp.broadcast_to(k_rope, k_nope.shape[:-1] + (MLA_ROPE,))], -1)
        return k, v

    def chunk_gmlp(gu, gv):
        b, tl, _ = gu.shape
        u = jax.nn.gelu(gu, approximate=False)
        v = layer_norm(jax.nn.gelu(gv, approximate=False), gmlp_ln_g, gmlp_ln_b)
        v = v.reshape(b, tl // GMLP_CHUNK, GMLP_CHUNK, GMLP_GROUPS, GMLP_GROUP_CH)
        mixed = jnp.einsum('gij,bnjgc->bnigc', gmlp_ws, v) + gmlp_b.T[:, :, None]
        return u * mixed.reshape(b, tl, GMLP_WIDTH)

    k_c, v_c = mla_kv(ckv_c, kr_c, None)
    k_x, v_x = mla_kv(ckv_x, kr_x, pos)
    oc_x = softmax_attention(mla_q(cq_x, pos), jnp.concatenate([k_x, k_c], axis=2),
                             jnp.concatenate([v_x, v_c], axis=2), scale)
    y_x = jnp.concatenate([merge(oc_x), chunk_gmlp(gu_x, gv_x)], -1) @ w_out
    if not need_ctx:
        return y_x, None
    oc_c = softmax_attention(mla_q(cq_c, None), k_c, v_c, scale)
    y_c = jnp.concatenate([merge(oc_c), chunk_gmlp(gu_c, gv_c)], -1) @ w_out
    return y_x, y_c


def modulation(cond, w, b):
    return (jax.nn.silu(cond) @ w + b).reshape(cond.shape[:-1] + (N_MOD, cond.shape[-1]))


def modulate(h, m, k):
    return h * (1.0 + m[..., k + 1, :]) + m[..., k, :]


def swiglu(u, w_in, w_out):
    g, a = jnp.split(u @ w_in, 2, axis=-1)
    return (jax.nn.silu(g) * a) @ w_out


def setup_inputs(seed: int = 0) -> dict:
    key = jax.random.key(seed)
    ks = iter(jax.random.split(key, 32))

    def nrm(shape, s):
        return s * jax.random.normal(next(ks), shape, jnp.float32)

    D = D_MODEL
    return {
        "x": nrm((BATCH, SEQ, D), 1.0),
        "c": nrm((BATCH, D), 1.0),
        "ctx": nrm((BATCH, CTX_LEN, D), 1.0),
        "c_ctx": nrm((D,), 1.0),
        "mod_w": nrm((DEPTH, D, N_MOD * D), D ** -0.5),
        "mod_b": nrm((DEPTH, N_MOD * D), 0.01),
        "ln_mix_g": 1.0 + nrm((DEPTH, D), 0.02),
        "ln_mix_b": nrm((DEPTH, D), 0.02),
        "ln_ffn_g": 1.0 + nrm((DEPTH, D), 0.02),
        "ln_ffn_b": nrm((DEPTH, D), 0.02),
        "ffn_w_in": nrm((DEPTH, D, 2 * FFN_HIDDEN), D ** -0.5),
        "ffn_w_out": nrm((DEPTH, FFN_HIDDEN, D), BETA * FFN_HIDDEN ** -0.5),
        "ev_w_in": nrm((N_EVEN, D, EVEN_IN), D ** -0.5),
        "ev_w_out": nrm((N_EVEN, EVEN_MIX, D), BETA * EVEN_MIX ** -0.5),
        "diff_lambda": nrm((N_EVEN, 4, HEAD_DIM), 0.1),
        "diff_subln_g": 1.0 + nrm((N_EVEN, DIFF_V), 0.02),
        "na_rpb": nrm((N_EVEN, NA_HEADS, 2 * NA_ROWS - 1, 2 * NA_COLS - 1), 0.05),
        "od_w_in": nrm((N_ODD, D, ODD_IN), D ** -0.5),
        "od_w_out": nrm((N_ODD, ODD_MIX, D), BETA * ODD_MIX ** -0.5),
        "mla_q_norm_g": 1.0 + nrm((N_ODD, MLA_Q_RANK), 0.02),
        "mla_w_uq": nrm((N_ODD, MLA_Q_RANK, MLA_HEADS * (MLA_NOPE + MLA_ROPE)), MLA_Q_RANK ** -0.5),
        "mla_kv_norm_g": 1.0 + nrm((N_ODD, MLA_KV_RANK), 0.02),
        "mla_w_ukv": nrm((N_ODD, MLA_KV_RANK, MLA_HEADS * (MLA_NOPE + MLA_V)), MLA_KV_RANK ** -0.5),
        "gmlp_ln_g": 1.0 + nrm((N_ODD, GMLP_WIDTH), 0.02),
        "gmlp_ln_b": nrm((N_ODD, GMLP_WIDTH), 0.02),
        "gmlp_ws": nrm((N_ODD, GMLP_GROUPS, GMLP_CHUNK, GMLP_CHUNK), GMLP_CHUNK ** -0.5),
        "gmlp_b": 1.0 + nrm((N_ODD, GMLP_GROUPS, GMLP_CHUNK), 0.02),
    }


def reference(x, c, ctx, c_ctx, mod_w, mod_b, ln_mix_g, ln_mix_b, ln_ffn_g, ln_ffn_b,
              ffn_w_in, ffn_w_out, ev_w_in, ev_w_out, diff_lambda, diff_subln_g, na_rpb,
              od_w_in, od_w_out, mla_q_norm_g, mla_w_uq, mla_kv_norm_g, mla_w_ukv,
              gmlp_ln_g, gmlp_ln_b, gmlp_ws, gmlp_b):
    h_x, h_c = x, ctx
    for i in range(DEPTH):
        last = i == DEPTH - 1
        j = i // 2
        m_x = modulation(c, mod_w[i], mod_b[i])[:, None]
        m_c = modulation(c_ctx, mod_w[i], mod_b[i])[None, None]
        u_x, u_c = modulate(h_x, m_x, 0), modulate(h_c, m_c, 0)
        if i % 2 == 0:
            y_x, y_c = even_mixer(u_x @ ev_w_in[j], u_c @ ev_w_in[j], ev_w_out[j], diff_lambda[j],
                                  diff_subln_g[j], na_rpb[j], i, not last)
        else:
            y_x, y_c = odd_mixer(u_x @ od_w_in[j], u_c @ od_w_in[j], od_w_out[j], mla_q_norm_g[j],
                                 mla_w_uq[j], mla_kv_norm_g[j], mla_w_ukv[j], gmlp_ln_g[j],
                                 gmlp_ln_b[j], gmlp_ws[j], gmlp_b[j], not last)
        h_x = layer_norm(ALPHA * h_x + m_x[..., 2, :] * y_x, ln_mix_g[i], ln_mix_b[i])
        y_x = swiglu(modulate(h_x, m_x, 3), ffn_w_in[i], ffn_w_out[i])
        h_x = layer_norm(ALPHA * h_x + m_x[..., 5, :] * y_x, ln_ffn_g[i], ln_ffn_b[i])
        if not last:
            h_c = layer_norm(ALPHA * h_c + m_c[..., 2, :] * y_c, ln_mix_g[i], ln_mix_b[i])
            y_c = swiglu(modulate(h_c, m_c, 3), ffn_w_in[i], ffn_w_out[i])
            h_c = layer_norm(ALPHA * h_c + m_c[..., 5, :] * y_c, ln_ffn_g[i], ln_ffn_b[i])
    return h_x
```

```python
import ml_dtypes
import contextlib
import numpy as np
import concourse.bass as bass
import concourse.mybir as mybir
from concourse.bass_utils import run_bass_kernel_spmd

F32 = mybir.dt.float32
BF16 = mybir.dt.bfloat16
AF = mybir.ActivationFunctionType
ALU = mybir.AluOpType
AX = mybir.AxisListType

SAME_ENGINE_SYNC = True


class Buf:
    _n = 0

    def __init__(self, name=""):
        Buf._n += 1
        self.name = f"{name}_{Buf._n}"
        self.last_w = None
        self.rd_eng = {}
        self.rd_dma = []
        self.sem = None
        self.cnt = 0
        self.excl = False


class Prog:
    ENGS = ("pe", "act", "dve", "pool", "sp")

    def __init__(self, nc):
        self.nc = nc
        self.q = {e: [] for e in self.ENGS}
        self.slots = []
        self.free_slots = []
        self.epoch_bufs = []
        self.last_ev = {e: None for e in self.ENGS}
        self.open_dma = []

    def _deps(self, reads, writes):
        deps = []
        for b in reads:
            if b.last_w is not None:
                deps.append(b.last_w)
            if b.excl:
                deps.extend(b.rd_eng.values())
        for b in writes:
            if b.last_w is not None:
                deps.append(b.last_w)
            deps.extend(b.rd_eng.values())
            deps.extend(b.rd_dma)
        return deps

    def _commit(self, ev, reads, writes):
        for b in reads:
            if ev[0] == "e":
                b.rd_eng[ev[1]] = ev
            else:
                b.rd_dma.append(ev)
        for b in writes:
            b.last_w = ev
            b.rd_eng = {}
            b.rd_dma = []

    def op(self, eng, fn, reads=(), writes=(), extra=()):
        deps = self._deps(reads, writes) + list(extra)
        idx = len(self.q[eng])
        ev = ("e", eng, idx)
        self.q[eng].append({"fn": fn, "deps": deps, "needed": False, "sem": None})
        self._commit(ev, reads, writes)
        if fn is not None:
            self.last_ev[eng] = ev
        return ev

    def dma(self, eng, fn, reads=(), writes=(), sembuf=None, inc=16, extra=(), dedicated=False, defer=False):
        deps = self._deps(reads, writes) + list(extra)
        sb = sembuf if sembuf is not None else writes[0]
        if dedicated:
            sb = Buf("dedicated")
            sb.sem = {"cnt": 0, "sem": None, "id": len(self.slots)}
            self.slots.append(sb.sem)
        if sb.sem is None:
            if self.free_slots:
                sb.sem = self.free_slots.pop()
            else:
                sb.sem = {"cnt": 0, "sem": None, "id": len(self.slots)}
                self.slots.append(sb.sem)
            self.epoch_bufs.append(sb)
        slot = sb.sem
        slot["cnt"] += inc
        ev = ("d", slot, slot["cnt"])
        self.q[eng].append({"fn": fn, "deps": deps, "needed": False, "sem": slot, "inc": inc})
        self._commit(ev, reads, writes)
        if not defer:
            self.open_dma.append(ev)
        return ev

    def barrier(self):
        mx = {}
        for d in self.open_dma:
            k = d[1]["id"]
            if k not in mx or mx[k][2] < d[2]:
                mx[k] = d
        evs = [v for v in self.last_ev.values() if v is not None] + list(mx.values())
        self.open_dma = []
        for b in self.epoch_bufs:
            self.free_slots.append(b.sem)
            b.sem = None
        self.epoch_bufs = []
        for e in self.ENGS:
            self.op(e, None, extra=evs)

    def emit(self, final_events=()):
        nc = self.nc
        skip_same = lambda d, e: d[0] == "e" and d[1] == e and (e == "pe" or not SAME_ENGINE_SYNC)
        for e in self.ENGS:
            for o in self.q[e]:
                for d in o["deps"]:
                    if d[0] == "e" and not skip_same(d, e):
                        self.q[d[1]][d[2]]["needed"] = True
        for d in final_events:
            if d[0] == "e":
                self.q[d[1]][d[2]]["needed"] = True
        for e in self.ENGS:
            for i, o in enumerate(self.q[e]):
                if o["fn"] is None and o["needed"]:
                    raise RuntimeError("wait-only op used as dependency")
        for e in self.ENGS:
            c = 0
            for o in self.q[e]:
                if o["needed"]:
                    c += 1
                    o["val"] = c
        self.n_wait = 0
        self.n_ins = 0
        with contextlib.ExitStack() as st:
            esem = {e: st.enter_context(nc.semaphore(f"s_{e}")) for e in self.ENGS}
            for sl in self.slots:
                sl["sem"] = st.enter_context(nc.semaphore(f"d_slot{sl['id']}"))
            block = st.enter_context(nc.Block())

            def run(e, engobj, extra_final=()):
                waited = {}

                def wait_for(d):
                    if skip_same(d, e):
                        return
                    if d[0] == "e":
                        key = ("e", d[1])
                        sem = esem[d[1]]
                        val = self.q[d[1]][d[2]]["val"]
                    else:
                        key = ("d", d[1]["id"])
                        sem = d[1]["sem"]
                        val = d[2]
                    if waited.get(key, 0) >= val:
                        return
                    waited[key] = val
                    engobj.wait_ge(sem, val)
                    self.n_wait += 1

                for o in self.q[e]:
                    for d in o["deps"]:
                        wait_for(d)
                    if o["fn"] is None:
                        continue
                    ins = o["fn"](engobj)
                    self.n_ins += 1
                    if o["sem"] is not None:
                        if o["inc"] == 1:
                            ins.then_inc(o["sem"]["sem"])
                        else:
                            ins.then_inc(o["sem"]["sem"], o["inc"])
                    elif o["needed"]:
                        ins.then_inc(esem[e], 1)
                for d in extra_final:
                    wait_for(d)

            @block.tensor
            def _(t):
                run("pe", t)

            @block.scalar
            def _(s):
                run("act", s)

            @block.vector
            def _(v):
                run("dve", v)

            @block.gpsimd
            def _(g):
                run("pool", g)

            @block.sync
            def _(s):
                run("sp", s, final_events)


D = 1024
KC = 8
SEQ = 8192
LCTX = 256
TOWN = 2048
NOWN = TOWN + LCTX
NKEY = SEQ + LCTX
NKT = NKEY // 128
NNA = 2816
ALPHA = 4.0 ** 0.25
LN_EPS = 1e-5
RMS_EPS = 1e-6
NEG = -30000.0
FFN_H = 2816
HC = FFN_H // 128
NVEC = 24


def _prod(xs):
    r = 1
    for x in xs:
        r *= x
    return r


class Arena:
    def __init__(self, nc, base=16384, limit=16384 + 212000):
        self.nc, self.base, self.top, self.limit = nc, base, base, limit
        self.n = 0
        self.peak = base

    def alloc(self, name, shape, dtype):
        sz = 2 if dtype == BF16 else 4
        nbytes = _prod(shape[1:]) * sz
        off = (self.top + 31) // 32 * 32
        self.top = off + nbytes
        self.peak = max(self.peak, self.top)
        assert self.top <= self.limit, f"SBUF arena overflow at {name}: {self.top - self.base}"
        self.n += 1
        h = self.nc.alloc_sbuf_tensor_at(f"{name}_{self.n}", list(shape), dtype, offset=off)
        t = TT(h)
        t.off = off
        return t

    def alloc_at(self, name, shape, dtype, off):
        self.n += 1
        h = self.nc.alloc_sbuf_tensor_at(f"{name}_{self.n}", list(shape), dtype, offset=off)
        t = TT(h)
        t.off = off
        return t

    def mark(self):
        return self.top

    def release(self, m):
        self.top = m


class V:
    def __init__(self, ap, buf):
        self.ap, self.buf = ap, buf


class TT:
    def __init__(self, h, buf=None):
        self.h = h
        self.ap = h.ap()
        self.buf = buf if buf is not None else Buf(h.name)

    def __getitem__(self, k):
        return self.ap[k]


class Ring:
    def __init__(self, arena, name, shape, dtype, n):
        self.items = [arena.alloc(f"{name}{i}", shape, dtype) for i in range(n)]
        self.i = 0

    def next(self):
        t = self.items[self.i % len(self.items)]
        self.i += 1
        return t


class KB:
    def __init__(self, mode):
        self.mode = mode
        nc = bass.Bass("TRN2", target_bir_lowering=False)
        self.nc = nc
        self.P = Prog(nc)
        self.A = Arena(nc)
        self.din = {}
        self.dbuf = {}
        self.pp = []
        for i in range(4):
            h = nc.alloc_psum_tensor(f"ps{i}", [128, 1024], F32)
            self.pp.append((h.ap(), [Buf(f"ps{i}a"), Buf(f"ps{i}b")]))
            for b in self.pp[-1][1]:
                b.excl = True

    def bank(self, k):
        ap, bufs = self.pp[k // 2]
        j = k % 2
        return ap[:, j * 512:(j + 1) * 512], bufs[j]

    def inp(self, name, shape, dtype=F32):
        t = self.nc.dram_tensor(name, list(shape), dtype, kind="ExternalInput")
        self.din[name] = t.ap()
        self.dbuf[name] = Buf(name)
        return t.ap()

    def outp(self, name, shape, dtype=F32):
        t = self.nc.dram_tensor(name, list(shape), dtype, kind="ExternalOutput")
        self.din[name] = t.ap()
        self.dbuf[name] = Buf(name)
        return t.ap()

    def scratch(self, name, shape, dtype):
        t = self.nc.dram_tensor(name, list(shape), dtype)
        self.din[name] = t.ap()
        self.dbuf[name] = Buf(name)
        return t.ap()

    def mm(self, out, lhsT, rhs, start, stop, rd, wr):
        self.P.op("pe", lambda e: e.matmul(out, lhsT=lhsT, rhs=rhs, start=start, stop=stop),
                  reads=rd, writes=wr)

    def act(self, out, in_, func, rd, wr, bias=None, scale=None):
        kw = {}
        if bias is not None:
            kw["bias"] = bias
        if scale is not None:
            kw["scale"] = scale
        return self.P.op("act", lambda e: e.activation(out=out, in_=in_, func=func, **kw),
                         reads=rd, writes=wr)

    def tt(self, out, in0, in1, op, rd, wr, eng="dve"):
        return self.P.op(eng, lambda e: e.tensor_tensor(out=out, in0=in0, in1=in1, op=op),
                         reads=rd, writes=wr)

    def ts(self, out, in0, s1, s2, op0, op1, rd, wr, eng="dve"):
        if op1 is None:
            return self.P.op(eng, lambda e: e.tensor_scalar(out=out, in0=in0, scalar1=s1, scalar2=None, op0=op0),
                             reads=rd, writes=wr)
        return self.P.op(eng, lambda e: e.tensor_scalar(out=out, in0=in0, scalar1=s1, scalar2=s2, op0=op0, op1=op1),
                         reads=rd, writes=wr)

    def stt(self, out, in0, scalar, in1, op0, op1, rd, wr):
        return self.P.op("dve", lambda e: e.scalar_tensor_tensor(out=out, in0=in0, scalar=scalar, in1=in1, op0=op0, op1=op1),
                         reads=rd, writes=wr)

    def cp(self, out, in_, rd, wr, eng="dve"):
        if eng == "act":
            return self.P.op("act", lambda e: e.copy(out=out, in_=in_), reads=rd, writes=wr)
        return self.P.op(eng, lambda e: e.tensor_copy(out=out, in_=in_), reads=rd, writes=wr)

    def recip(self, out, in_, rd, wr):
        return self.P.op("dve", lambda e: e.reciprocal(out=out, in_=in_), reads=rd, writes=wr)

    def memset(self, t, val, eng="dve"):
        ap = t.ap
        return self.P.op(eng, lambda e: e.memset(ap, val), writes=[t.buf])

    def load(self, t_ap, t_buf, src_ap, src_buf, q="sp", slow=False):
        kw = {"allow_slow_non_contiguous": True} if slow else {}
        return self.P.dma(q, lambda e: e.dma_start(out=t_ap, in_=src_ap, **kw), reads=[src_buf], writes=[t_buf])

    def store(self, dst_ap, dst_buf, t_ap, t_buf, q="sp"):
        return self.P.dma(q, lambda e: e.dma_start(out=dst_ap, in_=t_ap), reads=[t_buf], writes=[dst_buf], sembuf=t_buf)

    def declare_inputs(self):
        I = self.inp
        mode = self.mode
        I("vecs", [NVEC, D]); I("ident", [128, 128]); I("mod_w", [2, D, 6 * D])
        I("ffn_w_in", [2, D, 2 * FFN_H]); I("ffn_w_out", [2, FFN_H, D])
        if mode in ("A", "F", "dbg"):
            I("ctxb", [LCTX, D]); I("xown", [TOWN, D]); I("xna", [NNA, D])
            I("ev_w_in", [D, 3072]); I("ev_w_in_sw", [D, 1024]); I("ev_w_out", [D, D])
            I("diff_lambda", [256]); I("diff_subln_g", [128])
            I("rope_cd_own", [128, TOWN]); I("rope_sd_own", [128, TOWN])
            I("na_tb", [128, 8 * 14 * 64]); I("na_qm", [64, TOWN], BF16); I("na_km", [64, NNA], BF16)
        I("od_w_in", [D, 1696]); I("od_w_in_krsw", [D, 32])
        I("ropeq_c_own", [96, TOWN]); I("ropeq_s_own", [96, TOWN])
        if mode in ("B", "F", "dbg"):
            I("od_w_out", [D, D])
            I("mla_w_uq", [384, 768]); I("mla_w_uq_sw", [384, 768]); I("mla_w_ukv", [256, 1024])
            I("gmlp_ln_g", [512]); I("gmlp_ln_b", [512]); I("gmlp_ws", [4, 128, 128]); I("gmlp_b", [512])

    def phase_const(self):
        A, P = self.A, self.P
        self.ident = A.alloc("ident", [128, 128], F32)
        self.load(self.ident.ap, self.ident.buf, self.din["ident"], self.dbuf["ident"])
        self.ones_f = A.alloc("ones_f", [128, 128], F32)
        self.memset(self.ones_f, 1.0)
        self.ones_b = A.alloc("ones_b", [128, 128], BF16)
        self.memset(self.ones_b, 1.0)
        self.eps_rms = A.alloc("eps_rms", [128, 1], F32)
        self.memset(self.eps_rms, RMS_EPS)
        self.eps_ln = A.alloc("eps_ln", [128, 1], F32)
        self.memset(self.eps_ln, LN_EPS / (ALPHA * ALPHA))
        self.vT = A.alloc("vT", [128, KC, NVEC], F32)
        self.sT = A.alloc("sT", [128, KC, 2], BF16)
        m0 = A.mark()
        vs = A.alloc("vecs_sb", [NVEC, D], F32)
        self.load(vs.ap, vs.buf, self.din["vecs"], self.dbuf["vecs"])
        bap, bb = self.bank(0)
        for kc in range(KC):
            self.mm(bap[:, kc * NVEC:(kc + 1) * NVEC], vs.ap[0:NVEC, kc * 128:(kc + 1) * 128],
                    self.ident.ap[0:NVEC, 0:NVEC], True, True, [vs.buf, self.ident.buf], [bb])
        self.cp(self.vT.ap, bap[:, 0:KC * NVEC].rearrange("p (k r) -> p k r", r=NVEC), [bb], [self.vT.buf])
        self.act(self.sT.ap, self.vT.ap[:, :, 22:24], AF.Silu, [self.vT.buf], [self.sT.buf])
        P.barrier()
        A.release(m0)

    def phase_mod(self):
        A, P = self.A, self.P
        self.m = [A.alloc(f"m{l}", [128, 6, KC, 2], F32) for l in range(2)]
        m0 = A.mark()
        ring = Ring(A, "modw", [128, KC, 1024], BF16, 2)
        for l in range(2):
            bap, bb = self.bank(l)
            for g in range(6):
                w = ring.next()
                src = self.din["mod_w"][l, :, g * 1024:(g + 1) * 1024].rearrange("(k p) n -> p k n", p=128)
                for hh in range(2):
                    self.load(w.ap[:, hh * 4:(hh + 1) * 4, :], w.buf, src[:, hh * 4:(hh + 1) * 4, :], self.dbuf["mod_w"], q="pool")
                for oc in range(8):
                    col = (g * 8 + oc) * 2
                    for kc in range(KC):
                        self.mm(bap[:, col:col + 2], w.ap[:, kc, oc * 128:(oc + 1) * 128], self.sT.ap[:, kc, :],
                                kc == 0, kc == KC - 1, [w.buf, self.sT.buf], [bb])
            pv = bap[:, 0:96].rearrange("p (g k r) -> p g k r", g=6, k=8)
            for g in range(6):
                for r in range(2):
                    self.tt(self.m[l].ap[:, g, :, r], pv[:, g, :, r], self.vT.ap[:, :, 8 + l * 6 + g], ALU.add,
                            [bb, self.vT.buf], [self.m[l].buf])
        A.release(m0)
        def newv(name):
            return A.alloc(name, [128, KC, 2], F32)
        self.sc1, self.gt1, self.sc2, self.gt2 = [], [], [], []
        self.sh1 = [V(self.m[l].ap[:, 0], self.m[l].buf) for l in range(2)]
        self.G2, self.B2, self.G3, self.B3 = [], [], [], []
        for l in range(2):
            m = self.m[l]
            sc1 = newv(f"sc1_{l}"); self.ts(sc1.ap, m.ap[:, 1], 1.0, None, ALU.add, None, [m.buf], [sc1.buf])
            gt1 = newv(f"gt1_{l}"); self.ts(gt1.ap, m.ap[:, 2], 1.0 / ALPHA, None, ALU.mult, None, [m.buf], [gt1.buf])
            sc2 = newv(f"sc2_{l}"); self.ts(sc2.ap, m.ap[:, 4], 1.0, None, ALU.add, None, [m.buf], [sc2.buf])
            gt2 = newv(f"gt2_{l}"); self.ts(gt2.ap, m.ap[:, 5], 1.0 / ALPHA, None, ALU.mult, None, [m.buf], [gt2.buf])
            self.sc1.append(sc1); self.gt1.append(gt1); self.sc2.append(sc2); self.gt2.append(gt2)
        for l in range(2):
            m = self.m[l]
            G2 = newv(f"G2_{l}"); B2 = newv(f"B2_{l}")
            for r in range(2):
                self.tt(G2.ap[:, :, r], self.sc2[l].ap[:, :, r], self.vT.ap[:, :, 4 * l + 0], ALU.mult,
                        [self.sc2[l].buf, self.vT.buf], [G2.buf])
                self.tt(B2.ap[:, :, r], self.sc2[l].ap[:, :, r], self.vT.ap[:, :, 4 * l + 1], ALU.mult,
                        [self.sc2[l].buf, self.vT.buf], [B2.buf])
            self.tt(B2.ap, B2.ap, m.ap[:, 3], ALU.add, [B2.buf, m.buf], [B2.buf])
            self.G2.append(G2); self.B2.append(B2)
        G3 = newv("G3"); B3 = newv("B3")
        for r in range(2):
            self.tt(G3.ap[:, :, r], self.sc1[1].ap[:, :, r], self.vT.ap[:, :, 2], ALU.mult, [self.sc1[1].buf, self.vT.buf], [G3.buf])
            self.tt(B3.ap[:, :, r], self.sc1[1].ap[:, :, r], self.vT.ap[:, :, 3], ALU.mult, [self.sc1[1].buf, self.vT.buf], [B3.buf])
        self.tt(B3.ap, B3.ap, self.m[1].ap[:, 0], ALU.add, [B3.buf, self.m[1].buf], [B3.buf])
        self.G3, self.B3 = G3, B3
        P.barrier()

    def load_xT(self, src_ap, src_buf, ntok, xin, uT, sc, sh, r, hT=None, h_off=0, pbanks=(0, 1)):
        nt = ntok // 128
        self.load(xin.ap[:, 0:nt, :], xin.buf, src_ap.rearrange("(t p) d -> p t d", p=128), src_buf)
        for kc in range(KC):
            bap, bb = self.bank(pbanks[kc % len(pbanks)])
            for t in range(nt):
                self.P.op("pe", (lambda o, i: lambda e: e.transpose(o, i, self.ident.ap))(
                    bap[:, t * 128:(t + 1) * 128], xin.ap[:, t, kc * 128:(kc + 1) * 128]),
                    reads=[xin.buf, self.ident.buf], writes=[bb])
            if uT is not None:
                self.act(uT.ap[:, kc, 0:ntok], bap[:, 0:ntok], AF.Identity, [bb, sc.buf, sh.buf], [uT.buf],
                         bias=sh.ap[:, kc, r:r + 1], scale=sc.ap[:, kc, r:r + 1])
            if hT is not None:
                self.cp(hT.ap[:, kc, h_off:h_off + ntok], bap[:, 0:ntok], [bb], [hT.buf])

    def phase_k0(self):
        A, P = self.A, self.P
        KTO = [self.scratch(f"KTO{h}", [128, TOWN], BF16) for h in range(4)]
        VO = [self.scratch(f"VO{h}", [128, TOWN], BF16) for h in range(4)]
        self.scratch("KTC", [4, 128, LCTX], BF16)
        self.scratch("VC", [4, 128, 2, 128], BF16)
        KTC, VC = self.din["KTC"], self.din["VC"]
        m0 = A.mark()
        wk = A.alloc("wk", [128, KC, 512], BF16)
        wks = A.alloc("wks", [128, KC, 512], BF16)
        wv = A.alloc("wv", [128, KC, 512], BF16)
        wsrc = self.din["ev_w_in"].rearrange("(k p) n -> p k n", p=128)
        wsw = self.din["ev_w_in_sw"].rearrange("(k p) n -> p k n", p=128)
        for hh in range(2):
            ks = slice(hh * 4, hh * 4 + 4)
            self.load(wk.ap[:, ks, :], wk.buf, wsrc[:, ks, 512:1024], self.dbuf["ev_w_in"], q="pool")
            self.load(wks.ap[:, ks, :], wks.buf, wsw[:, ks, 512:1024], self.dbuf["ev_w_in_sw"], q="pool")
            self.load(wv.ap[:, ks, :], wv.buf, wsrc[:, ks, 1024:1536], self.dbuf["ev_w_in"], q="pool")
        xin_r = Ring(A, "xin", [128, 4, D], F32, 2)
        uT_r = Ring(A, "uT", [128, KC, 512], BF16, 2)
        cd_r = Ring(A, "cd", [128, 512], F32, 2)
        sd_r = Ring(A, "sd", [128, 512], F32, 2)
        t1_r = Ring(A, "t1", [128, 512], F32, 2)
        t2_r = Ring(A, "t2", [128, 512], F32, 2)
        kt_r = Ring(A, "kt", [128, 512], BF16, 4)
        vt_r = Ring(A, "vt", [128, 512], BF16, 4)
        for blk in range(5):
            ctx = blk == 4
            ntok = 256 if ctx else 512
            tok0 = blk * 512
            r = 1 if ctx else 0
            xin, uT = xin_r.next(), uT_r.next()
            src = self.din["ctxb"] if ctx else self.din["xown"][tok0:tok0 + 512, :]
            sbuf = self.dbuf["ctxb"] if ctx else self.dbuf["xown"]
            self.load_xT(src, sbuf, ntok, xin, uT, self.sc1[0], self.sh1[0], r, pbanks=(0, 1))
            if not ctx:
                cd, sd = cd_r.next(), sd_r.next()
                self.load(cd.ap, cd.buf, self.din["rope_cd_own"][:, tok0:tok0 + 512], self.dbuf["rope_cd_own"])
                self.load(sd.ap, sd.buf, self.din["rope_sd_own"][:, tok0:tok0 + 512], self.dbuf["rope_sd_own"])
            for h in range(4):
                pa, pab = self.bank(2 + (h % 2) * 2)
                pb, pbb = self.bank(3 + (h % 2) * 2)
                for kc in range(KC):
                    self.mm(pa[:, 0:ntok], wk.ap[:, kc, h * 128:(h + 1) * 128], uT.ap[:, kc, 0:ntok], kc == 0, kc == KC - 1,
                            [wk.buf, uT.buf], [pab])
                kt = kt_r.next()
                if not ctx:
                    for kc in range(KC):
                        self.mm(pb[:, 0:ntok], wks.ap[:, kc, h * 128:(h + 1) * 128], uT.ap[:, kc, 0:ntok], kc == 0, kc == KC - 1,
                                [wks.buf, uT.buf], [pbb])
                    t1, t2 = t1_r.next(), t2_r.next()
                    self.tt(t1.ap, pa, cd.ap, ALU.mult, [pab, cd.buf], [t1.buf])
                    self.tt(t2.ap, pb, sd.ap, ALU.mult, [pbb, sd.buf], [t2.buf])
                    self.tt(kt.ap, t1.ap, t2.ap, ALU.add, [t1.buf, t2.buf], [kt.buf])
                    self.store(KTO[h][:, tok0:tok0 + ntok], self.dbuf[f"KTO{h}"], kt.ap[:, 0:ntok], kt.buf)
                else:
                    self.cp(kt.ap[:, 0:ntok], pa[:, 0:ntok], [pab], [kt.buf])
                    self.store(KTC[h, :, 0:ntok], self.dbuf["KTC"], kt.ap[:, 0:ntok], kt.buf)
            for t in range(ntok // 128):
                pv, pvb = self.bank(6 + (t % 2))
                for kc in range(KC):
                    self.mm(pv, uT.ap[:, kc, t * 128:(t + 1) * 128], wv.ap[:, kc, :], kc == 0, kc == KC - 1,
                            [uT.buf, wv.buf], [pvb])
                vt = vt_r.next()
                self.cp(vt.ap, pv, [pvb], [vt.buf], eng="act")
                T = blk * 4 + t
                for h in range(4):
                    if ctx:
                        self.store(VC[h, :, t, :], self.dbuf["VC"], vt.ap[:, h * 128:(h + 1) * 128], vt.buf)
                    else:
                        self.store(VO[h][:, T * 128:(T + 1) * 128], self.dbuf[f"VO{h}"], vt.ap[:, h * 128:(h + 1) * 128], vt.buf)
        P.barrier()
        A.release(m0)
        for h in range(4):
            for (sn, dn) in ((f"KTO{h}", f"KTG{h}"), (f"VO{h}", f"VG{h}")):
                self.scratch(dn, [4 * 128, TOWN], BF16)
                src, dst = self.din[sn], self.din[dn]
                self.P.dma("pool", (lambda s_, d_: lambda e: e.collective_compute(
                    "AllGather", ALU.bypass, replica_groups=[[0, 1, 2, 3], [4, 5, 6, 7]], ins=[s_.opt()], outs=[d_.opt()]))(src, dst),
                    reads=[self.dbuf[sn]], writes=[self.dbuf[dn]], inc=1, dedicated=True, defer=True)

    def phase_kna(self):
        A, P = self.A, self.P
        self.KTn = A.alloc("KTn", [128, 4, NNA], BF16)
        self.Vn = A.alloc("Vn", [128, NNA // 128, 8, 128], BF16)
        self.KTnc = A.alloc("KTnc", [128, 4, LCTX], BF16)
        self.Vnc = A.alloc("Vnc", [128, 2, 8, 128], BF16)
        self.memset(V(self.Vn.ap[:, :, :, 64:128], self.Vn.buf), 1.0)
        self.memset(V(self.Vnc.ap[:, :, :, 64:128], self.Vnc.buf), 1.0)
        m0 = A.mark()
        wbk = A.alloc("wbk", [128, KC, 512], BF16)
        wbv = A.alloc("wbv", [128, KC, 512], BF16)
        wsrc = self.din["ev_w_in"].rearrange("(k p) n -> p k n", p=128)
        for hh in range(2):
            ks = slice(hh * 4, hh * 4 + 4)
            self.load(wbk.ap[:, ks, :], wbk.buf, wsrc[:, ks, 2048:2560], self.dbuf["ev_w_in"], q="pool")
            self.load(wbv.ap[:, ks, :], wbv.buf, wsrc[:, ks, 2560:3072], self.dbuf["ev_w_in"], q="pool")
        xin_r = Ring(A, "xin", [128, 4, D], F32, 1)
        uT_r = Ring(A, "uT", [128, KC, 512], BF16, 2)
        blocks = [(False, i * 512, min(512, NNA - i * 512)) for i in range((NNA + 511) // 512)] + [(True, 0, LCTX)]
        for (ctx, tok0, ntok) in blocks:
            r = 1 if ctx else 0
            xin, uT = xin_r.next(), uT_r.next()
            src = self.din["ctxb"] if ctx else self.din["xna"][tok0:tok0 + ntok, :]
            sbuf = self.dbuf["ctxb"] if ctx else self.dbuf["xna"]
            self.load_xT(src, sbuf, ntok, xin, uT, self.sc1[0], self.sh1[0], r, pbanks=(0, 1))
            KT = self.KTnc if ctx else self.KTn
            VV = self.Vnc if ctx else self.Vn
            for pr in range(4):
                pa, pab = self.bank(2 + pr % 2)
                for kc in range(KC):
                    self.mm(pa[:, 0:ntok], wbk.ap[:, kc, pr * 128:(pr + 1) * 128], uT.ap[:, kc, 0:ntok], kc == 0, kc == KC - 1,
                            [wbk.buf, uT.buf], [pab])
                self.cp(KT.ap[:, pr, tok0:tok0 + ntok], pa[:, 0:ntok], [pab], [KT.buf])
            for t in range(ntok // 128):
                pv, pvb = self.bank(4 + (t % 2))
                for kc in range(KC):
                    self.mm(pv, uT.ap[:, kc, t * 128:(t + 1) * 128], wbv.ap[:, kc, :], kc == 0, kc == KC - 1,
                            [uT.buf, wbv.buf], [pvb])
                T = tok0 // 128 + t
                self.cp(VV.ap[:, T, :, 0:64], pv.rearrange("p (h d) -> p h d", h=8), [pvb], [VV.buf], eng="act")
        P.barrier()
        A.release(m0)

    def phase_q(self, which):
        A, P = self.A, self.P
        if which == "na":
            self.mixT = A.alloc("mixT", [128, KC, NOWN], BF16)
            self.m_att = A.mark()
            self.QTn = A.alloc("QTn", [128, 8, NOWN], BF16)
            self.memset(self.QTn, 0.0)
        else:
            self.QTd = A.alloc("QTd", [128, 4, 2, NOWN], BF16)
            self.memset(self.QTd, 0.0)
        m0 = A.mark()
        wsrc = self.din["ev_w_in"].rearrange("(k p) n -> p k n", p=128)
        wsw = self.din["ev_w_in_sw"].rearrange("(k p) n -> p k n", p=128)
        if which == "na":
            wbq = A.alloc("wbq", [128, KC, 512], BF16)
        else:
            wq = A.alloc("wq", [128, KC, 512], BF16)
            wqs = A.alloc("wqs", [128, KC, 512], BF16)
        for hh in range(2):
            ks = slice(hh * 4, hh * 4 + 4)
            if which == "na":
                self.load(wbq.ap[:, ks, :], wbq.buf, wsrc[:, ks, 1536:2048], self.dbuf["ev_w_in"], q="pool")
            else:
                self.load(wq.ap[:, ks, :], wq.buf, wsrc[:, ks, 0:512], self.dbuf["ev_w_in"], q="pool")
                self.load(wqs.ap[:, ks, :], wqs.buf, wsw[:, ks, 0:512], self.dbuf["ev_w_in_sw"], q="pool")
        xin_r = Ring(A, "xin", [128, 4, D], F32, 2)
        uT_r = Ring(A, "uT", [128, KC, 512], BF16, 2)
        if which != "na":
            cd_r = Ring(A, "cd", [128, 512], F32, 2)
            sd_r = Ring(A, "sd", [128, 512], F32, 2)
            t1_r = Ring(A, "t1", [128, 512], F32, 2)
            t2_r = Ring(A, "t2", [128, 512], F32, 2)
        for blk in range(5):
            ctx = blk == 4
            ntok = 256 if ctx else 512
            tok0 = blk * 512
            r = 1 if ctx else 0
            xin, uT = xin_r.next(), uT_r.next()
            src = self.din["ctxb"] if ctx else self.din["xown"][tok0:tok0 + 512, :]
            sbuf = self.dbuf["ctxb"] if ctx else self.dbuf["xown"]
            self.load_xT(src, sbuf, ntok, xin, uT, self.sc1[0], self.sh1[0], r, pbanks=(0, 1))
            if which == "na":
                for pr in range(4):
                    pa, pab = self.bank(2 + pr % 2)
                    for kc in range(KC):
                        self.mm(pa[:, 0:ntok], wbq.ap[:, kc, pr * 128:(pr + 1) * 128], uT.ap[:, kc, 0:ntok], kc == 0, kc == KC - 1,
                                [wbq.buf, uT.buf], [pab])
                    self.cp(self.QTn.ap[0:64, 2 * pr, tok0:tok0 + ntok], pa[0:64, 0:ntok], [pab], [self.QTn.buf], eng="act")
                    self.cp(self.QTn.ap[64:128, 2 * pr + 1, tok0:tok0 + ntok], pa[64:128, 0:ntok], [pab], [self.QTn.buf], eng="dve")
                continue
            if not ctx:
                cd, sd = cd_r.next(), sd_r.next()
                self.load(cd.ap, cd.buf, self.din["rope_cd_own"][:, tok0:tok0 + 512], self.dbuf["rope_cd_own"])
                self.load(sd.ap, sd.buf, self.din["rope_sd_own"][:, tok0:tok0 + 512], self.dbuf["rope_sd_own"])
            for h in range(4):
                pa, pab = self.bank(2 + (h % 2) * 2)
                pb, pbb = self.bank(3 + (h % 2) * 2)
                for kc in range(KC):
                    self.mm(pa[:, 0:ntok], wq.ap[:, kc, h * 128:(h + 1) * 128], uT.ap[:, kc, 0:ntok], kc == 0, kc == KC - 1,
                            [wq.buf, uT.buf], [pab])
                if not ctx:
                    for kc in range(KC):
                        self.mm(pb[:, 0:ntok], wqs.ap[:, kc, h * 128:(h + 1) * 128], uT.ap[:, kc, 0:ntok], kc == 0, kc == KC - 1,
                                [wqs.buf, uT.buf], [pbb])
                    t1, t2 = t1_r.next(), t2_r.next()
                    self.tt(t1.ap, pa, cd.ap, ALU.mult, [pab, cd.buf], [t1.buf])
                    self.tt(t2.ap, pb, sd.ap, ALU.mult, [pbb, sd.buf], [t2.buf])
                    for m in range(2):
                        rows = slice(64 * m, 64 * m + 64)
                        self.tt(self.QTd.ap[rows, h, m, tok0:tok0 + ntok], t1.ap[rows, :], t2.ap[rows, :], ALU.add,
                                [t1.buf, t2.buf], [self.QTd.buf])
                else:
                    for m in range(2):
                        rows = slice(64 * m, 64 * m + 64)
                        self.cp(self.QTd.ap[rows, h, m, tok0:tok0 + ntok], pa[rows, 0:ntok], [pab], [self.QTd.buf])
        P.barrier()
        A.release(m0)

    def phase_na(self):
        A, P = self.A, self.P
        tb = A.alloc("tb", [128, 8 * 14 * 64], F32)
        self.load(tb.ap, tb.buf, self.din["na_tb"], self.dbuf["na_tb"])
        self.ts(tb.ap, tb.ap, 8.0, None, ALU.mult, None, [tb.buf], [tb.buf])
        tbv = tb.ap.rearrange("p (h x) -> p h x", h=8)
        qm = A.alloc("qm", [128, TOWN], BF16)
        km = A.alloc("km", [128, NNA], BF16)
        self.memset(qm, 0.0)
        self.memset(km, 0.0)
        self.load(qm.ap[0:64, :], qm.buf, self.din["na_qm"], self.dbuf["na_qm"])
        self.load(km.ap[0:64, :], km.buf, self.din["na_km"], self.dbuf["na_km"])
        sb_r = Ring(A, "nasb", [128, 896], F32, 2)
        e_r = Ring(A, "nae", [128, 1152], BF16, 3)
        rz_r = Ring(A, "narz", [64, 128], F32, 2)
        steps = []
        it = 0
        for qt in range(NOWN // 128):
            ctxq = qt >= 16
            qs = slice(qt * 128, (qt + 1) * 128)
            for h in range(8):
                sw, swb = self.pp[it % 2]
                sc, scb = self.bank(4 + it % 2)
                ob, obb = self.bank(6 + it % 2)
                it += 1

                def front(qt=qt, h=h, ctxq=ctxq, qs=qs, sw=sw, swb=swb, sc=sc, scb=scb):
                    pr = h // 2
                    e = e_r.next()
                    if not ctxq:
                        for j in range(7):
                            kt = qt + 6 - j
                            cols = slice(j * 128, (j + 1) * 128)
                            bb_ = swb[j // 4]
                            self.mm(sw[:, cols], self.KTn.ap[:, pr, kt * 128:(kt + 1) * 128], self.QTn.ap[:, h, qs], True, False,
                                    [self.KTn.buf, self.QTn.buf], [bb_])
                            self.mm(sw[:, cols], km.ap[:, kt * 128:(kt + 1) * 128], qm.ap[:, qs], False, True, [km.buf, qm.buf], [bb_])
                        sb = sb_r.next()
                        self.tt(sb.ap, sw[:, 0:896], tbv[:, h, :], ALU.add, swb + [tb.buf], [sb.buf])
                        self.act(e.ap[:, 0:896], sb.ap, AF.Exp, [sb.buf], [e.buf], scale=0.125)
                    for c in range(2):
                        self.mm(sc[:, c * 128:(c + 1) * 128], self.KTnc.ap[:, pr, c * 128:(c + 1) * 128], self.QTn.ap[:, h, qs], True, True,
                                [self.KTnc.buf, self.QTn.buf], [scb])
                    self.act(e.ap[:, 896:1152], sc[:, 0:256], AF.Exp, [scb], [e.buf], scale=0.125)
                    return e

                def back(e, qt=qt, h=h, ctxq=ctxq, qs=qs, ob=ob, obb=obb):
                    pr, half = h // 2, h % 2
                    nk = 0 if ctxq else 7
                    for j in range(nk):
                        kt = qt + 6 - j
                        self.mm(ob[:, 0:128], self.Vn.ap[:, kt, h, :], e.ap[:, j * 128:(j + 1) * 128], j == 0, False,
                                [self.Vn.buf, e.buf], [obb])
                    for c in range(2):
                        self.mm(ob[:, 0:128], self.Vnc.ap[:, c, h, :], e.ap[:, 896 + c * 128:896 + (c + 1) * 128], (nk == 0 and c == 0), c == 1,
                                [self.Vnc.buf, e.buf], [obb])
                    rz = rz_r.next()
                    self.recip(rz.ap, ob[64:128, 0:128], [obb], [rz.buf])
                    self.tt(self.mixT.ap[64 * half:64 * half + 64, 4 + pr, qs], ob[0:64, 0:128], rz.ap, ALU.mult,
                            [obb, rz.buf], [self.mixT.buf])
                steps.append((front, back))
        prev = None
        for (fr, bk) in steps:
            e = fr()
            if prev is not None:
                prev[0](prev[1])
            prev = (bk, e)
        prev[0](prev[1])
        P.barrier()
        A.release(self.m_att)

    def phase_diff(self):
        A, P = self.A, self.P
        lam_in = A.alloc("lam_in", [128, 256], F32)
        self.load(lam_in.ap, lam_in.buf, self.din["diff_lambda"].partition_broadcast(128), self.dbuf["diff_lambda"])
        lp = A.alloc("lam_p", [128, 128], F32)
        self.tt(lp.ap[:, 0:64], lam_in.ap[:, 0:64], lam_in.ap[:, 64:128], ALU.mult, [lam_in.buf], [lp.buf])
        self.tt(lp.ap[:, 64:128], lam_in.ap[:, 128:192], lam_in.ap[:, 192:256], ALU.mult, [lam_in.buf], [lp.buf])
        ls = A.alloc("lam_s", [128, 4], F32)
        self.P.op("dve", lambda e: e.reduce_sum(out=ls.ap[:, 0:2], in_=lp.ap.rearrange("p (a d) -> p a d", a=2), axis=AX.X),
                  reads=[lp.buf], writes=[ls.buf])
        self.act(ls.ap[:, 2:4], ls.ap[:, 0:2], AF.Exp, [ls.buf], [ls.buf])
        nlam = A.alloc("nlam", [128, 1], F32)
        self.stt(nlam.ap, ls.ap[:, 3:4], -0.2, ls.ap[:, 2:3], ALU.add, ALU.subtract, [ls.buf], [nlam.buf])
        gsub = A.alloc("gsub", [128, 1], F32)
        self.load(gsub.ap, gsub.buf, self.din["diff_subln_g"].rearrange("(p o) -> p o", o=1), self.dbuf["diff_subln_g"], slow=True)
        self.ts(gsub.ap, gsub.ap, 0.8, None, ALU.mult, None, [gsub.buf], [gsub.buf])
        kt_r = Ring(A, "KTh", [128, NKEY], BF16, 2)
        v_r = Ring(A, "Vh", [128, NKT, 128], BF16, 2)
        e_r = Ring(A, "dE", [128, 2, 512], BF16, 4)
        f_r = Ring(A, "dF", [128, 512], F32, 6)
        za_r = Ring(A, "dZ", [128, 2, 512], F32, 2)
        qblocks = [(i * 512, 512, list(range(NKT))) for i in range(4)] + [(TOWN, LCTX, [NKT - 2, NKT - 1])]
        heads = []
        for h in range(4):
            KT, VH = kt_r.next(), v_r.next()
            heads.append((KT, VH))

        def load_head(h):
            KT, VH = heads[h]
            for rnk in range(4):
                self.load(KT.ap[:, rnk * TOWN:(rnk + 1) * TOWN], KT.buf, self.din[f"KTG{h}"][rnk * 128:(rnk + 1) * 128, :], self.dbuf[f"KTG{h}"])
                self.load(VH.ap[:, rnk * 16:(rnk + 1) * 16, :], VH.buf,
                          self.din[f"VG{h}"][rnk * 128:(rnk + 1) * 128, :].rearrange("p (t d) -> p t d", d=128), self.dbuf[f"VG{h}"])
            self.load(KT.ap[:, SEQ:NKEY], KT.buf, self.din["KTC"][h], self.dbuf["KTC"])
            self.load(VH.ap[:, 64:66, :], VH.buf, self.din["VC"][h], self.dbuf["VC"])

        def post(h, q0, nq, O, za):
            pz, pzb = self.bank(7)
            r0, r1, t0_, t1b, dd, sq = [f_r.next() for _ in range(6)]
            n = slice(0, nq)
            self.mm(pz[:, n], self.ones_f.ap, za.ap[:, 0, n], True, True, [self.ones_f.buf, za.buf], [pzb])
            self.recip(r0.ap[:, n], pz[:, n], [pzb], [r0.buf])
            self.mm(pz[:, n], self.ones_f.ap, za.ap[:, 1, n], True, True, [self.ones_f.buf, za.buf], [pzb])
            self.recip(r1.ap[:, n], pz[:, n], [pzb], [r1.buf])
            self.tt(t0_.ap[:, n], O[0][0][:, n], r0.ap[:, n], ALU.mult, [O[0][1], r0.buf], [t0_.buf])
            self.tt(t1b.ap[:, n], O[1][0][:, n], r1.ap[:, n], ALU.mult, [O[1][1], r1.buf], [t1b.buf])
            self.stt(dd.ap[:, n], t1b.ap[:, n], nlam.ap[:, 0:1], t0_.ap[:, n], ALU.mult, ALU.add,
                     [t1b.buf, t0_.buf, nlam.buf], [dd.buf])
            self.act(sq.ap[:, n], dd.ap[:, n], AF.Square, [dd.buf], [sq.buf])
            self.mm(pz[:, n], self.ones_f.ap, sq.ap[:, n], True, True, [self.ones_f.buf, sq.buf], [pzb])
            self.act(r0.ap[:, n], pz[:, n], AF.Sqrt, [pzb], [r0.buf], bias=self.eps_rms.ap[:, 0:1], scale=1.0 / 128)
            self.recip(r1.ap[:, n], r0.ap[:, n], [r0.buf], [r1.buf])
            self.stt(self.mixT.ap[:, h, q0:q0 + nq], dd.ap[:, n], gsub.ap[:, 0:1], r1.ap[:, n], ALU.mult, ALU.mult,
                     [dd.buf, gsub.buf, r1.buf], [self.mixT.buf])

        steps = []
        sidx = [0]
        oset = [0]
        load_head(0)
        for h in range(4):
            KT, VH = heads[h]
            for qi, (q0, nq, kts) in enumerate(qblocks):
                ob = 3 + 2 * (oset[0] % 2)
                oset[0] += 1
                O = [self.bank(ob), self.bank(ob + 1)]
                za = za_r.next()
                for ki, kt in enumerate(kts):
                    def front(h=h, KT=KT, q0=q0, nq=nq, kt=kt, ki=ki, za=za, qi=qi):
                        if qi == 0 and ki == 1 and h + 1 < 4:
                            load_head(h + 1)
                        e = e_r.next()
                        for m in range(2):
                            sap, sbuf_ = self.bank(sidx[0] % 3)
                            sidx[0] += 1
                            self.mm(sap[:, 0:nq], KT.ap[:, kt * 128:(kt + 1) * 128], self.QTd.ap[:, h, m, q0:q0 + nq], True, True,
                                    [KT.buf, self.QTd.buf], [sbuf_])
                            self.act(e.ap[:, m, 0:nq], sap[:, 0:nq], AF.Exp, [sbuf_], [e.buf], scale=0.125)
                        if ki == 0:
                            self.cp(za.ap[:, :, 0:nq], e.ap[:, :, 0:nq], [e.buf], [za.buf])
                        else:
                            self.tt(za.ap[:, :, 0:nq], za.ap[:, :, 0:nq], e.ap[:, :, 0:nq], ALU.add, [za.buf, e.buf], [za.buf])
                        return e

                    def back(es, h=h, VH=VH, q0=q0, nq=nq, kt=kt, ki=ki, nk=len(kts), O=O, za=za):
                        for m in range(2):
                            self.mm(O[m][0][:, 0:nq], VH.ap[:, kt, :], es.ap[:, m, 0:nq], ki == 0, ki == nk - 1,
                                    [VH.buf, es.buf], [O[m][1]])
                        if ki == nk - 1:
                            post(h, q0, nq, O, za)
                    steps.append((front, back))
        prev = None
        for (fr, bk) in steps:
            es = fr()
            if prev is not None:
                prev[0](prev[1])
            prev = (bk, es)
        prev[0](prev[1])
        P.barrier()
        A.release(self.m_att)

    def ln_block(self, hT, c0, ntok, gi, bi, G, B, r, uT, u0, st):
        sq, zb, mean, msq, var, rstd, nmr = st
        zs = hT.ap[:, :, c0:c0 + ntok]
        s1, s1b = self.bank(6)
        s2, s2b = self.bank(7)
        self.act(sq.ap[:, :, 0:ntok], zs, AF.Square, [hT.buf], [sq.buf])
        self.cp(zb.ap[:, :, 0:ntok], zs, [hT.buf], [zb.buf], eng="pool")
        for kc in range(KC):
            self.mm(s1[:, 0:ntok], self.ones_b.ap, zb.ap[:, kc, 0:ntok], kc == 0, kc == KC - 1, [self.ones_b.buf, zb.buf], [s1b])
        for kc in range(KC):
            self.mm(s2[:, 0:ntok], self.ones_b.ap, sq.ap[:, kc, 0:ntok], kc == 0, kc == KC - 1, [self.ones_b.buf, sq.buf], [s2b])
        n = slice(0, ntok)
        self.ts(mean.ap[:, n], s1[:, n], 1.0 / D, None, ALU.mult, None, [s1b], [mean.buf])
        self.tt(msq.ap[:, n], mean.ap[:, n], mean.ap[:, n], ALU.mult, [mean.buf], [msq.buf])
        self.stt(var.ap[:, n], s2[:, n], 1.0 / D, msq.ap[:, n], ALU.mult, ALU.subtract, [s2b, msq.buf], [var.buf])
        self.act(msq.ap[:, n], var.ap[:, n], AF.Sqrt, [var.buf], [msq.buf], bias=self.eps_ln.ap[:, 0:1], scale=1.0)
        self.recip(rstd.ap[:, n], msq.ap[:, n], [msq.buf], [rstd.buf])
        self.stt(nmr.ap[:, n], mean.ap[:, n], -1.0, rstd.ap[:, n], ALU.mult, ALU.mult, [mean.buf, rstd.buf], [nmr.buf])
        for kc in range(KC):
            z = hT.ap[:, kc, c0:c0 + ntok]
            self.tt(z, z, rstd.ap[:, n], ALU.mult, [hT.buf, rstd.buf], [hT.buf])
            self.tt(z, z, nmr.ap[:, n], ALU.add, [hT.buf, nmr.buf], [hT.buf])
            if uT is not None:
                self.act(uT.ap[:, kc, u0:u0 + ntok], z, AF.Identity, [hT.buf, G.buf, B.buf], [uT.buf],
                         bias=B.ap[:, kc, r:r + 1], scale=G.ap[:, kc, r:r + 1])
            self.ts(z, z, self.vT.ap[:, kc, gi:gi + 1], self.vT.ap[:, kc, bi:bi + 1], ALU.mult, ALU.add,
                    [hT.buf, self.vT.buf], [hT.buf])

    def alloc_ln_state(self):
        A = self.A
        sq = A.alloc("ln_sq", [128, KC, 512], BF16)
        zb = A.alloc("ln_zb", [128, KC, 512], BF16)
        rest = [A.alloc(f"ln_{n}", [128, 512], F32) for n in ("mean", "msq", "var", "rstd", "nmr")]
        return [sq, zb] + rest

    def phase_o(self, l, w_name, first, nblk=5, mix_dram=None):
        A, P = self.A, self.P
        if first:
            self.hT = A.alloc("hT", [128, KC, NOWN], F32)
        m0 = A.mark()
        mx_r = Ring(A, "mxblk", [128, KC, 512], BF16, 2) if mix_dram is not None else None
        wo = A.alloc("wo", [128, KC, D], BF16)
        wsrc = self.din[w_name].rearrange("(k p) n -> p k n", p=128)
        for hh in range(4):
            ks = slice(hh * 2, hh * 2 + 2)
            self.load(wo.ap[:, ks, :], wo.buf, wsrc[:, ks, :], self.dbuf[w_name], q="pool")
        st = self.alloc_ln_state()
        xin = A.alloc("xin_o", [128, 4, D], F32) if first else None
        pend_ln = []
        for blk in range(nblk):
            ctx = blk == 4
            ntok = 256 if ctx else 512
            tok0 = blk * 512
            r = 1 if ctx else 0
            if first:
                src = self.din["ctxb"] if ctx else self.din["xown"][tok0:tok0 + 512, :]
                sbuf = self.dbuf["ctxb"] if ctx else self.dbuf["xown"]
                self.load_xT(src, sbuf, ntok, xin, None, None, None, r, hT=self.hT, h_off=tok0, pbanks=(0, 1))
            if mix_dram is not None:
                mx = mx_r.next()
                for hh in range(2):
                    self.load(mx.ap[:, hh * 4:hh * 4 + 4, :], mx.buf,
                              self.din[mix_dram][hh * 4:hh * 4 + 4, :, tok0:tok0 + 512].rearrange("k p n -> p k n"), self.dbuf[mix_dram])
                mxa = lambda kc: mx.ap[:, kc, 0:ntok]
                mxb = mx.buf
            else:
                mxa = lambda kc: self.mixT.ap[:, kc, tok0:tok0 + ntok]
                mxb = self.mixT.buf
            for oc in range(KC):
                yb, ybb = self.bank(2 + oc % 4)
                for kc in range(KC):
                    self.mm(yb[:, 0:ntok], wo.ap[:, kc, oc * 128:(oc + 1) * 128], mxa(kc),
                            kc == 0, kc == KC - 1, [wo.buf, mxb], [ybb])
                z = self.hT.ap[:, oc, tok0:tok0 + ntok]
                self.stt(z, yb[:, 0:ntok], self.gt1[l].ap[:, oc, r:r + 1], z, ALU.mult, ALU.add,
                         [ybb, self.gt1[l].buf, self.hT.buf], [self.hT.buf])
            pend_ln.append((tok0, ntok, r))
            if len(pend_ln) > 1:
                t0_, n_, r_ = pend_ln.pop(0)
                self.ln_block(self.hT, t0_, n_, 4 * l + 0, 4 * l + 1, self.G2[l], self.B2[l], r_, self.mixT, t0_, st)
        for (t0_, n_, r_) in pend_ln:
            self.ln_block(self.hT, t0_, n_, 4 * l + 0, 4 * l + 1, self.G2[l], self.B2[l], r_, self.mixT, t0_, st)
        P.barrier()
        A.release(m0)

    def phase_f(self, l, Gn, Bn, ntb=5):
        A, P = self.A, self.P
        m0 = A.mark()
        st = self.alloc_ln_state()
        hf = A.alloc("hffn", [128, HC, 512], BF16)
        wg_r = Ring(A, "wg", [128, KC, 128], BF16, 3)
        wa_r = Ring(A, "wa", [128, KC, 128], BF16, 3)
        w2_r = Ring(A, "w2", [128, HC, 128], BF16, 2)
        sg_r = Ring(A, "sg", [128, 512], F32, 2)
        w1src = self.din["ffn_w_in"][l].rearrange("(k p) n -> p k n", p=128)
        w2src = self.din["ffn_w_out"][l].rearrange("(c p) n -> p c n", p=128)
        pidx = 0
        pend = None
        for blk in range(ntb):
            ctx = blk == 4
            ntok = 256 if ctx else 512
            tok0 = blk * 512
            r = 1 if ctx else 0
            for hc in range(HC):
                wg, wa = wg_r.next(), wa_r.next()
                self.load(wg.ap, wg.buf, w1src[:, :, hc * 128:(hc + 1) * 128], self.dbuf["ffn_w_in"], q="pool")
                self.load(wa.ap, wa.buf, w1src[:, :, FFN_H + hc * 128:FFN_H + (hc + 1) * 128], self.dbuf["ffn_w_in"], q="pool")
                gb, gbb = self.bank(pidx % 4)
                ab, abb = self.bank((pidx + 1) % 4)
                pidx += 2
                for kc in range(KC):
                    self.mm(gb[:, 0:ntok], wg.ap[:, kc, :], self.mixT.ap[:, kc, tok0:tok0 + ntok], kc == 0, kc == KC - 1,
                            [wg.buf, self.mixT.buf], [gbb])
                for kc in range(KC):
                    self.mm(ab[:, 0:ntok], wa.ap[:, kc, :], self.mixT.ap[:, kc, tok0:tok0 + ntok], kc == 0, kc == KC - 1,
                            [wa.buf, self.mixT.buf], [abb])
                sg = sg_r.next()
                self.act(sg.ap[:, 0:ntok], gb[:, 0:ntok], AF.Silu, [gbb], [sg.buf])
                self.tt(hf.ap[:, hc, 0:ntok], ab[:, 0:ntok], sg.ap[:, 0:ntok], ALU.mult, [abb, sg.buf], [hf.buf])
            if pend is not None:
                self.ln_block(self.hT, pend[0], pend[1], 4 * l + 2, 4 * l + 3, Gn, Bn, pend[2], self.mixT if Gn is not None else None, pend[0], st)
                pend = None
            for oc in range(KC):
                w2 = w2_r.next()
                self.load(w2.ap, w2.buf, w2src[:, :, oc * 128:(oc + 1) * 128], self.dbuf["ffn_w_out"], q="pool")
                yb, ybb = self.bank(4 + oc % 2)
                for hc in range(HC):
                    self.mm(yb[:, 0:ntok], w2.ap[:, hc, :], hf.ap[:, hc, 0:ntok], hc == 0, hc == HC - 1, [w2.buf, hf.buf], [ybb])
                z = self.hT.ap[:, oc, tok0:tok0 + ntok]
                self.stt(z, yb[:, 0:ntok], self.gt2[l].ap[:, oc, r:r + 1], z, ALU.mult, ALU.add,
                         [ybb, self.gt2[l].buf, self.hT.buf], [self.hT.buf])
            pend = (tok0, ntok, r)
        if pend is not None:
            self.ln_block(self.hT, pend[0], pend[1], 4 * l + 2, 4 * l + 3, Gn, Bn, pend[2], self.mixT if Gn is not None else None, pend[0], st)
        P.barrier()
        A.release(m0)

    def phase_out(self, dst_name, ntok_total):
        A, P = self.A, self.P
        m0 = A.mark()
        o_r = Ring(A, "orow", [128, D], F32, 3)
        evs = []
        for t in range(ntok_total // 128):
            o = o_r.next()
            for half in range(2):
                pb, pbb = self.bank((2 * t + half) % 4)
                for j in range(4):
                    kc = half * 4 + j
                    self.P.op("pe", (lambda oo, ii: lambda e: e.transpose(oo, ii, self.ident.ap))(
                        pb[:, j * 128:(j + 1) * 128], self.hT.ap[:, kc, t * 128:(t + 1) * 128]),
                        reads=[self.hT.buf, self.ident.buf], writes=[pbb])
                self.cp(o.ap[:, half * 512:(half + 1) * 512], pb, [pbb], [o.buf], eng=("act" if half else "dve"))
            evs.append(self.store(self.din[dst_name][t * 128:(t + 1) * 128, :], self.dbuf[dst_name], o.ap, o.buf))
        A.release(m0)
        return evs

    def phase_p1(self, lat_only=False, mid=None):
        A, P = self.A, self.P
        LATP = [self.scratch("LATA", [128, TOWN], BF16), self.scratch("LATB", [128, TOWN], BF16), self.scratch("LATK", [32, TOWN], BF16)]
        LATN = ["LATA", "LATB", "LATK"]
        LATC = self.scratch("LATC", [288, LCTX], BF16)
        if not lat_only:
            MIX1 = self.scratch("MIX1", [KC, 128, TOWN], BF16)
            self.cqn = A.alloc("cqn", [128, 3, TOWN], BF16)
        wsrc = self.din["od_w_in"].rearrange("(k p) n -> p k n", p=128)

        def proj_T(wt, c0, ncol, bank_i, tok0, ntok):
            bap, bbuf = self.bank(bank_i)
            for kc in range(KC):
                self.mm(bap[0:ncol, 0:ntok], wt.ap[:, kc, c0:c0 + ncol], self.mixT.ap[:, kc, tok0:tok0 + ntok], kc == 0, kc == KC - 1,
                        [wt.buf, self.mixT.buf], [bbuf])
            return bap, bbuf

        m0 = A.mark()
        w1 = A.alloc("w1a", [128, KC, 672], BF16)
        for hh in range(2):
            ks = slice(hh * 4, hh * 4 + 4)
            self.load(w1.ap[:, ks, :], w1.buf, wsrc[:, ks, 0:672], self.dbuf["od_w_in"], q="pool")
        wkrs = A.alloc("wkrs", [128, KC, 32], BF16)
        self.load(wkrs.ap, wkrs.buf, self.din["od_w_in_krsw"].rearrange("(k p) n -> p k n", p=128), self.dbuf["od_w_in_krsw"], q="pool")
        f_r = Ring(A, "p1f", [128, 3, 512], F32, 2)
        s_r = Ring(A, "p1s", [128, 3, 512], F32, 2)
        r_r = Ring(A, "p1r", [128, 512], F32, 4)
        lat_r = Ring(A, "latst", [128, 2, 512], BF16, 2)
        kr_r = Ring(A, "krst", [32, 512], BF16, 2)
        tq_r = Ring(A, "p1tab", [32, 2, 512], F32, 2)
        k1_r = Ring(A, "p1k", [32, 2, 512], F32, 1)

        def rms_norm_T(ps_list, nch, ntok, nfeat, outs):
            f, s_ = f_r.next(), s_r.next()
            for c in range(nch):
                self.cp(f.ap[:, c, 0:ntok], ps_list[c][0][:, 0:ntok], [ps_list[c][1]], [f.buf], eng="dve")
                self.act(s_.ap[:, c, 0:ntok], ps_list[c][0][:, 0:ntok], AF.Square, [ps_list[c][1]], [s_.buf])
            sb_, sbb_ = self.bank(7)
            for c in range(nch):
                self.mm(sb_[:, 0:ntok], self.ones_f.ap, s_.ap[:, c, 0:ntok], c == 0, c == nch - 1, [self.ones_f.buf, s_.buf], [sbb_])
            r0, r1 = r_r.next(), r_r.next()
            self.act(r0.ap[:, 0:ntok], sb_[:, 0:ntok], AF.Sqrt, [sbb_], [r0.buf], bias=self.eps_rms.ap[:, 0:1], scale=1.0 / nfeat)
            self.recip(r1.ap[:, 0:ntok], r0.ap[:, 0:ntok], [r0.buf], [r1.buf])
            for c in range(nch):
                oap, obuf = outs[c]
                self.tt(oap, f.ap[:, c, 0:ntok], r1.ap[:, 0:ntok], ALU.mult, [f.buf, r1.buf], [obuf])

        for blk in range(5):
            ctx = blk == 4
            ntok = 256 if ctx else 512
            tok0 = blk * 512
            ps = [proj_T(w1, 384 + c * 128, 128, c, tok0, ntok) for c in range(2)]
            lat = lat_r.next()
            rms_norm_T(ps, 2, ntok, 256, [(lat.ap[:, c, 0:ntok], lat.buf) for c in range(2)])
            if ctx:
                self.store(LATC[0:256, 0:ntok].rearrange("(c p) n -> p c n", p=128), self.dbuf["LATC"], lat.ap[:, :, 0:ntok], lat.buf)
            else:
                for c in range(2):
                    self.store(LATP[c][:, tok0:tok0 + ntok], self.dbuf[LATN[c]], lat.ap[:, c, 0:ntok], lat.buf)
            pa, pab = proj_T(w1, 640, 32, 2, tok0, ntok)
            krs = kr_r.next()
            if not ctx:
                pb, pbb = proj_T(wkrs, 0, 32, 3, tok0, ntok)
                tq = tq_r.next()
                self.load(tq.ap[:, 0, :], tq.buf, self.din["ropeq_c_own"][64:96, tok0:tok0 + 512], self.dbuf["ropeq_c_own"])
                self.load(tq.ap[:, 1, :], tq.buf, self.din["ropeq_s_own"][64:96, tok0:tok0 + 512], self.dbuf["ropeq_s_own"])
                k1 = k1_r.next()
                self.tt(k1.ap[:, 0, :], pa[0:32, :], tq.ap[:, 0, :], ALU.mult, [pab, tq.buf], [k1.buf])
                self.tt(k1.ap[:, 1, :], pb[0:32, :], tq.ap[:, 1, :], ALU.mult, [pbb, tq.buf], [k1.buf])
                self.tt(krs.ap, k1.ap[:, 0, :], k1.ap[:, 1, :], ALU.add, [k1.buf], [krs.buf])
            else:
                self.cp(krs.ap[:, 0:ntok], pa[0:32, 0:ntok], [pab], [krs.buf])
            if ctx:
                self.store(LATC[256:288, 0:ntok], self.dbuf["LATC"], krs.ap[:, 0:ntok], krs.buf)
            else:
                self.store(LATP[2][:, tok0:tok0 + ntok], self.dbuf["LATK"], krs.ap[:, 0:ntok], krs.buf)
            if ctx or lat_only:
                continue
            ps = [proj_T(w1, c * 128, 128, 4 + c, tok0, ntok) for c in range(3)]
            rms_norm_T(ps, 3, ntok, 384, [(self.cqn.ap[:, c, tok0:tok0 + ntok], self.cqn.buf) for c in range(3)])
        P.barrier()
        A.release(m0)
        if lat_only:
            return
        if mid is not None:
            mid()
        m0 = A.mark()
        w1 = A.alloc("w1b", [128, KC, 1024], BF16)
        for hh in range(4):
            ks = slice(hh * 2, hh * 2 + 2)
            self.load(w1.ap[:, ks, :], w1.buf, wsrc[:, ks, 672:1696], self.dbuf["od_w_in"], q="pool")
        gbc = A.alloc("gln_g", [128, 512], F32)
        bbc = A.alloc("gln_b", [128, 512], F32)
        gbb = A.alloc("gm_b", [128, 512], F32)
        self.load(gbc.ap, gbc.buf, self.din["gmlp_ln_g"].partition_broadcast(128), self.dbuf["gmlp_ln_g"])
        self.load(bbc.ap, bbc.buf, self.din["gmlp_ln_b"].partition_broadcast(128), self.dbuf["gmlp_ln_b"])
        self.load(gbb.ap, gbb.buf, self.din["gmlp_b"].partition_broadcast(128), self.dbuf["gmlp_b"])
        wsT = A.alloc("wsT", [128, 4, 128], BF16)
        wss = A.alloc("wss", [128, 4, 128], F32)
        self.load(wss.ap, wss.buf, self.din["gmlp_ws"].rearrange("g i j -> i g j"), self.dbuf["gmlp_ws"])
        tb_, tbb_ = self.bank(0)
        for g in range(4):
            self.P.op("pe", (lambda o, i: lambda e: e.transpose(o, i, self.ident.ap))(tb_[:, g * 128:(g + 1) * 128], wss.ap[:, g, :]),
                      reads=[wss.buf, self.ident.buf], writes=[tbb_])
        self.cp(wsT.ap.rearrange("p g i -> p (g i)"), tb_, [tbb_], [wsT.buf])
        gu_r = Ring(A, "gu", [128, 4, 512], BF16, 2)
        gvf_r = Ring(A, "gvf", [128, 512], F32, 2)
        vg_r = Ring(A, "vg", [128, 512], BF16, 2)
        gm_r = Ring(A, "gm", [128, 512], F32, 2)
        go_r = Ring(A, "go", [128, 4, 128], BF16, 3)
        st_r = Ring(A, "bnst", [128, 16], F32, 2)
        eps_g = A.alloc("eps_g", [128, 1], F32)
        self.memset(eps_g, LN_EPS)
        for blk in range(4):
            ntok = 512
            tok0 = blk * 512
            gu = gu_r.next()
            for g in range(4):
                bap, bbuf = proj_T(w1, g * 128, 128, g % 4, tok0, ntok)
                self.act(gu.ap[:, g, :], bap, AF.Gelu, [bbuf], [gu.buf])
            for t in range(4):
                pv, pvb = self.bank(4 + t % 2)
                for kc in range(KC):
                    self.mm(pv, self.mixT.ap[:, kc, tok0 + t * 128:tok0 + (t + 1) * 128], w1.ap[:, kc, 512:1024], kc == 0, kc == KC - 1,
                            [self.mixT.buf, w1.buf], [pvb])
                gvf = gvf_r.next()
                self.act(gvf.ap, pv, AF.Gelu, [pvb], [gvf.buf])
                stt_ = st_r.next()
                self.P.op("dve", (lambda o, i: lambda e: e.bn_stats(out=o, in_=i))(stt_.ap[:, 0:6], gvf.ap), reads=[gvf.buf], writes=[stt_.buf])
                self.P.op("dve", (lambda o, i: lambda e: e.bn_aggr(out=o, in_=i))(stt_.ap[:, 6:8], stt_.ap[:, 0:6]), reads=[stt_.buf], writes=[stt_.buf])
                self.act(stt_.ap[:, 8:9], stt_.ap[:, 7:8], AF.Sqrt, [stt_.buf], [stt_.buf], bias=eps_g.ap[:, 0:1], scale=1.0)
                self.recip(stt_.ap[:, 9:10], stt_.ap[:, 8:9], [stt_.buf], [stt_.buf])
                self.stt(stt_.ap[:, 10:11], stt_.ap[:, 6:7], -1.0, stt_.ap[:, 9:10], ALU.mult, ALU.mult, [stt_.buf], [stt_.buf])
                self.ts(gvf.ap, gvf.ap, stt_.ap[:, 9:10], stt_.ap[:, 10:11], ALU.mult, ALU.add, [gvf.buf, stt_.buf], [gvf.buf])
                self.tt(gvf.ap, gvf.ap, gbc.ap, ALU.mult, [gvf.buf, gbc.buf], [gvf.buf])
                vg = vg_r.next()
                self.tt(vg.ap, gvf.ap, bbc.ap, ALU.add, [gvf.buf, bbc.buf], [vg.buf])
                pm, pmb = self.bank(6)
                for g in range(4):
                    self.mm(pm[:, g * 128:(g + 1) * 128], vg.ap[:, g * 128:(g + 1) * 128], wsT.ap[:, g, :], True, True, [vg.buf, wsT.buf], [pmb])
                gm = gm_r.next()
                self.tt(gm.ap, pm, gbb.ap, ALU.add, [pmb, gbb.buf], [gm.buf])
                go = go_r.next()
                self.tt(go.ap, gm.ap.rearrange("p (g i) -> p g i", g=4), gu.ap[:, :, t * 128:(t + 1) * 128], ALU.mult, [gm.buf, gu.buf], [go.buf])
                c0 = tok0 + t * 128
                self.store(MIX1[4:8, :, c0:c0 + 128].rearrange("g p i -> p g i"), self.dbuf["MIX1"], go.ap, go.buf)
        P.barrier()
        A.release(m0)

    def exchange(self, mode):
        if mode == "B":
            self.inp("latg", [4 * 288, TOWN], BF16)
            lg, lb = self.din["latg"], self.dbuf["latg"]
            self.lat_piece = lambda r, p: (lg[r * 288 + (0, 128, 256)[p]:r * 288 + (128, 256, 288)[p], :], lb)
            self.prefetch_ck()
            return
        names = [("LATA", "LGA", 128), ("LATB", "LGB", 128), ("LATK", "LGK", 32)]
        for (sn, dn, rows) in names:
            self.scratch(dn, [4 * rows, TOWN], BF16)
            src, dst = self.din[sn], self.din[dn]
            self.P.dma("pool", (lambda s_, d_: lambda e: e.collective_compute(
                "AllGather", ALU.bypass, replica_groups=[[0, 1, 2, 3], [4, 5, 6, 7]], ins=[s_.opt()], outs=[d_.opt()]))(src, dst),
                reads=[self.dbuf[sn]], writes=[self.dbuf[dn]], inc=1, dedicated=True)
        self.lat_piece = lambda r, p: (self.din[names[p][1]][r * names[p][2]:(r + 1) * names[p][2], :], self.dbuf[names[p][1]])
        self.prefetch_ck()

    def prefetch_ck(self):
        A = self.A
        self.m_ck = A.mark()
        self.ck = A.alloc("ckall", [128, 2, NKEY], BF16)
        self.ckb = [Buf(f"ck{r}") for r in range(5)]
        for rnk in range(4):
            for c in range(2):
                pap, pbuf = self.lat_piece(rnk, c)
                self.load(self.ck.ap[:, c, rnk * TOWN:(rnk + 1) * TOWN], self.ckb[rnk], pap, pbuf, q=("sp" if c == 0 else "act"))

    def phase_k1(self):
        A, P = self.A, self.P
        KN = self.scratch("KN1", [8 * 64, NKEY], BF16)
        V1 = self.scratch("V1", [8, 128, NKT, 65], BF16)
        LC, lcb = self.din["LATC"], self.dbuf["LATC"]
        ck, ckb = self.ck, self.ckb
        for c in range(2):
            self.load(ck.ap[:, c, SEQ:NKEY], ckb[4], LC[c * 128:(c + 1) * 128, :], lcb)
        wst = A.alloc("wukv_f", [128, 2, 1024], F32)
        self.load(wst.ap, wst.buf, self.din["mla_w_ukv"].rearrange("(c p) n -> p c n", p=128), self.dbuf["mla_w_ukv"])
        wkv = A.alloc("wukv", [128, 2, 2, 512], BF16)
        for c in range(2):
            src4 = wst.ap[:, c, :].rearrange("p (h x) -> p h x", h=8)
            for kv in range(2):
                self.ts(wkv.ap[:, c, kv, :].rearrange("p (h d) -> p h d", h=8), src4[:, :, kv * 64:(kv + 1) * 64],
                        self.vT.ap[:, c, 21:22], None, ALU.mult, None, [wst.buf, self.vT.buf], [wkv.buf])
        kt_r = Ring(A, "k1kt", [128, 512], BF16, 6)
        vt_r = Ring(A, "k1vt", [128, 8, 8, 65], BF16, 2)
        for v in vt_r.items:
            self.memset(v, 1.0)
        bi = 0
        vt = None
        si = 0
        for T in range(NKT):
            if T % 8 == 0:
                vt = vt_r.next()
            cb = ckb[min(T // 16, 4)]
            pv, pvb = self.bank(4 + T % 4)
            for c in range(2):
                self.mm(pv, ck.ap[:, c, T * 128:(T + 1) * 128], wkv.ap[:, c, 1, :],
                        c == 0, c == 1, [cb, wkv.buf], [pvb])
            self.cp(vt.ap[:, :, T % 8, 0:64], pv.rearrange("p (h d) -> p h d", h=8), [pvb], [vt.buf], eng=("act" if T % 2 else "dve"))
            if T % 8 == 7 or T == NKT - 1:
                tg = (T // 8) * 8
                nt = T - tg + 1
                for h in range(8):
                    self.store(V1[h, :, tg:tg + nt, :], self.dbuf["V1"], vt.ap[:, h, 0:nt, :], vt.buf, q=("sp" if si % 2 == 0 else "act"))
                    si += 1
        for hp in range(4):
            for blk in range(17):
                ntok = 256 if blk == 16 else 512
                tok0 = blk * 512
                cb = ckb[min(blk // 4, 4)]
                pa, pab = self.bank(bi % 4)
                bi += 1
                for c in range(2):
                    self.mm(pa[:, 0:ntok], wkv.ap[:, c, 0, hp * 128:(hp + 1) * 128], ck.ap[:, c, tok0:tok0 + ntok], c == 0, c == 1,
                            [wkv.buf, cb], [pab])
                kt = kt_r.next()
                self.cp(kt.ap[:, 0:ntok], pa[:, 0:ntok], [pab], [kt.buf], eng=("act" if blk % 2 else "dve"))
                self.store(KN[hp * 128:(hp + 1) * 128, tok0:tok0 + ntok], self.dbuf["KN1"], kt.ap[:, 0:ntok], kt.buf, q=("sp" if si % 2 == 0 else "act"))
                si += 1
        P.barrier()
        A.release(self.m_ck)

    def phase_mla(self):
        A, P = self.A, self.P
        KN, V1, MIX1 = self.din["KN1"], self.din["V1"], self.din["MIX1"]
        m0 = A.mark()
        wq = [A.alloc(nm + "_b", [128, 3, 768], BF16) for nm in ("mla_w_uq", "mla_w_uq_sw")]
        m1 = A.mark()
        for i, nm in enumerate(("mla_w_uq", "mla_w_uq_sw")):
            wst = A.alloc(nm + "_f", [128, 3, 768], F32)
            self.load(wst.ap, wst.buf, self.din[nm].rearrange("(c p) n -> p c n", p=128), self.dbuf[nm])
            for c in range(3):
                self.ts(wq[i].ap[:, c, :], wst.ap[:, c, :], self.vT.ap[:, c, 20:21], None, ALU.mult, None, [wst.buf, self.vT.buf], [wq[i].buf])
        P.barrier()
        A.release(m1)
        tab_r = Ring(A, "qtab", [96, 2, 512], F32, 2)
        kt_r = Ring(A, "KT1h", [96, 8, 1], BF16, 1)
        kt_r.items = [A.alloc_at(f"KT1h{i}", [96, NKEY], BF16, self.mixT.off + i * NKEY * 2) for i in range(2)]
        v_r = Ring(A, "V1h", [128, NKT, 65], BF16, 2)
        q_r = Ring(A, "Q1h", [96, TOWN], BF16, 2)
        e_r = Ring(A, "mE", [128, 1024], BF16, 4)
        t_r = Ring(A, "mT", [96, 512], F32, 4)
        oc_r = Ring(A, "mO", [64, 512], F32, 2)
        rz_r = Ring(A, "mZ", [1, 512], F32, 2)
        on_r = Ring(A, "mN", [64, 512], BF16, 3)
        scale = 96.0 ** -0.5
        heads = [(kt_r.items[h % 2], v_r.next(), q_r.next()) for h in range(8)]

        for i in range(2):
            KTb = kt_r.items[i]
            for rnk in range(4):
                pap, pbuf = self.lat_piece(rnk, 2)
                self.load(KTb.ap[64:96, rnk * TOWN:(rnk + 1) * TOWN], KTb.buf, pap, pbuf)
            self.load(KTb.ap[64:96, SEQ:NKEY], KTb.buf, self.din["LATC"][256:288, :], self.dbuf["LATC"])

        def prep_head(h):
            KT, VH, QH = heads[h]
            for part in range(3):
                c0, c1 = part * 2816, (part + 1) * 2816
                self.load(KT.ap[0:64, c0:c1], KT.buf, KN[h * 64:(h + 1) * 64, c0:c1], self.dbuf["KN1"], q="act")
                t0, t1_ = part * 22, (part + 1) * 22
                self.load(VH.ap[:, t0:t1_, :], VH.buf, V1[h, :, t0:t1_, :], self.dbuf["V1"])
            for qb in range(4):
                qs = slice(qb * 512, (qb + 1) * 512)
                pa, pab = self.bank(6)
                pb, pbb = self.bank(7)
                for c in range(3):
                    self.mm(pa[0:96, :], wq[0].ap[:, c, h * 96:(h + 1) * 96], self.cqn.ap[:, c, qs], c == 0, c == 2, [wq[0].buf, self.cqn.buf], [pab])
                for c in range(3):
                    self.mm(pb[0:96, :], wq[1].ap[:, c, h * 96:(h + 1) * 96], self.cqn.ap[:, c, qs], c == 0, c == 2, [wq[1].buf, self.cqn.buf], [pbb])
                t1, t2 = t_r.next(), t_r.next()
                tab = tab_r.next()
                self.load(tab.ap[:, 0, :], tab.buf, self.din["ropeq_c_own"][:, qs], self.dbuf["ropeq_c_own"])
                self.load(tab.ap[:, 1, :], tab.buf, self.din["ropeq_s_own"][:, qs], self.dbuf["ropeq_s_own"])
                self.tt(t1.ap, pa[0:96, :], tab.ap[:, 0, :], ALU.mult, [pab, tab.buf], [t1.buf])
                self.tt(t2.ap, pb[0:96, :], tab.ap[:, 1, :], ALU.mult, [pbb, tab.buf], [t2.buf])
                self.tt(QH.ap[:, qs], t1.ap, t2.ap, ALU.add, [t1.buf, t2.buf], [QH.buf])

        def post(h, qs, ob, obb):
            oc, rz, on = oc_r.next(), rz_r.next(), on_r.next()
            self.cp(oc.ap, ob[0:64, :], [obb], [oc.buf], eng="act")
            self.recip(rz.ap, ob[64:65, :], [obb], [rz.buf])
            zb, zbb = self.bank(6)
            self.mm(zb[0:64, :], self.ones_f.ap[0:1, 0:64], rz.ap[0:1, :], True, True, [self.ones_f.buf, rz.buf], [zbb])
            self.tt(on.ap, oc.ap, zb[0:64, :], ALU.mult, [oc.buf, zbb], [on.buf])
            pr, half = h // 2, h % 2
            self.store(MIX1[pr, 64 * half:64 * half + 64, qs], self.dbuf["MIX1"], on.ap, on.buf)

        steps = []
        sidx = [0]
        oidx = [0]
        prep_head(0)
        for h in range(8):
            KT, VH, QH = heads[h]
            for qb in range(4):
                qs = slice(qb * 512, (qb + 1) * 512)
                ob, obb = self.bank(4 + oidx[0] % 2)
                oidx[0] += 1
                for kp in range(NKT // 2):
                    def front(h=h, KT=KT, QH=QH, qs=qs, kp=kp, qb=qb):
                        if qb == 0 and kp == 2 and h + 1 < 8:
                            prep_head(h + 1)
                        sap, sbufs = self.pp[sidx[0] % 2]
                        sidx[0] += 1
                        for j in range(2):
                            kt = 2 * kp + j
                            self.mm(sap[:, j * 512:(j + 1) * 512], KT.ap[0:96, kt * 128:(kt + 1) * 128], QH.ap[0:96, qs], True, True,
                                    [KT.buf, QH.buf], [sbufs[j]])
                        e = e_r.next()
                        self.act(e.ap, sap, AF.Exp, sbufs, [e.buf], scale=scale)
                        return e

                    def back(e, h=h, VH=VH, qs=qs, kp=kp, ob=ob, obb=obb):
                        for j in range(2):
                            kt = 2 * kp + j
                            self.mm(ob[0:65, :], VH.ap[:, kt, :], e.ap[:, j * 512:(j + 1) * 512], (kp == 0 and j == 0),
                                    (kp == NKT // 2 - 1 and j == 1), [VH.buf, e.buf], [obb])
                        if kp == NKT // 2 - 1:
                            post(h, qs, ob, obb)
                    steps.append((front, back))
        prev = None
        for (fr, bk) in steps:
            e = fr()
            if prev is not None:
                prev[0](prev[1])
            prev = (bk, e)
        prev[0](prev[1])
        P.barrier()
        A.release(m0)

    def phase_load_h(self):
        A, P = self.A, self.P
        self.mixT = A.alloc("mixT", [128, KC, NOWN], BF16)
        self.hT = A.alloc("hT", [128, KC, NOWN], F32)
        m0 = A.mark()
        xin_r = Ring(A, "xin", [128, 4, D], F32, 2)
        for blk in range(5):
            ctx = blk == 4
            ntok = 256 if ctx else 512
            tok0 = blk * 512
            r = 1 if ctx else 0
            xin = xin_r.next()
            self.load_xT(self.din["h_own"][tok0:tok0 + ntok, :], self.dbuf["h_own"], ntok, xin,
                         V(self.mixT.ap[:, :, tok0:tok0 + ntok], self.mixT.buf), self.sc1[1], self.sh1[1], r,
                         hT=self.hT, h_off=tok0, pbanks=(0, 1))
        P.barrier()
        A.release(m0)


def build(mode):
    kb = KB(mode)
    kb.declare_inputs()
    evs = []
    if mode in ("A", "F"):
        kb.phase_const(); kb.phase_mod()
        kb.phase_k0()
        kb.phase_q("na"); kb.phase_kna(); kb.phase_na()
        kb.phase_q("diff"); kb.phase_diff()
        kb.phase_o(0, "ev_w_out", True)
        kb.phase_f(0, kb.G3, kb.B3)
    if mode == "A":
        kb.outp("out_h", [NOWN, D])
        evs += kb.phase_out("out_h", NOWN)
        kb.P.barrier()
        kb.phase_p1(lat_only=True)
        kb.outp("lat_out", [288, TOWN], BF16)
        for (nm, r0, r1) in (("LATA", 0, 128), ("LATB", 128, 256), ("LATK", 256, 288)):
            evs.append(kb.P.dma("sp", (lambda o, i: lambda e: e.dma_start(out=o, in_=i))(kb.din["lat_out"][r0:r1, :], kb.din[nm]),
                                reads=[kb.dbuf[nm]], writes=[kb.dbuf["lat_out"]]))
    if mode == "B":
        kb.inp("h_own", [NOWN, D])
        kb.phase_const(); kb.phase_mod()
        kb.phase_load_h()
    if mode in ("B", "F"):
        kb.phase_p1(mid=lambda: kb.exchange(mode))
        kb.phase_k1()
        kb.phase_mla()
        kb.phase_o(1, "od_w_out", False, nblk=4, mix_dram="MIX1")
        kb.phase_f(1, None, None, ntb=4)
        kb.outp("out", [TOWN, D])
        evs += kb.phase_out("out", TOWN)
    kb.P.emit(final_events=evs)
    return kb


def rope_tables():
    t = np.arange(SEQ); row = (t // 64).astype(np.float32); col = (t % 64).astype(np.float32)
    inv32 = np.power(np.float32(10000.0), -np.arange(0, 32, 2, dtype=np.float32) / np.float32(32)).astype(np.float32)
    cd = np.zeros((64, SEQ), np.float32); sd = np.zeros((64, SEQ), np.float32)
    for d in range(64):
        pos = row if d < 32 else col
        j = d % 16
        ang = (pos * inv32[j]).astype(np.float32)
        cd[d] = np.cos(ang); s = np.sin(ang)
        sd[d] = -s if (d % 32) < 16 else s
    cd = np.concatenate([cd, cd], 0); sd = np.concatenate([sd, sd], 0)
    inv16 = np.power(np.float32(10000.0), -np.arange(0, 16, 2, dtype=np.float32) / np.float32(16)).astype(np.float32)
    cm = np.zeros((32, SEQ), np.float32); sm = np.zeros((32, SEQ), np.float32)
    for d in range(32):
        pos = row if d < 16 else col
        j = d % 8
        ang = (pos * inv16[j]).astype(np.float32)
        cm[d] = np.cos(ang); s = np.sin(ang)
        sm[d] = -s if (d % 16) < 8 else s
    cq = np.concatenate([np.ones((64, SEQ), np.float32), cm], 0)
    sq = np.concatenate([np.zeros((64, SEQ), np.float32), sm], 0)
    return cd, sd, cq, sq

def swap_perm(n, blk):
    idx = np.arange(n); h = blk // 2
    return np.where((idx % blk) < h, idx + h, idx - h)

def na_tables(rpb, q0):
    ck = np.arange(64)[:, None]; cq = np.arange(64)[None, :]
    cs = np.clip(cq - 8, 0, 48)
    colvalid = (ck >= cs) & (ck < cs + 16)
    coff = np.clip(ck - cq + 15, 0, 30)
    tb = np.zeros((2, 64, 8, 14, 64), np.float32)
    for a in range(2):
        for s in range(14):
            dr = 6 - s + a
            for h in range(8):
                blk = rpb[h, dr + 7][coff]
                tb[a, :, h, s, :] = np.where(colvalid, blk, np.float32(NEG))
    tb = tb.reshape(128, 8 * 14 * 64)
    qm = np.zeros((64, TOWN), np.float32); km = np.zeros((64, NNA), np.float32)
    r0 = q0 // 64
    iq = np.arange(TOWN); rq = r0 + iq // 64
    rs = np.clip(rq - 4, 0, 120)
    for rho in range(44):
        rk = r0 - 6 + rho
        valid = (rk >= 0) & (rk < 128) & (rk >= rs) & (rk < rs + 8)
        qm[rho] = np.where(valid, 0.0, NEG)
        km[rho, rho * 64:(rho + 1) * 64] = 1.0
    return tb, qm.astype(ml_dtypes.bfloat16), km.astype(ml_dtypes.bfloat16)

def core_inputs(INP, core, tabs):
    b, q0 = core // 4, (core % 4) * TOWN
    cd, sd, cq, sq = tabs
    x = INP['x'][b]
    xna = np.zeros((NNA, D), np.float32)
    lo, hi = q0 - 384, q0 - 384 + NNA
    slo, shi = max(lo, 0), min(hi, SEQ)
    xna[slo - lo:shi - lo] = x[slo:shi]
    v = np.zeros((24, 1024), np.float32)
    for l in range(2):
        v[4*l+0] = INP['ln_mix_g'][l]; v[4*l+1] = INP['ln_mix_b'][l]; v[4*l+2] = INP['ln_ffn_g'][l]; v[4*l+3] = INP['ln_ffn_b'][l]
        v[8+6*l:14+6*l] = INP['mod_b'][l].reshape(6, 1024)
    v[20, :384] = INP['mla_q_norm_g'][0]; v[21, :256] = INP['mla_kv_norm_g'][0]
    v[22] = INP['c'][b]; v[23] = INP['c_ctx']
    ev = INP['ev_w_in'][0]
    p64 = swap_perm(1024, 32)
    tb, qm, km = na_tables(INP['na_rpb'][0], q0)
    d = {
        "xall": x, "ctxb": INP['ctx'][b], "xown": x[q0:q0 + TOWN], "xna": xna, "vecs": v,
        "ident": np.eye(128, dtype=np.float32), "mod_w": INP['mod_w'],
        "ev_w_in": ev, "ev_w_in_sw": np.ascontiguousarray(ev[:, :1024][:, p64]), "ev_w_out": INP['ev_w_out'][0],
        "ffn_w_in": INP['ffn_w_in'], "ffn_w_out": INP['ffn_w_out'],
        "diff_lambda": INP['diff_lambda'][0].reshape(256), "diff_subln_g": INP['diff_subln_g'][0],
        "rope_cd": cd, "rope_sd": sd, "rope_cd_own": np.ascontiguousarray(cd[:, q0:q0 + TOWN]),
        "rope_sd_own": np.ascontiguousarray(sd[:, q0:q0 + TOWN]),
        "na_tb": tb, "na_qm": qm, "na_km": km,
    }
    return d


def core_inputs_l1(INP, core, tabs):
    b, q0 = core // 4, (core % 4) * TOWN
    cd, sd, cq, sq = tabs
    od = INP['od_w_in'][0]
    p32 = swap_perm(32, 16)
    wuq = INP['mla_w_uq'][0]
    wuq_sw = np.zeros_like(wuq)
    for h in range(8):
        c0 = h * 96 + 64
        wuq_sw[:, c0:c0 + 32] = wuq[:, c0:c0 + 32][:, p32]
    return {
        "od_w_in": od, "od_w_in_krsw": np.ascontiguousarray(od[:, 640:672][:, p32]), "od_w_out": INP['od_w_out'][0],
        "mla_w_uq": wuq, "mla_w_uq_sw": wuq_sw, "mla_w_ukv": INP['mla_w_ukv'][0],
        "gmlp_ln_g": INP['gmlp_ln_g'][0], "gmlp_ln_b": INP['gmlp_ln_b'][0], "gmlp_ws": INP['gmlp_ws'][0],
        "gmlp_b": INP['gmlp_b'][0].reshape(512),
        "ropeq_c_own": np.ascontiguousarray(cq[:, q0:q0 + TOWN]), "ropeq_s_own": np.ascontiguousarray(sq[:, q0:q0 + TOWN]),
    }


MODE = "F"
_CACHE = {}


def _get(mode):
    if mode not in _CACHE:
        _CACHE[mode] = build(mode)
    return _CACHE[mode]


def kernel(**inputs):
    INP = {k: np.asarray(v) for k, v in inputs.items()}
    tabs = rope_tables()
    per_core = []
    for core in range(8):
        d = core_inputs(INP, core, tabs)
        d.update(core_inputs_l1(INP, core, tabs))
        per_core.append(d)
    out = np.zeros((2, SEQ, D), np.float32)
    if MODE == "F":
        kb = _get("F")
        maps = [{k: v for k, v in d.items() if k in kb.din} for d in per_core]
        res = run_bass_kernel_spmd(kb.nc, maps, core_ids=list(range(8)))
        outs = [r["out"] for r in res.results]
    else:
        ka = _get("A")
        maps = [{k: v for k, v in d.items() if k in ka.din} for d in per_core]
        ra = run_bass_kernel_spmd(ka.nc, maps, core_ids=list(range(8))).results
        kbb = _get("B")
        maps = []
        for core in range(8):
            g = core // 4
            d = {k: v for k, v in per_core[core].items() if k in kbb.din}
            d["h_own"] = np.asarray(ra[core]["out_h"])
            d["latg"] = np.concatenate([np.asarray(ra[4 * g + r]["lat_out"]) for r in range(4)], 0)
            maps.append(d)
        rb = run_bass_kernel_spmd(kbb.nc, maps, core_ids=list(range(8))).results
        outs = [r["out"] for r in rb]
    for core in range(8):
        b, q0 = core // 4, (core % 4) * TOWN
        out[b, q0:q0 + TOWN] = np.asarray(outs[core])
    return out
```

```python
import ml_dtypes
import contextlib
import numpy as np
import concourse.bass as bass
import concourse.mybir as mybir
from concourse.bass_utils import run_bass_kernel_spmd

F32 = mybir.dt.float32
BF16 = mybir.dt.bfloat16
AF = mybir.ActivationFunctionType
ALU = mybir.AluOpType
AX = mybir.AxisListType

SAME_ENGINE_SYNC = True


class Buf:
    _n = 0

    def __init__(self, name=""):
        Buf._n += 1
        self.name = f"{name}_{Buf._n}"
        self.last_w = None
        self.rd_eng = {}
        self.rd_dma = []
        self.sem = None
        self.cnt = 0
        self.excl = False


class Prog:
    ENGS = ("pe", "act", "dve", "pool", "sp")

    def __init__(self, nc):
        self.nc = nc
        self.q = {e: [] for e in self.ENGS}
        self.slots = []
        self.free_slots = []
        self.epoch_bufs = []
        self.last_ev = {e: None for e in self.ENGS}
        self.open_dma = []

    def _deps(self, reads, writes):
        deps = []
        for b in reads:
            if b.last_w is not None:
                deps.append(b.last_w)
            if b.excl:
                deps.extend(b.rd_eng.values())
        for b in writes:
            if b.last_w is not None:
                deps.append(b.last_w)
            deps.extend(b.rd_eng.values())
            deps.extend(b.rd_dma)
        return deps

    def _commit(self, ev, reads, writes):
        for b in reads:
            if ev[0] == "e":
                b.rd_eng[ev[1]] = ev
            else:
                b.rd_dma.append(ev)
        for b in writes:
            b.last_w = ev
            b.rd_eng = {}
            b.rd_dma = []

    def op(self, eng, fn, reads=(), writes=(), extra=()):
        deps = self._deps(reads, writes) + list(extra)
        idx = len(self.q[eng])
        ev = ("e", eng, idx)
        self.q[eng].append({"fn": fn, "deps": deps, "needed": False, "sem": None})
        self._commit(ev, reads, writes)
        if fn is not None:
            self.last_ev[eng] = ev
        return ev

    def dma(self, eng, fn, reads=(), writes=(), sembuf=None, inc=16, extra=(), dedicated=False, defer=False):
        deps = self._deps(reads, writes) + list(extra)
        sb = sembuf if sembuf is not None else writes[0]
        if dedicated:
            sb = Buf("dedicated")
            sb.sem = {"cnt": 0, "sem": None, "id": len(self.slots)}
            self.slots.append(sb.sem)
        if sb.sem is None:
            if self.free_slots:
                sb.sem = self.free_slots.pop()
            else:
                sb.sem = {"cnt": 0, "sem": None, "id": len(self.slots)}
                self.slots.append(sb.sem)
            self.epoch_bufs.append(sb)
        slot = sb.sem
        slot["cnt"] += inc
        ev = ("d", slot, slot["cnt"])
        self.q[eng].append({"fn": fn, "deps": deps, "needed": False, "sem": slot, "inc": inc})
        self._commit(ev, reads, writes)
        if not defer:
            self.open_dma.append(ev)
        return ev

    def barrier(self):
        mx = {}
        for d in self.open_dma:
            k = d[1]["id"]
            if k not in mx or mx[k][2] < d[2]:
                mx[k] = d
        evs = [v for v in self.last_ev.values() if v is not None] + list(mx.values())
        self.open_dma = []
        for b in self.epoch_bufs:
            self.free_slots.append(b.sem)
            b.sem = None
        self.epoch_bufs = []
        for e in self.ENGS:
            self.op(e, None, extra=evs)

    def emit(self, final_events=()):
        nc = self.nc
        skip_same = lambda d, e: d[0] == "e" and d[1] == e and (e == "pe" or not SAME_ENGINE_SYNC)
        for e in self.ENGS:
            for o in self.q[e]:
                for d in o["deps"]:
                    if d[0] == "e" and not skip_same(d, e):
                        self.q[d[1]][d[2]]["needed"] = True
        for d in final_events:
            if d[0] == "e":
                self.q[d[1]][d[2]]["needed"] = True
        for e in self.ENGS:
            for i, o in enumerate(self.q[e]):
                if o["fn"] is None and o["needed"]:
                    raise RuntimeError("wait-only op used as dependency")
        for e in self.ENGS:
            c = 0
            for o in self.q[e]:
                if o["needed"]:
                    c += 1
                    o["val"] = c
        self.n_wait = 0
        self.n_ins = 0
        with contextlib.ExitStack() as st:
            esem = {e: st.enter_context(nc.semaphore(f"s_{e}")) for e in self.ENGS}
            for sl in self.slots:
                sl["sem"] = st.enter_context(nc.semaphore(f"d_slot{sl['id']}"))
            block = st.enter_context(nc.Block())

            def run(e, engobj, extra_final=()):
                waited = {}

                def wait_for(d):
                    if skip_same(d, e):
                        return
                    if d[0] == "e":
                        key = ("e", d[1])
                        sem = esem[d[1]]
                        val = self.q[d[1]][d[2]]["val"]
                    else:
                        key = ("d", d[1]["id"])
                        sem = d[1]["sem"]
                        val = d[2]
                    if waited.get(key, 0) >= val:
                        return
                    waited[key] = val
                    engobj.wait_ge(sem, val)
                    self.n_wait += 1

                for o in self.q[e]:
                    for d in o["deps"]:
                        wait_for(d)
                    if o["fn"] is None:
                        continue
                    ins = o["fn"](engobj)
                    self.n_ins += 1
                    if o["sem"] is not None:
                        if o["inc"] == 1:
                            ins.then_inc(o["sem"]["sem"])
                        else:
                            ins.then_inc(o["sem"]["sem"], o["inc"])
                    elif o["needed"]:
                        ins.then_inc(esem[e], 1)
                for d in extra_final:
                    wait_for(d)

            @block.tensor
            def _(t):
                run("pe", t)

            @block.scalar
            def _(s):
                run("act", s)

            @block.vector
            def _(v):
                run("dve", v)

            @block.gpsimd
            def _(g):
                run("pool", g)

            @block.sync
            def _(s):
                run("sp", s, final_events)


D = 1024
KC = 8
SEQ = 8192
LCTX = 256
TOWN = 2048
NOWN = TOWN + LCTX
NKEY = SEQ + LCTX
NKT = NKEY // 128
NNA = 2816
ALPHA = 4.0 ** 0.25
LN_EPS = 1e-5
RMS_EPS = 1e-6
NEG = -30000.0
FFN_H = 2816
HC = FFN_H // 128
NVEC = 24


def _prod(xs):
    r = 1
    for x in xs:
        r *= x
    return r


class Arena:
    def __init__(self, nc, base=16384, limit=16384 + 212000):
        self.nc, self.base, self.top, self.limit = nc, base, base, limit
        self.n = 0
        self.peak = base

    def alloc(self, name, shape, dtype):
        sz = 2 if dtype == BF16 else 4
        nbytes = _prod(shape[1:]) * sz
        off = (self.top + 31) // 32 * 32
        self.top = off + nbytes
        self.peak = max(self.peak, self.top)
        assert self.top <= self.limit, f"SBUF arena overflow at {name}: {self.top - self.base}"
        self.n += 1
        h = self.nc.alloc_sbuf_tensor_at(f"{name}_{self.n}", list(shape), dtype, offset=off)
        t = TT(h)
        t.off = off
        return t

    def alloc_at(self, name, shape, dtype, off):
        self.n += 1
        h = self.nc.alloc_sbuf_tensor_at(f"{name}_{self.n}", list(shape), dtype, offset=off)
        t = TT(h)
        t.off = off
        return t

    def mark(self):
        return self.top

    def release(self, m):
        self.top = m


class V:
    def __init__(self, ap, buf):
        self.ap, self.buf = ap, buf


class TT:
    def __init__(self, h, buf=None):
        self.h = h
        self.ap = h.ap()
        self.buf = buf if buf is not None else Buf(h.name)

    def __getitem__(self, k):
        return self.ap[k]


class Ring:
    def __init__(self, arena, name, shape, dtype, n):
        self.items = [arena.alloc(f"{name}{i}", shape, dtype) for i in range(n)]
        self.i = 0

    def next(self):
        t = self.items[self.i % len(self.items)]
        self.i += 1
        return t


class KB:
    def __init__(self, mode):
        self.mode = mode
        nc = bass.Bass("TRN2", target_bir_lowering=False)
        self.nc = nc
        self.P = Prog(nc)
        self.A = Arena(nc)
        self.din = {}
        self.dbuf = {}
        self.pp = []
        for i in range(4):
            h = nc.alloc_psum_tensor(f"ps{i}", [128, 1024], F32)
            self.pp.append((h.ap(), [Buf(f"ps{i}a"), Buf(f"ps{i}b")]))
            for b in self.pp[-1][1]:
                b.excl = True

    def bank(self, k):
        ap, bufs = self.pp[k // 2]
        j = k % 2
        return ap[:, j * 512:(j + 1) * 512], bufs[j]

    def inp(self, name, shape, dtype=F32):
        t = self.nc.dram_tensor(name, list(shape), dtype, kind="ExternalInput")
        self.din[name] = t.ap()
        self.dbuf[name] = Buf(name)
        return t.ap()

    def outp(self, name, shape, dtype=F32):
        t = self.nc.dram_tensor(name, list(shape), dtype, kind="ExternalOutput")
        self.din[name] = t.ap()
        self.dbuf[name] = Buf(name)
        return t.ap()

    def scratch(self, name, shape, dtype):
        t = self.nc.dram_tensor(name, list(shape), dtype)
        self.din[name] = t.ap()
        self.dbuf[name] = Buf(name)
        return t.ap()

    def mm(self, out, lhsT, rhs, start, stop, rd, wr):
        self.P.op("pe", lambda e: e.matmul(out, lhsT=lhsT, rhs=rhs, start=start, stop=stop),
                  reads=rd, writes=wr)

    def act(self, out, in_, func, rd, wr, bias=None, scale=None):
        kw = {}
        if bias is not None:
            kw["bias"] = bias
        if scale is not None:
            kw["scale"] = scale
        return self.P.op("act", lambda e: e.activation(out=out, in_=in_, func=func, **kw),
                         reads=rd, writes=wr)

    def tt(self, out, in0, in1, op, rd, wr, eng="dve"):
        return self.P.op(eng, lambda e: e.tensor_tensor(out=out, in0=in0, in1=in1, op=op),
                         reads=rd, writes=wr)

    def ts(self, out, in0, s1, s2, op0, op1, rd, wr, eng="dve"):
        if op1 is None:
            return self.P.op(eng, lambda e: e.tensor_scalar(out=out, in0=in0, scalar1=s1, scalar2=None, op0=op0),
                             reads=rd, writes=wr)
        return self.P.op(eng, lambda e: e.tensor_scalar(out=out, in0=in0, scalar1=s1, scalar2=s2, op0=op0, op1=op1),
                         reads=rd, writes=wr)

    def stt(self, out, in0, scalar, in1, op0, op1, rd, wr):
        return self.P.op("dve", lambda e: e.scalar_tensor_tensor(out=out, in0=in0, scalar=scalar, in1=in1, op0=op0, op1=op1),
                         reads=rd, writes=wr)

    def cp(self, out, in_, rd, wr, eng="dve"):
        if eng == "act":
            return self.P.op("act", lambda e: e.copy(out=out, in_=in_), reads=rd, writes=wr)
        return self.P.op(eng, lambda e: e.tensor_copy(out=out, in_=in_), reads=rd, writes=wr)

    def recip(self, out, in_, rd, wr):
        return self.P.op("dve", lambda e: e.reciprocal(out=out, in_=in_), reads=rd, writes=wr)

    def memset(self, t, val, eng="dve"):
        ap = t.ap
        return self.P.op(eng, lambda e: e.memset(ap, val), writes=[t.buf])

    def load(self, t_ap, t_buf, src_ap, src_buf, q="sp", slow=False):
        kw = {"allow_slow_non_contiguous": True} if slow else {}
        return self.P.dma(q, lambda e: e.dma_start(out=t_ap, in_=src_ap, **kw), reads=[src_buf], writes=[t_buf])

    def store(self, dst_ap, dst_buf, t_ap, t_buf, q="sp"):
        return self.P.dma(q, lambda e: e.dma_start(out=dst_ap, in_=t_ap), reads=[t_buf], writes=[dst_buf], sembuf=t_buf)

    def declare_inputs(self):
        I = self.inp
        mode = self.mode
        I("vecs", [NVEC, D]); I("ident", [128, 128]); I("mod_w", [2, D, 6 * D])
        I("ffn_w_in", [2, D, 2 * FFN_H]); I("ffn_w_out", [2, FFN_H, D])
        if mode in ("A", "F", "dbg"):
            I("ctxb", [LCTX, D]); I("xown", [TOWN, D]); I("xna", [NNA, D])
            I("ev_w_in", [D, 3072]); I("ev_w_in_sw", [D, 1024]); I("ev_w_out", [D, D])
            I("diff_lambda", [256]); I("diff_subln_g", [128])
            I("rope_cd_own", [128, TOWN]); I("rope_sd_own", [128, TOWN])
            I("na_tb", [128, 8 * 14 * 64]); I("na_qm", [64, TOWN], BF16); I("na_km", [64, NNA], BF16)
        I("od_w_in", [D, 1696]); I("od_w_in_krsw", [D, 32])
        I("ropeq_c_own", [96, TOWN]); I("ropeq_s_own", [96, TOWN])
        if mode in ("B", "F", "dbg"):
            I("od_w_out", [D, D])
            I("mla_w_uq", [384, 768]); I("mla_w_uq_sw", [384, 768]); I("mla_w_ukv", [256, 1024])
            I("gmlp_ln_g", [512]); I("gmlp_ln_b", [512]); I("gmlp_ws", [4, 128, 128]); I("gmlp_b", [512])

    def phase_const(self):
        A, P = self.A, self.P
        self.ident = A.alloc("ident", [128, 128], F32)
        self.load(self.ident.ap, self.ident.buf, self.din["ident"], self.dbuf["ident"])
        self.ones_f = A.alloc("ones_f", [128, 128], F32)
        self.memset(self.ones_f, 1.0)
        self.ones_b = A.alloc("ones_b", [128, 128], BF16)
        self.memset(self.ones_b, 1.0)
        self.eps_rms = A.alloc("eps_rms", [128, 1], F32)
        self.memset(self.eps_rms, RMS_EPS)
        self.eps_ln = A.alloc("eps_ln", [128, 1], F32)
        self.memset(self.eps_ln, LN_EPS / (ALPHA * ALPHA))
        self.vT = A.alloc("vT", [128, KC, NVEC], F32)
        self.sT = A.alloc("sT", [128, KC, 2], BF16)
        m0 = A.mark()
        vs = A.alloc("vecs_sb", [NVEC, D], F32)
        self.load(vs.ap, vs.buf, self.din["vecs"], self.dbuf["vecs"])
        bap, bb = self.bank(0)
        for kc in range(KC):
            self.mm(bap[:, kc * NVEC:(kc + 1) * NVEC], vs.ap[0:NVEC, kc * 128:(kc + 1) * 128],
                    self.ident.ap[0:NVEC, 0:NVEC], True, True, [vs.buf, self.ident.buf], [bb])
        self.cp(self.vT.ap, bap[:, 0:KC * NVEC].rearrange("p (k r) -> p k r", r=NVEC), [bb], [self.vT.buf])
        self.act(self.sT.ap, self.vT.ap[:, :, 22:24], AF.Silu, [self.vT.buf], [self.sT.buf])
        P.barrier()
        A.release(m0)

    def phase_mod(self):
        A, P = self.A, self.P
        self.m = [A.alloc(f"m{l}", [128, 6, KC, 2], F32) for l in range(2)]
        m0 = A.mark()
        ring = Ring(A, "modw", [128, KC, 1024], BF16, 2)
        for l in range(2):
            bap, bb = self.bank(l)
            for g in range(6):
                w = ring.next()
                src = self.din["mod_w"][l, :, g * 1024:(g + 1) * 1024].rearrange("(k p) n -> p k n", p=128)
                for hh in range(2):
                    self.load(w.ap[:, hh * 4:(hh + 1) * 4, :], w.buf, src[:, hh * 4:(hh + 1) * 4, :], self.dbuf["mod_w"], q="pool")
                for oc in range(8):
                    col = (g * 8 + oc) * 2
                    for kc in range(KC):
                        self.mm(bap[:, col:col + 2], w.ap[:, kc, oc * 128:(oc + 1) * 128], self.sT.ap[:, kc, :],
                                kc == 0, kc == KC - 1, [w.buf, self.sT.buf], [bb])
            pv = bap[:, 0:96].rearrange("p (g k r) -> p g k r", g=6, k=8)
            for g in range(6):
                for r in range(2):
                    self.tt(self.m[l].ap[:, g, :, r], pv[:, g, :, r], self.vT.ap[:, :, 8 + l * 6 + g], ALU.add,
                            [bb, self.vT.buf], [self.m[l].buf])
        A.release(m0)
        def newv(name):
            return A.alloc(name, [128, KC, 2], F32)
        self.sc1, self.gt1, self.sc2, self.gt2 = [], [], [], []
        self.sh1 = [V(self.m[l].ap[:, 0], self.m[l].buf) for l in range(2)]
        self.G2, self.B2, self.G3, self.B3 = [], [], [], []
        for l in range(2):
            m = self.m[l]
            sc1 = newv(f"sc1_{l}"); self.ts(sc1.ap, m.ap[:, 1], 1.0, None, ALU.add, None, [m.buf], [sc1.buf])
            gt1 = newv(f"gt1_{l}"); self.ts(gt1.ap, m.ap[:, 2], 1.0 / ALPHA, None, ALU.mult, None, [m.buf], [gt1.buf])
            sc2 = newv(f"sc2_{l}"); self.ts(sc2.ap, m.ap[:, 4], 1.0, None, ALU.add, None, [m.buf], [sc2.buf])
            gt2 = newv(f"gt2_{l}"); self.ts(gt2.ap, m.ap[:, 5], 1.0 / ALPHA, None, ALU.mult, None, [m.buf], [gt2.buf])
            self.sc1.append(sc1); self.gt1.append(gt1); self.sc2.append(sc2); self.gt2.append(gt2)
        for l in range(2):
            m = self.m[l]
            G2 = newv(f"G2_{l}"); B2 = newv(f"B2_{l}")
            for r in range(2):
                self.tt(G2.ap[:, :, r], self.sc2[l].ap[:, :, r], self.vT.ap[:, :, 4 * l + 0], ALU.mult,
                        [self.sc2[l].buf, self.vT.buf], [G2.buf])
                self.tt(B2.ap[:, :, r], self.sc2[l].ap[:, :, r], self.vT.ap[:, :, 4 * l + 1], ALU.mult,
                        [self.sc2[l].buf, self.vT.buf], [B2.buf])
            self.tt(B2.ap, B2.ap, m.ap[:, 3], ALU.add, [B2.buf, m.buf], [B2.buf])
            self.G2.append(G2); self.B2.append(B2)
        G3 = newv("G3"); B3 = newv("B3")
        for r in range(2):
            self.tt(G3.ap[:, :, r], self.sc1[1].ap[:, :, r], self.vT.ap[:, :, 2], ALU.mult, [self.sc1[1].buf, self.vT.buf], [G3.buf])
            self.tt(B3.ap[:, :, r], self.sc1[1].ap[:, :, r], self.vT.ap[:, :, 3], ALU.mult, [self.sc1[1].buf, self.vT.buf], [B3.buf])
        self.tt(B3.ap, B3.ap, self.m[1].ap[:, 0], ALU.add, [B3.buf, self.m[1].buf], [B3.buf])
        self.G3, self.B3 = G3, B3
        P.barrier()

    def load_xT(self, src_ap, src_buf, ntok, xin, uT, sc, sh, r, hT=None, h_off=0, pbanks=(0, 1)):
        nt = ntok // 128
        self.load(xin.ap[:, 0:nt, :], xin.buf, src_ap.rearrange("(t p) d -> p t d", p=128), src_buf)
        for kc in range(KC):
            bap, bb = self.bank(pbanks[kc % len(pbanks)])
            for t in range(nt):
                self.P.op("pe", (lambda o, i: lambda e: e.transpose(o, i, self.ident.ap))(
                    bap[:, t * 128:(t + 1) * 128], xin.ap[:, t, kc * 128:(kc + 1) * 128]),
                    reads=[xin.buf, self.ident.buf], writes=[bb])
            if uT is not None:
                self.act(uT.ap[:, kc, 0:ntok], bap[:, 0:ntok], AF.Identity, [bb, sc.buf, sh.buf], [uT.buf],
                         bias=sh.ap[:, kc, r:r + 1], scale=sc.ap[:, kc, r:r + 1])
            if hT is not None:
                self.cp(hT.ap[:, kc, h_off:h_off + ntok], bap[:, 0:ntok], [bb], [hT.buf])

    def phase_k0(self):
        A, P = self.A, self.P
        KTO = [self.scratch(f"KTO{h}", [128, TOWN], BF16) for h in range(4)]
        VO = [self.scratch(f"VO{h}", [128, TOWN], BF16) for h in range(4)]
        self.scratch("KTC", [4, 128, LCTX], BF16)
        self.scratch("VC", [4, 128, 2, 128], BF16)
        KTC, VC = self.din["KTC"], self.din["VC"]
        m0 = A.mark()
        wk = A.alloc("wk", [128, KC, 512], BF16)
        wks = A.alloc("wks", [128, KC, 512], BF16)
        wv = A.alloc("wv", [128, KC, 512], BF16)
        wsrc = self.din["ev_w_in"].rearrange("(k p) n -> p k n", p=128)
        wsw = self.din["ev_w_in_sw"].rearrange("(k p) n -> p k n", p=128)
        for hh in range(2):
            ks = slice(hh * 4, hh * 4 + 4)
            self.load(wk.ap[:, ks, :], wk.buf, wsrc[:, ks, 512:1024], self.dbuf["ev_w_in"], q="pool")
            self.load(wks.ap[:, ks, :], wks.buf, wsw[:, ks, 512:1024], self.dbuf["ev_w_in_sw"], q="pool")
            self.load(wv.ap[:, ks, :], wv.buf, wsrc[:, ks, 1024:1536], self.dbuf["ev_w_in"], q="pool")
        xin_r = Ring(A, "xin", [128, 4, D], F32, 2)
        uT_r = Ring(A, "uT", [128, KC, 512], BF16, 2)
        cd_r = Ring(A, "cd", [128, 512], F32, 2)
        sd_r = Ring(A, "sd", [128, 512], F32, 2)
        t1_r = Ring(A, "t1", [128, 512], F32, 2)
        t2_r = Ring(A, "t2", [128, 512], F32, 2)
        kt_r = Ring(A, "kt", [128, 512], BF16, 4)
        vt_r = Ring(A, "vt", [128, 512], BF16, 4)
        for blk in range(5):
            ctx = blk == 4
            ntok = 256 if ctx else 512
            tok0 = blk * 512
            r = 1 if ctx else 0
            xin, uT = xin_r.next(), uT_r.next()
            src = self.din["ctxb"] if ctx else self.din["xown"][tok0:tok0 + 512, :]
            sbuf = self.dbuf["ctxb"] if ctx else self.dbuf["xown"]
            self.load_xT(src, sbuf, ntok, xin, uT, self.sc1[0], self.sh1[0], r, pbanks=(0, 1))
            if not ctx:
                cd, sd = cd_r.next(), sd_r.next()
                self.load(cd.ap, cd.buf, self.din["rope_cd_own"][:, tok0:tok0 + 512], self.dbuf["rope_cd_own"])
                self.load(sd.ap, sd.buf, self.din["rope_sd_own"][:, tok0:tok0 + 512], self.dbuf["rope_sd_own"])
            for h in range(4):
                pa, pab = self.bank(2 + (h % 2) * 2)
                pb, pbb = self.bank(3 + (h % 2) * 2)
                for kc in range(KC):
                    self.mm(pa[:, 0:ntok], wk.ap[:, kc, h * 128:(h + 1) * 128], uT.ap[:, kc, 0:ntok], kc == 0, kc == KC - 1,
                            [wk.buf, uT.buf], [pab])
                kt = kt_r.next()
                if not ctx:
                    for kc in range(KC):
                        self.mm(pb[:, 0:ntok], wks.ap[:, kc, h * 128:(h + 1) * 128], uT.ap[:, kc, 0:ntok], kc == 0, kc == KC - 1,
                                [wks.buf, uT.buf], [pbb])
                    t1, t2 = t1_r.next(), t2_r.next()
                    self.tt(t1.ap, pa, cd.ap, ALU.mult, [pab, cd.buf], [t1.buf])
                    self.tt(t2.ap, pb, sd.ap, ALU.mult, [pbb, sd.buf], [t2.buf])
                    self.tt(kt.ap, t1.ap, t2.ap, ALU.add, [t1.buf, t2.buf], [kt.buf])
                    self.store(KTO[h][:, tok0:tok0 + ntok], self.dbuf[f"KTO{h}"], kt.ap[:, 0:ntok], kt.buf)
                else:
                    self.cp(kt.ap[:, 0:ntok], pa[:, 0:ntok], [pab], [kt.buf])
                    self.store(KTC[h, :, 0:ntok], self.dbuf["KTC"], kt.ap[:, 0:ntok], kt.buf)
            for t in range(ntok // 128):
                pv, pvb = self.bank(6 + (t % 2))
                for kc in range(KC):
                    self.mm(pv, uT.ap[:, kc, t * 128:(t + 1) * 128], wv.ap[:, kc, :], kc == 0, kc == KC - 1,
                            [uT.buf, wv.buf], [pvb])
                vt = vt_r.next()
                self.cp(vt.ap, pv, [pvb], [vt.buf], eng="act")
                T = blk * 4 + t
                for h in range(4):
                    if ctx:
                        self.store(VC[h, :, t, :], self.dbuf["VC"], vt.ap[:, h * 128:(h + 1) * 128], vt.buf)
                    else:
                        self.store(VO[h][:, T * 128:(T + 1) * 128], self.dbuf[f"VO{h}"], vt.ap[:, h * 128:(h + 1) * 128], vt.buf)
        P.barrier()
        A.release(m0)
        for h in range(4):
            for (sn, dn) in ((f"KTO{h}", f"KTG{h}"), (f"VO{h}", f"VG{h}")):
                self.scratch(dn, [4 * 128, TOWN], BF16)
                src, dst = self.din[sn], self.din[dn]
                self.P.dma("pool", (lambda s_, d_: lambda e: e.collective_compute(
                    "AllGather", ALU.bypass, replica_groups=[[0, 1, 2, 3], [4, 5, 6, 7]], ins=[s_.opt()], outs=[d_.opt()]))(src, dst),
                    reads=[self.dbuf[sn]], writes=[self.dbuf[dn]], inc=1, dedicated=True, defer=True)

    def phase_kna(self):
        A, P = self.A, self.P
        self.KTn = A.alloc("KTn", [128, 4, NNA], BF16)
        self.Vn = A.alloc("Vn", [128, NNA // 128, 8, 128], BF16)
        self.KTnc = A.alloc("KTnc", [128, 4, LCTX], BF16)
        self.Vnc = A.alloc("Vnc", [128, 2, 8, 128], BF16)
        self.memset(V(self.Vn.ap[:, :, :, 64:128], self.Vn.buf), 1.0)
        self.memset(V(self.Vnc.ap[:, :, :, 64:128], self.Vnc.buf), 1.0)
        m0 = A.mark()
        wbk = A.alloc("wbk", [128, KC, 512], BF16)
        wbv = A.alloc("wbv", [128, KC, 512], BF16)
        wsrc = self.din["ev_w_in"].rearrange("(k p) n -> p k n", p=128)
        for hh in range(2):
            ks = slice(hh * 4, hh * 4 + 4)
            self.load(wbk.ap[:, ks, :], wbk.buf, wsrc[:, ks, 2048:2560], self.dbuf["ev_w_in"], q="pool")
            self.load(wbv.ap[:, ks, :], wbv.buf, wsrc[:, ks, 2560:3072], self.dbuf["ev_w_in"], q="pool")
        xin_r = Ring(A, "xin", [128, 4, D], F32, 1)
        uT_r = Ring(A, "uT", [128, KC, 512], BF16, 2)
        blocks = [(False, i * 512, min(512, NNA - i * 512)) for i in range((NNA + 511) // 512)] + [(True, 0, LCTX)]
        for (ctx, tok0, ntok) in blocks:
            r = 1 if ctx else 0
            xin, uT = xin_r.next(), uT_r.next()
            src = self.din["ctxb"] if ctx else self.din["xna"][tok0:tok0 + ntok, :]
            sbuf = self.dbuf["ctxb"] if ctx else self.dbuf["xna"]
            self.load_xT(src, sbuf, ntok, xin, uT, self.sc1[0], self.sh1[0], r, pbanks=(0, 1))
            KT = self.KTnc if ctx else self.KTn
            VV = self.Vnc if ctx else self.Vn
            for pr in range(4):
                pa, pab = self.bank(2 + pr % 2)
                for kc in range(KC):
                    self.mm(pa[:, 0:ntok], wbk.ap[:, kc, pr * 128:(pr + 1) * 128], uT.ap[:, kc, 0:ntok], kc == 0, kc == KC - 1,
                            [wbk.buf, uT.buf], [pab])
                self.cp(KT.ap[:, pr, tok0:tok0 + ntok], pa[:, 0:ntok], [pab], [KT.buf])
            for t in range(ntok // 128):
                pv, pvb = self.bank(4 + (t % 2))
                for kc in range(KC):
                    self.mm(pv, uT.ap[:, kc, t * 128:(t + 1) * 128], wbv.ap[:, kc, :], kc == 0, kc == KC - 1,
                            [uT.buf, wbv.buf], [pvb])
                T = tok0 // 128 + t
                self.cp(VV.ap[:, T, :, 0:64], pv.rearrange("p (h d) -> p h d", h=8), [pvb], [VV.buf], eng="act")
        P.barrier()
        A.release(m0)

    def phase_q(self, which):
        A, P = self.A, self.P
        if which == "na":
            self.mixT = A.alloc("mixT", [128, KC, NOWN], BF16)
            self.m_att = A.mark()
            self.QTn = A.alloc("QTn", [128, 8, NOWN], BF16)
            self.memset(self.QTn, 0.0)
        else:
            self.QTd = A.alloc("QTd", [128, 4, 2, NOWN], BF16)
            self.memset(self.QTd, 0.0)
        m0 = A.mark()
        wsrc = self.din["ev_w_in"].rearrange("(k p) n -> p k n", p=128)
        wsw = self.din["ev_w_in_sw"].rearrange("(k p) n -> p k n", p=128)
        if which == "na":
            wbq = A.alloc("wbq", [128, KC, 512], BF16)
        else:
            wq = A.alloc("wq", [128, KC, 512], BF16)
            wqs = A.alloc("wqs", [128, KC, 512], BF16)
        for hh in range(2):
            ks = slice(hh * 4, hh * 4 + 4)
            if which == "na":
                self.load(wbq.ap[:, ks, :], wbq.buf, wsrc[:, ks, 1536:2048], self.dbuf["ev_w_in"], q="pool")
            else:
                self.load(wq.ap[:, ks, :], wq.buf, wsrc[:, ks, 0:512], self.dbuf["ev_w_in"], q="pool")
                self.load(wqs.ap[:, ks, :], wqs.buf, wsw[:, ks, 0:512], self.dbuf["ev_w_in_sw"], q="pool")
        xin_r = Ring(A, "xin", [128, 4, D], F32, 2)
        uT_r = Ring(A, "uT", [128, KC, 512], BF16, 2)
        if which != "na":
            cd_r = Ring(A, "cd", [128, 512], F32, 2)
            sd_r = Ring(A, "sd", [128, 512], F32, 2)
            t1_r = Ring(A, "t1", [128, 512], F32, 2)
            t2_r = Ring(A, "t2", [128, 512], F32, 2)
        for blk in range(5):
            ctx = blk == 4
            ntok = 256 if ctx else 512
            tok0 = blk * 512
            r = 1 if ctx else 0
            xin, uT = xin_r.next(), uT_r.next()
            src = self.din["ctxb"] if ctx else self.din["xown"][tok0:tok0 + 512, :]
            sbuf = self.dbuf["ctxb"] if ctx else self.dbuf["xown"]
            self.load_xT(src, sbuf, ntok, xin, uT, self.sc1[0], self.sh1[0], r, pbanks=(0, 1))
            if which == "na":
                for pr in range(4):
                    pa, pab = self.bank(2 + pr % 2)
                    for kc in range(KC):
                        self.mm(pa[:, 0:ntok], wbq.ap[:, kc, pr * 128:(pr + 1) * 128], uT.ap[:, kc, 0:ntok], kc == 0, kc == KC - 1,
                                [wbq.buf, uT.buf], [pab])
                    self.cp(self.QTn.ap[0:64, 2 * pr, tok0:tok0 + ntok], pa[0:64, 0:ntok], [pab], [self.QTn.buf], eng="act")
                    self.cp(self.QTn.ap[64:128, 2 * pr + 1, tok0:tok0 + ntok], pa[64:128, 0:ntok], [pab], [self.QTn.buf], eng="dve")
                continue
            if not ctx:
                cd, sd = cd_r.next(), sd_r.next()
                self.load(cd.ap, cd.buf, self.din["rope_cd_own"][:, tok0:tok0 + 512], self.dbuf["rope_cd_own"])
                self.load(sd.ap, sd.buf, self.din["rope_sd_own"][:, tok0:tok0 + 512], self.dbuf["rope_sd_own"])
            for h in range(4):
                pa, pab = self.bank(2 + (h % 2) * 2)
                pb, pbb = self.bank(3 + (h % 2) * 2)
                for kc in range(KC):
                    self.mm(pa[:, 0:ntok], wq.ap[:, kc, h * 128:(h + 1) * 128], uT.ap[:, kc, 0:ntok], kc == 0, kc == KC - 1,
                            [wq.buf, uT.buf], [pab])
                if not ctx:
                    for kc in range(KC):
                        self.mm(pb[:, 0:ntok], wqs.ap[:, kc, h * 128:(h + 1) * 128], uT.ap[:, kc, 0:ntok], kc == 0, kc == KC - 1,
                                [wqs.buf, uT.buf], [pbb])
                    t1, t2 = t1_r.next(), t2_r.next()
                    self.tt(t1.ap, pa, cd.ap, ALU.mult, [pab, cd.buf], [t1.buf])
                    self.tt(t2.ap, pb, sd.ap, ALU.mult, [pbb, sd.buf], [t2.buf])
                    for m in range(2):
                        rows = slice(64 * m, 64 * m + 64)
                        self.tt(self.QTd.ap[rows, h, m, tok0:tok0 + ntok], t1.ap[rows, :], t2.ap[rows, :], ALU.add,
                                [t1.buf, t2.buf], [self.QTd.buf])
                else:
                    for m in range(2):
                        rows = slice(64 * m, 64 * m + 64)
                        self.cp(self.QTd.ap[rows, h, m, tok0:tok0 + ntok], pa[rows, 0:ntok], [pab], [self.QTd.buf])
        P.barrier()
        A.release(m0)

    def phase_na(self):
        A, P = self.A, self.P
        tb = A.alloc("tb", [128, 8 * 14 * 64], F32)
        self.load(tb.ap, tb.buf, self.din["na_tb"], self.dbuf["na_tb"])
        self.ts(tb.ap, tb.ap, 8.0, None, ALU.mult, None, [tb.buf], [tb.buf])
        tbv = tb.ap.rearrange("p (h x) -> p h x", h=8)
        qm = A.alloc("qm", [128, TOWN], BF16)
        km = A.alloc("km", [128, NNA], BF16)
        self.memset(qm, 0.0)
        self.memset(km, 0.0)
        self.load(qm.ap[0:64, :], qm.buf, self.din["na_qm"], self.dbuf["na_qm"])
        self.load(km.ap[0:64, :], km.buf, self.din["na_km"], self.dbuf["na_km"])
        sb_r = Ring(A, "nasb", [128, 896], F32, 2)
        e_r = Ring(A, "nae", [128, 1152], BF16, 3)
        rz_r = Ring(A, "narz", [64, 128], F32, 2)
        steps = []
        it = 0
        for qt in range(NOWN // 128):
            ctxq = qt >= 16
            qs = slice(qt * 128, (qt + 1) * 128)
            for h in range(8):
                sw, swb = self.pp[it % 2]
                sc, scb = self.bank(4 + it % 2)
                ob, obb = self.bank(6 + it % 2)
                it += 1

                def front(qt=qt, h=h, ctxq=ctxq, qs=qs, sw=sw, swb=swb, sc=sc, scb=scb):
                    pr = h // 2
                    e = e_r.next()
                    if not ctxq:
                        for j in range(7):
                            kt = qt + 6 - j
                            cols = slice(j * 128, (j + 1) * 128)
                            bb_ = swb[j // 4]
                            self.mm(sw[:, cols], self.KTn.ap[:, pr, kt * 128:(kt + 1) * 128], self.QTn.ap[:, h, qs], True, False,
                                    [self.KTn.buf, self.QTn.buf], [bb_])
                            self.mm(sw[:, cols], km.ap[:, kt * 128:(kt + 1) * 128], qm.ap[:, qs], False, True, [km.buf, qm.buf], [bb_])
                        sb = sb_r.next()
                        self.tt(sb.ap, sw[:, 0:896], tbv[:, h, :], ALU.add, swb + [tb.buf], [sb.buf])
                        self.act(e.ap[:, 0:896], sb.ap, AF.Exp, [sb.buf], [e.buf], scale=0.125)
                    for c in range(2):
                        self.mm(sc[:, c * 128:(c + 1) * 128], self.KTnc.ap[:, pr, c * 128:(c + 1) * 128], self.QTn.ap[:, h, qs], True, True,
                                [self.KTnc.buf, self.QTn.buf], [scb])
                    self.act(e.ap[:, 896:1152], sc[:, 0:256], AF.Exp, [scb], [e.buf], scale=0.125)
                    return e

                def back(e, qt=qt, h=h, ctxq=ctxq, qs=qs, ob=ob, obb=obb):
                    pr, half = h // 2, h % 2
                    nk = 0 if ctxq else 7
                    for j in range(nk):
                        kt = qt + 6 - j
                        self.mm(ob[:, 0:128], self.Vn.ap[:, kt, h, :], e.ap[:, j * 128:(j + 1) * 128], j == 0, False,
                                [self.Vn.buf, e.buf], [obb])
                    for c in range(2):
                        self.mm(ob[:, 0:128], self.Vnc.ap[:, c, h, :], e.ap[:, 896 + c * 128:896 + (c + 1) * 128], (nk == 0 and c == 0), c == 1,
                                [self.Vnc.buf, e.buf], [obb])
                    rz = rz_r.next()
                    self.recip(rz.ap, ob[64:128, 0:128], [obb], [rz.buf])
                    self.tt(self.mixT.ap[64 * half:64 * half + 64, 4 + pr, qs], ob[0:64, 0:128], rz.ap, ALU.mult,
                            [obb, rz.buf], [self.mixT.buf])
                steps.append((front, back))
        prev = None
        for (fr, bk) in steps:
            e = fr()
            if prev is not None:
                prev[0](prev[1])
            prev = (bk, e)
        prev[0](prev[1])
        P.barrier()
        A.release(self.m_att)

    def phase_diff(self):
        A, P = self.A, self.P
        lam_in = A.alloc("lam_in", [128, 256], F32)
        self.load(lam_in.ap, lam_in.buf, self.din["diff_lambda"].partition_broadcast(128), self.dbuf["diff_lambda"])
        lp = A.alloc("lam_p", [128, 128], F32)
        self.tt(lp.ap[:, 0:64], lam_in.ap[:, 0:64], lam_in.ap[:, 64:128], ALU.mult, [lam_in.buf], [lp.buf])
        self.tt(lp.ap[:, 64:128], lam_in.ap[:, 128:192], lam_in.ap[:, 192:256], ALU.mult, [lam_in.buf], [lp.buf])
        ls = A.alloc("lam_s", [128, 4], F32)
        self.P.op("dve", lambda e: e.reduce_sum(out=ls.ap[:, 0:2], in_=lp.ap.rearrange("p (a d) -> p a d", a=2), axis=AX.X),
                  reads=[lp.buf], writes=[ls.buf])
        self.act(ls.ap[:, 2:4], ls.ap[:, 0:2], AF.Exp, [ls.buf], [ls.buf])
        nlam = A.alloc("nlam", [128, 1], F32)
        self.stt(nlam.ap, ls.ap[:, 3:4], -0.2, ls.ap[:, 2:3], ALU.add, ALU.subtract, [ls.buf], [nlam.buf])
        gsub = A.alloc("gsub", [128, 1], F32)
        self.load(gsub.ap, gsub.buf, self.din["diff_subln_g"].rearrange("(p o) -> p o", o=1), self.dbuf["diff_subln_g"], slow=True)
        self.ts(gsub.ap, gsub.ap, 0.8, None, ALU.mult, None, [gsub.buf], [gsub.buf])
        kt_r = Ring(A, "KTh", [128, NKEY], BF16, 2)
        v_r = Ring(A, "Vh", [128, NKT, 128], BF16, 2)
        e_r = Ring(A, "dE", [128, 2, 512], BF16, 4)
        f_r = Ring(A, "dF", [128, 512], F32, 6)
        za_r = Ring(A, "dZ", [128, 2, 512], F32, 2)
        qblocks = [(i * 512, 512, list(range(NKT))) for i in range(4)] + [(TOWN, LCTX, [NKT - 2, NKT - 1])]
        heads = []
        for h in range(4):
            KT, VH = kt_r.next(), v_r.next()
            heads.append((KT, VH))

        def load_head(h):
            KT, VH = heads[h]
            for rnk in range(4):
                self.load(KT.ap[:, rnk * TOWN:(rnk + 1) * TOWN], KT.buf, self.din[f"KTG{h}"][rnk * 128:(rnk + 1) * 128, :], self.dbuf[f"KTG{h}"])
                self.load(VH.ap[:, rnk * 16:(rnk + 1) * 16, :], VH.buf,
                          self.din[f"VG{h}"][rnk * 128:(rnk + 1) * 128, :].rearrange("p (t d) -> p t d", d=128), self.dbuf[f"VG{h}"])
            self.load(KT.ap[:, SEQ:NKEY], KT.buf, self.din["KTC"][h], self.dbuf["KTC"])
            self.load(VH.ap[:, 64:66, :], VH.buf, self.din["VC"][h], self.dbuf["VC"])

        def post(h, q0, nq, O, za):
            pz, pzb = self.bank(7)
            r0, r1, t0_, t1b, dd, sq = [f_r.next() for _ in range(6)]
            n = slice(0, nq)
            self.mm(pz[:, n], self.ones_f.ap, za.ap[:, 0, n], True, True, [self.ones_f.buf, za.buf], [pzb])
            self.recip(r0.ap[:, n], pz[:, n], [pzb], [r0.buf])
            self.mm(pz[:, n], self.ones_f.ap, za.ap[:, 1, n], True, True, [self.ones_f.buf, za.buf], [pzb])
            self.recip(r1.ap[:, n], pz[:, n], [pzb], [r1.buf])
            self.tt(t0_.ap[:, n], O[0][0][:, n], r0.ap[:, n], ALU.mult, [O[0][1], r0.buf], [t0_.buf])
            self.tt(t1b.ap[:, n], O[1][0][:, n], r1.ap[:, n], ALU.mult, [O[1][1], r1.buf], [t1b.buf])
            self.stt(dd.ap[:, n], t1b.ap[:, n], nlam.ap[:, 0:1], t0_.ap[:, n], ALU.mult, ALU.add,
                     [t1b.buf, t0_.buf, nlam.buf], [dd.buf])
            self.act(sq.ap[:, n], dd.ap[:, n], AF.Square, [dd.buf], [sq.buf])
            self.mm(pz[:, n], self.ones_f.ap, sq.ap[:, n], True, True, [self.ones_f.buf, sq.buf], [pzb])
            self.act(r0.ap[:, n], pz[:, n], AF.Sqrt, [pzb], [r0.buf], bias=self.eps_rms.ap[:, 0:1], scale=1.0 / 128)
            self.recip(r1.ap[:, n], r0.ap[:, n], [r0.buf], [r1.buf])
            self.stt(self.mixT.ap[:, h, q0:q0 + nq], dd.ap[:, n], gsub.ap[:, 0:1], r1.ap[:, n], ALU.mult, ALU.mult,
                     [dd.buf, gsub.buf, r1.buf], [self.mixT.buf])

        steps = []
        sidx = [0]
        oset = [0]
        load_head(0)
        for h in range(4):
            KT, VH = heads[h]
            for qi, (q0, nq, kts) in enumerate(qblocks):
                ob = 3 + 2 * (oset[0] % 2)
                oset[0] += 1
                O = [self.bank(ob), self.bank(ob + 1)]
                za = za_r.next()
                for ki, kt in enumerate(kts):
                    def front(h=h, KT=KT, q0=q0, nq=nq, kt=kt, ki=ki, za=za, qi=qi):
                        if qi == 0 and ki == 1 and h + 1 < 4:
                            load_head(h + 1)
                        e = e_r.next()
                        for m in range(2):
                            sap, sbuf_ = self.bank(sidx[0] % 3)
                            sidx[0] += 1
                            self.mm(sap[:, 0:nq], KT.ap[:, kt * 128:(kt + 1) * 128], self.QTd.ap[:, h, m, q0:q0 + nq], True, True,
                                    [KT.buf, self.QTd.buf], [sbuf_])
                            self.act(e.ap[:, m, 0:nq], sap[:, 0:nq], AF.Exp, [sbuf_], [e.buf], scale=0.125)
                        if ki == 0:
                            self.cp(za.ap[:, :, 0:nq], e.ap[:, :, 0:nq], [e.buf], [za.buf])
                        else:
                            self.tt(za.ap[:, :, 0:nq], za.ap[:, :, 0:nq], e.ap[:, :, 0:nq], ALU.add, [za.buf, e.buf], [za.buf])
                        return e

                    def back(es, h=h, VH=VH, q0=q0, nq=nq, kt=kt, ki=ki, nk=len(kts), O=O, za=za):
                        for m in range(2):
                            self.mm(O[m][0][:, 0:nq], VH.ap[:, kt, :], es.ap[:, m, 0:nq], ki == 0, ki == nk - 1,
                                    [VH.buf, es.buf], [O[m][1]])
                        if ki == nk - 1:
                            post(h, q0, nq, O, za)
                    steps.append((front, back))
        prev = None
        for (fr, bk) in steps:
            es = fr()
            if prev is not None:
                prev[0](prev[1])
            prev = (bk, es)
        prev[0](prev[1])
        P.barrier()
        A.release(self.m_att)

    def ln_block(self, hT, c0, ntok, gi, bi, G, B, r, uT, u0, st):
        sq, zb, mean, msq, var, rstd, nmr = st
        zs = hT.ap[:, :, c0:c0 + ntok]
        s1, s1b = self.bank(6)
        s2, s2b = self.bank(7)
        self.act(sq.ap[:, :, 0:ntok], zs, AF.Square, [hT.buf], [sq.buf])
        self.cp(zb.ap[:, :, 0:ntok], zs, [hT.buf], [zb.buf], eng="pool")
        for kc in range(KC):
            self.mm(s1[:, 0:ntok], self.ones_b.ap, zb.ap[:, kc, 0:ntok], kc == 0, kc == KC - 1, [self.ones_b.buf, zb.buf], [s1b])
        for kc in range(KC):
            self.mm(s2[:, 0:ntok], self.ones_b.ap, sq.ap[:, kc, 0:ntok], kc == 0, kc == KC - 1, [self.ones_b.buf, sq.buf], [s2b])
        n = slice(0, ntok)
        self.ts(mean.ap[:, n], s1[:, n], 1.0 / D, None, ALU.mult, None, [s1b], [mean.buf])
        self.tt(msq.ap[:, n], mean.ap[:, n], mean.ap[:, n], ALU.mult, [mean.buf], [msq.buf])
        self.stt(var.ap[:, n], s2[:, n], 1.0 / D, msq.ap[:, n], ALU.mult, ALU.subtract, [s2b, msq.buf], [var.buf])
        self.act(msq.ap[:, n], var.ap[:, n], AF.Sqrt, [var.buf], [msq.buf], bias=self.eps_ln.ap[:, 0:1], scale=1.0)
        self.recip(rstd.ap[:, n], msq.ap[:, n], [msq.buf], [rstd.buf])
        self.stt(nmr.ap[:, n], mean.ap[:, n], -1.0, rstd.ap[:, n], ALU.mult, ALU.mult, [mean.buf, rstd.buf], [nmr.buf])
        for kc in range(KC):
            z = hT.ap[:, kc, c0:c0 + ntok]
            self.tt(z, z, rstd.ap[:, n], ALU.mult, [hT.buf, rstd.buf], [hT.buf])
            self.tt(z, z, nmr.ap[:, n], ALU.add, [hT.buf, nmr.buf], [hT.buf])
            if uT is not None:
                self.act(uT.ap[:, kc, u0:u0 + ntok], z, AF.Identity, [hT.buf, G.buf, B.buf], [uT.buf],
                         bias=B.ap[:, kc, r:r + 1], scale=G.ap[:, kc, r:r + 1])
            self.ts(z, z, self.vT.ap[:, kc, gi:gi + 1], self.vT.ap[:, kc, bi:bi + 1], ALU.mult, ALU.add,
                    [hT.buf, self.vT.buf], [hT.buf])

    def alloc_ln_state(self):
        A = self.A
        sq = A.alloc("ln_sq", [128, KC, 512], BF16)
        zb = A.alloc("ln_zb", [128, KC, 512], BF16)
        rest = [A.alloc(f"ln_{n}", [128, 512], F32) for n in ("mean", "msq", "var", "rstd", "nmr")]
        return [sq, zb] + rest

    def phase_o(self, l, w_name, first, nblk=5, mix_dram=None):
        A, P = self.A, self.P
        if first:
            self.hT = A.alloc("hT", [128, KC, NOWN], F32)
        m0 = A.mark()
        mx_r = Ring(A, "mxblk", [128, KC, 512], BF16, 2) if mix_dram is not None else None
        wo = A.alloc("wo", [128, KC, D], BF16)
        wsrc = self.din[w_name].rearrange("(k p) n -> p k n", p=128)
        for hh in range(4):
            ks = slice(hh * 2, hh * 2 + 2)
            self.load(wo.ap[:, ks, :], wo.buf, wsrc[:, ks, :], self.dbuf[w_name], q="pool")
        st = self.alloc_ln_state()
        xin = A.alloc("xin_o", [128, 4, D], F32) if first else None
        pend_ln = []
        for blk in range(nblk):
            ctx = blk == 4
            ntok = 256 if ctx else 512
            tok0 = blk * 512
            r = 1 if ctx else 0
            if first:
                src = self.din["ctxb"] if ctx else self.din["xown"][tok0:tok0 + 512, :]
                sbuf = self.dbuf["ctxb"] if ctx else self.dbuf["xown"]
                self.load_xT(src, sbuf, ntok, xin, None, None, None, r, hT=self.hT, h_off=tok0, pbanks=(0, 1))
            if mix_dram is not None:
                mx = mx_r.next()
                for hh in range(2):
                    self.load(mx.ap[:, hh * 4:hh * 4 + 4, :], mx.buf,
                              self.din[mix_dram][hh * 4:hh * 4 + 4, :, tok0:tok0 + 512].rearrange("k p n -> p k n"), self.dbuf[mix_dram])
                mxa = lambda kc: mx.ap[:, kc, 0:ntok]
                mxb = mx.buf
            else:
                mxa = lambda kc: self.mixT.ap[:, kc, tok0:tok0 + ntok]
                mxb = self.mixT.buf
            for oc in range(KC):
                yb, ybb = self.bank(2 + oc % 4)
                for kc in range(KC):
                    self.mm(yb[:, 0:ntok], wo.ap[:, kc, oc * 128:(oc + 1) * 128], mxa(kc),
                            kc == 0, kc == KC - 1, [wo.buf, mxb], [ybb])
                z = self.hT.ap[:, oc, tok0:tok0 + ntok]
                self.stt(z, yb[:, 0:ntok], self.gt1[l].ap[:, oc, r:r + 1], z, ALU.mult, ALU.add,
                         [ybb, self.gt1[l].buf, self.hT.buf], [self.hT.buf])
            pend_ln.append((tok0, ntok, r))
            if len(pend_ln) > 1:
                t0_, n_, r_ = pend_ln.pop(0)
                self.ln_block(self.hT, t0_, n_, 4 * l + 0, 4 * l + 1, self.G2[l], self.B2[l], r_, self.mixT, t0_, st)
        for (t0_, n_, r_) in pend_ln:
            self.ln_block(self.hT, t0_, n_, 4 * l + 0, 4 * l + 1, self.G2[l], self.B2[l], r_, self.mixT, t0_, st)
        P.barrier()
        A.release(m0)

    def phase_f(self, l, Gn, Bn, ntb=5):
        A, P = self.A, self.P
        m0 = A.mark()
        st = self.alloc_ln_state()
        hf = A.alloc("hffn", [128, HC, 512], BF16)
        wg_r = Ring(A, "wg", [128, KC, 128], BF16, 3)
        wa_r = Ring(A, "wa", [128, KC, 128], BF16, 3)
        w2_r = Ring(A, "w2", [128, HC, 128], BF16, 2)
        sg_r = Ring(A, "sg", [128, 512], F32, 2)
        w1src = self.din["ffn_w_in"][l].rearrange("(k p) n -> p k n", p=128)
        w2src = self.din["ffn_w_out"][l].rearrange("(c p) n -> p c n", p=128)
        pidx = 0
        pend = None
        for blk in range(ntb):
            ctx = blk == 4
            ntok = 256 if ctx else 512
            tok0 = blk * 512
            r = 1 if ctx else 0
            for hc in range(HC):
                wg, wa = wg_r.next(), wa_r.next()
                self.load(wg.ap, wg.buf, w1src[:, :, hc * 128:(hc + 1) * 128], self.dbuf["ffn_w_in"], q="pool")
                self.load(wa.ap, wa.buf, w1src[:, :, FFN_H + hc * 128:FFN_H + (hc + 1) * 128], self.dbuf["ffn_w_in"], q="pool")
                gb, gbb = self.bank(pidx % 4)
                ab, abb = self.bank((pidx + 1) % 4)
                pidx += 2
                for kc in range(KC):
                    self.mm(gb[:, 0:ntok], wg.ap[:, kc, :], self.mixT.ap[:, kc, tok0:tok0 + ntok], kc == 0, kc == KC - 1,
                            [wg.buf, self.mixT.buf], [gbb])
                for kc in range(KC):
                    self.mm(ab[:, 0:ntok], wa.ap[:, kc, :], self.mixT.ap[:, kc, tok0:tok0 + ntok], kc == 0, kc == KC - 1,
                            [wa.buf, self.mixT.buf], [abb])
                sg = sg_r.next()
                self.act(sg.ap[:, 0:ntok], gb[:, 0:ntok], AF.Silu, [gbb], [sg.buf])
                self.tt(hf.ap[:, hc, 0:ntok], ab[:, 0:ntok], sg.ap[:, 0:ntok], ALU.mult, [abb, sg.buf], [hf.buf])
            if pend is not None:
                self.ln_block(self.hT, pend[0], pend[1], 4 * l + 2, 4 * l + 3, Gn, Bn, pend[2], self.mixT if Gn is not None else None, pend[0], st)
                pend = None
            for oc in range(KC):
                w2 = w2_r.next()
                self.load(w2.ap, w2.buf, w2src[:, :, oc * 128:(oc + 1) * 128], self.dbuf["ffn_w_out"], q="pool")
                yb, ybb = self.bank(4 + oc % 2)
                for hc in range(HC):
                    self.mm(yb[:, 0:ntok], w2.ap[:, hc, :], hf.ap[:, hc, 0:ntok], hc == 0, hc == HC - 1, [w2.buf, hf.buf], [ybb])
                z = self.hT.ap[:, oc, tok0:tok0 + ntok]
                self.stt(z, yb[:, 0:ntok], self.gt2[l].ap[:, oc, r:r + 1], z, ALU.mult, ALU.add,
                         [ybb, self.gt2[l].buf, self.hT.buf], [self.hT.buf])
            pend = (tok0, ntok, r)
        if pend is not None:
            self.ln_block(self.hT, pend[0], pend[1], 4 * l + 2, 4 * l + 3, Gn, Bn, pend[2], self.mixT if Gn is not None else None, pend[0], st)
        P.barrier()
        A.release(m0)

    def phase_out(self, dst_name, ntok_total):
        A, P = self.A, self.P
        m0 = A.mark()
        o_r = Ring(A, "orow", [128, D], F32, 3)
        evs = []
        for t in range(ntok_total // 128):
            o = o_r.next()
            for half in range(2):
                pb, pbb = self.bank((2 * t + half) % 4)
                for j in range(4):
                    kc = half * 4 + j
                    self.P.op("pe", (lambda oo, ii: lambda e: e.transpose(oo, ii, self.ident.ap))(
                        pb[:, j * 128:(j + 1) * 128], self.hT.ap[:, kc, t * 128:(t + 1) * 128]),
                        reads=[self.hT.buf, self.ident.buf], writes=[pbb])
                self.cp(o.ap[:, half * 512:(half + 1) * 512], pb, [pbb], [o.buf], eng=("act" if half else "dve"))
            evs.append(self.store(self.din[dst_name][t * 128:(t + 1) * 128, :], self.dbuf[dst_name], o.ap, o.buf))
        A.release(m0)
        return evs

    def phase_p1(self, lat_only=False, mid=None):
        A, P = self.A, self.P
        LATP = [self.scratch("LATA", [128, TOWN], BF16), self.scratch("LATB", [128, TOWN], BF16), self.scratch("LATK", [32, TOWN], BF16)]
        LATN = ["LATA", "LATB", "LATK"]
        LATC = self.scratch("LATC", [288, LCTX], BF16)
        if not lat_only:
            MIX1 = self.scratch("MIX1", [KC, 128, TOWN], BF16)
            self.cqn = A.alloc("cqn", [128, 3, TOWN], BF16)
        wsrc = self.din["od_w_in"].rearrange("(k p) n -> p k n", p=128)

        def proj_T(wt, c0, ncol, bank_i, tok0, ntok):
            bap, bbuf = self.bank(bank_i)
            for kc in range(KC):
                self.mm(bap[0:ncol, 0:ntok], wt.ap[:, kc, c0:c0 + ncol], self.mixT.ap[:, kc, tok0:tok0 + ntok], kc == 0, kc == KC - 1,
                        [wt.buf, self.mixT.buf], [bbuf])
            return bap, bbuf

        mW = A.mark()
        w1b = None if lat_only else A.alloc("w1b", [128, KC, 1024], BF16)
        m0 = A.mark()
        w1 = A.alloc("w1a", [128, KC, 672], BF16)
        for hh in range(2):
            ks = slice(hh * 4, hh * 4 + 4)
            self.load(w1.ap[:, ks, :], w1.buf, wsrc[:, ks, 0:672], self.dbuf["od_w_in"], q="pool")
        wkrs = A.alloc("wkrs", [128, KC, 32], BF16)
        self.load(wkrs.ap, wkrs.buf, self.din["od_w_in_krsw"].rearrange("(k p) n -> p k n", p=128), self.dbuf["od_w_in_krsw"], q="pool")
        if w1b is not None:
            for hh in range(4):
                ks = slice(hh * 2, hh * 2 + 2)
                self.load(w1b.ap[:, ks, :], w1b.buf, wsrc[:, ks, 672:1696], self.dbuf["od_w_in"], q="pool")
        f_r = Ring(A, "p1f", [128, 3, 512], F32, 2)
        s_r = Ring(A, "p1s", [128, 3, 512], F32, 2)
        r_r = Ring(A, "p1r", [128, 512], F32, 4)
        lat_r = Ring(A, "latst", [128, 2, 512], BF16, 2)
        kr_r = Ring(A, "krst", [32, 512], BF16, 2)
        tq_r = Ring(A, "p1tab", [32, 2, 512], F32, 2)
        k1_r = Ring(A, "p1k", [32, 2, 512], F32, 1)

        def rms_norm_T(ps_list, nch, ntok, nfeat, outs):
            f, s_ = f_r.next(), s_r.next()
            for c in range(nch):
                self.cp(f.ap[:, c, 0:ntok], ps_list[c][0][:, 0:ntok], [ps_list[c][1]], [f.buf], eng="dve")
                self.act(s_.ap[:, c, 0:ntok], ps_list[c][0][:, 0:ntok], AF.Square, [ps_list[c][1]], [s_.buf])
            sb_, sbb_ = self.bank(7)
            for c in range(nch):
                self.mm(sb_[:, 0:ntok], self.ones_f.ap, s_.ap[:, c, 0:ntok], c == 0, c == nch - 1, [self.ones_f.buf, s_.buf], [sbb_])
            r0, r1 = r_r.next(), r_r.next()
            self.act(r0.ap[:, 0:ntok], sb_[:, 0:ntok], AF.Sqrt, [sbb_], [r0.buf], bias=self.eps_rms.ap[:, 0:1], scale=1.0 / nfeat)
            self.recip(r1.ap[:, 0:ntok], r0.ap[:, 0:ntok], [r0.buf], [r1.buf])
            for c in range(nch):
                oap, obuf = outs[c]
                self.tt(oap, f.ap[:, c, 0:ntok], r1.ap[:, 0:ntok], ALU.mult, [f.buf, r1.buf], [obuf])

        for blk in range(5):
            ctx = blk == 4
            ntok = 256 if ctx else 512
            tok0 = blk * 512
            ps = [proj_T(w1, 384 + c * 128, 128, c, tok0, ntok) for c in range(2)]
            lat = lat_r.next()
            rms_norm_T(ps, 2, ntok, 256, [(lat.ap[:, c, 0:ntok], lat.buf) for c in range(2)])
            if ctx:
                self.store(LATC[0:256, 0:ntok].rearrange("(c p) n -> p c n", p=128), self.dbuf["LATC"], lat.ap[:, :, 0:ntok], lat.buf)
            else:
                for c in range(2):
                    self.store(LATP[c][:, tok0:tok0 + ntok], self.dbuf[LATN[c]], lat.ap[:, c, 0:ntok], lat.buf)
            pa, pab = proj_T(w1, 640, 32, 2, tok0, ntok)
            krs = kr_r.next()
            if not ctx:
                pb, pbb = proj_T(wkrs, 0, 32, 3, tok0, ntok)
                tq = tq_r.next()
                self.load(tq.ap[:, 0, :], tq.buf, self.din["ropeq_c_own"][64:96, tok0:tok0 + 512], self.dbuf["ropeq_c_own"])
                self.load(tq.ap[:, 1, :], tq.buf, self.din["ropeq_s_own"][64:96, tok0:tok0 + 512], self.dbuf["ropeq_s_own"])
                k1 = k1_r.next()
                self.tt(k1.ap[:, 0, :], pa[0:32, :], tq.ap[:, 0, :], ALU.mult, [pab, tq.buf], [k1.buf])
                self.tt(k1.ap[:, 1, :], pb[0:32, :], tq.ap[:, 1, :], ALU.mult, [pbb, tq.buf], [k1.buf])
                self.tt(krs.ap, k1.ap[:, 0, :], k1.ap[:, 1, :], ALU.add, [k1.buf], [krs.buf])
            else:
                self.cp(krs.ap[:, 0:ntok], pa[0:32, 0:ntok], [pab], [krs.buf])
            if ctx:
                self.store(LATC[256:288, 0:ntok], self.dbuf["LATC"], krs.ap[:, 0:ntok], krs.buf)
            else:
                self.store(LATP[2][:, tok0:tok0 + ntok], self.dbuf["LATK"], krs.ap[:, 0:ntok], krs.buf)
            if ctx or lat_only:
                continue
            ps = [proj_T(w1, c * 128, 128, 4 + c, tok0, ntok) for c in range(3)]
            rms_norm_T(ps, 3, ntok, 384, [(self.cqn.ap[:, c, tok0:tok0 + ntok], self.cqn.buf) for c in range(3)])
        P.barrier()
        A.release(m0)
        if lat_only:
            return
        if mid is not None:
            mid()
        m0 = A.mark()
        w1 = w1b
        gbc = A.alloc("gln_g", [128, 512], F32)
        bbc = A.alloc("gln_b", [128, 512], F32)
        gbb = A.alloc("gm_b", [128, 512], F32)
        self.load(gbc.ap, gbc.buf, self.din["gmlp_ln_g"].partition_broadcast(128), self.dbuf["gmlp_ln_g"])
        self.load(bbc.ap, bbc.buf, self.din["gmlp_ln_b"].partition_broadcast(128), self.dbuf["gmlp_ln_b"])
        self.load(gbb.ap, gbb.buf, self.din["gmlp_b"].partition_broadcast(128), self.dbuf["gmlp_b"])
        wsT = A.alloc("wsT", [128, 4, 128], BF16)
        wss = A.alloc("wss", [128, 4, 128], F32)
        self.load(wss.ap, wss.buf, self.din["gmlp_ws"].rearrange("g i j -> i g j"), self.dbuf["gmlp_ws"])
        tb_, tbb_ = self.bank(0)
        for g in range(4):
            self.P.op("pe", (lambda o, i: lambda e: e.transpose(o, i, self.ident.ap))(tb_[:, g * 128:(g + 1) * 128], wss.ap[:, g, :]),
                      reads=[wss.buf, self.ident.buf], writes=[tbb_])
        self.cp(wsT.ap.rearrange("p g i -> p (g i)"), tb_, [tbb_], [wsT.buf])
        gu_r = Ring(A, "gu", [128, 4, 512], BF16, 2)
        gvf_r = Ring(A, "gvf", [128, 512], F32, 2)
        vg_r = Ring(A, "vg", [128, 512], BF16, 2)
        gm_r = Ring(A, "gm", [128, 512], F32, 2)
        go_r = Ring(A, "go", [128, 4, 128], BF16, 3)
        st_r = Ring(A, "bnst", [128, 16], F32, 2)
        eps_g = A.alloc("eps_g", [128, 1], F32)
        self.memset(eps_g, LN_EPS)
        for blk in range(4):
            ntok = 512
            tok0 = blk * 512
            gu = gu_r.next()
            for g in range(4):
                bap, bbuf = proj_T(w1, g * 128, 128, g % 4, tok0, ntok)
                self.act(gu.ap[:, g, :], bap, AF.Gelu, [bbuf], [gu.buf])
            for t in range(4):
                pv, pvb = self.bank(4 + t % 2)
                for kc in range(KC):
                    self.mm(pv, self.mixT.ap[:, kc, tok0 + t * 128:tok0 + (t + 1) * 128], w1.ap[:, kc, 512:1024], kc == 0, kc == KC - 1,
                            [self.mixT.buf, w1.buf], [pvb])
                gvf = gvf_r.next()
                self.act(gvf.ap, pv, AF.Gelu, [pvb], [gvf.buf])
                stt_ = st_r.next()
                self.P.op("dve", (lambda o, i: lambda e: e.bn_stats(out=o, in_=i))(stt_.ap[:, 0:6], gvf.ap), reads=[gvf.buf], writes=[stt_.buf])
                self.P.op("dve", (lambda o, i: lambda e: e.bn_aggr(out=o, in_=i))(stt_.ap[:, 6:8], stt_.ap[:, 0:6]), reads=[stt_.buf], writes=[stt_.buf])
                self.act(stt_.ap[:, 8:9], stt_.ap[:, 7:8], AF.Sqrt, [stt_.buf], [stt_.buf], bias=eps_g.ap[:, 0:1], scale=1.0)
                self.recip(stt_.ap[:, 9:10], stt_.ap[:, 8:9], [stt_.buf], [stt_.buf])
                self.stt(stt_.ap[:, 10:11], stt_.ap[:, 6:7], -1.0, stt_.ap[:, 9:10], ALU.mult, ALU.mult, [stt_.buf], [stt_.buf])
                self.ts(gvf.ap, gvf.ap, stt_.ap[:, 9:10], stt_.ap[:, 10:11], ALU.mult, ALU.add, [gvf.buf, stt_.buf], [gvf.buf])
                self.tt(gvf.ap, gvf.ap, gbc.ap, ALU.mult, [gvf.buf, gbc.buf], [gvf.buf])
                vg = vg_r.next()
                self.tt(vg.ap, gvf.ap, bbc.ap, ALU.add, [gvf.buf, bbc.buf], [vg.buf])
                pm, pmb = self.bank(6)
                for g in range(4):
                    self.mm(pm[:, g * 128:(g + 1) * 128], vg.ap[:, g * 128:(g + 1) * 128], wsT.ap[:, g, :], True, True, [vg.buf, wsT.buf], [pmb])
                gm = gm_r.next()
                self.tt(gm.ap, pm, gbb.ap, ALU.add, [pmb, gbb.buf], [gm.buf])
                go = go_r.next()
                self.tt(go.ap, gm.ap.rearrange("p (g i) -> p g i", g=4), gu.ap[:, :, t * 128:(t + 1) * 128], ALU.mult, [gm.buf, gu.buf], [go.buf])
                c0 = tok0 + t * 128
                self.store(MIX1[4:8, :, c0:c0 + 128].rearrange("g p i -> p g i"), self.dbuf["MIX1"], go.ap, go.buf)
        P.barrier()
        A.release(mW)

    def exchange(self, mode):
        if mode == "B":
            self.inp("latg", [4 * 288, TOWN], BF16)
            lg, lb = self.din["latg"], self.dbuf["latg"]
            self.lat_piece = lambda r, p: (lg[r * 288 + (0, 128, 256)[p]:r * 288 + (128, 256, 288)[p], :], lb)
            return
        names = [("LATA", "LGA", 128), ("LATB", "LGB", 128), ("LATK", "LGK", 32)]
        for (sn, dn, rows) in names:
            self.scratch(dn, [4 * rows, TOWN], BF16)
            src, dst = self.din[sn], self.din[dn]
            self.P.dma("pool", (lambda s_, d_: lambda e: e.collective_compute(
                "AllGather", ALU.bypass, replica_groups=[[0, 1, 2, 3], [4, 5, 6, 7]], ins=[s_.opt()], outs=[d_.opt()]))(src, dst),
                reads=[self.dbuf[sn]], writes=[self.dbuf[dn]], inc=1, dedicated=True)
        self.lat_piece = lambda r, p: (self.din[names[p][1]][r * names[p][2]:(r + 1) * names[p][2], :], self.dbuf[names[p][1]])

    def phase_k1(self):
        A, P = self.A, self.P
        KN = self.scratch("KN1", [8 * 64, NKEY], BF16)
        V1 = self.scratch("V1", [8, 128, NKT, 65], BF16)
        LC, lcb = self.din["LATC"], self.dbuf["LATC"]
        m0 = A.mark()
        ck = A.alloc("ckall", [128, 2, NKEY], BF16)
        ckb = [Buf(f"ck{r}") for r in range(5)]
        for rnk in range(4):
            for c in range(2):
                pap, pbuf = self.lat_piece(rnk, c)
                self.load(ck.ap[:, c, rnk * TOWN:(rnk + 1) * TOWN], ckb[rnk], pap, pbuf, q=("sp" if c == 0 else "act"))
        for c in range(2):
            self.load(ck.ap[:, c, SEQ:NKEY], ckb[4], LC[c * 128:(c + 1) * 128, :], lcb)
        wst = A.alloc("wukv_f", [128, 2, 1024], F32)
        self.load(wst.ap, wst.buf, self.din["mla_w_ukv"].rearrange("(c p) n -> p c n", p=128), self.dbuf["mla_w_ukv"])
        wkv = A.alloc("wukv", [128, 2, 2, 512], BF16)
        for c in range(2):
            src4 = wst.ap[:, c, :].rearrange("p (h x) -> p h x", h=8)
            for kv in range(2):
                self.ts(wkv.ap[:, c, kv, :].rearrange("p (h d) -> p h d", h=8), src4[:, :, kv * 64:(kv + 1) * 64],
                        self.vT.ap[:, c, 21:22], None, ALU.mult, None, [wst.buf, self.vT.buf], [wkv.buf])
        kt_r = Ring(A, "k1kt", [128, 512], BF16, 6)
        vt_r = Ring(A, "k1vt", [128, 8, 8, 65], BF16, 2)
        for v in vt_r.items:
            self.memset(v, 1.0)
        bi = 0
        vt = None
        si = 0
        for T in range(NKT):
            if T % 8 == 0:
                vt = vt_r.next()
            cb = ckb[min(T // 16, 4)]
            pv, pvb = self.bank(4 + T % 4)
            for c in range(2):
                self.mm(pv, ck.ap[:, c, T * 128:(T + 1) * 128], wkv.ap[:, c, 1, :],
                        c == 0, c == 1, [cb, wkv.buf], [pvb])
            self.cp(vt.ap[:, :, T % 8, 0:64], pv.rearrange("p (h d) -> p h d", h=8), [pvb], [vt.buf], eng=("act" if T % 2 else "dve"))
            if T % 8 == 7 or T == NKT - 1:
                tg = (T // 8) * 8
                nt = T - tg + 1
                for h in range(8):
                    self.store(V1[h, :, tg:tg + nt, :], self.dbuf["V1"], vt.ap[:, h, 0:nt, :], vt.buf, q=("sp" if si % 2 == 0 else "act"))
                    si += 1
        for hp in range(4):
            for blk in range(17):
                ntok = 256 if blk == 16 else 512
                tok0 = blk * 512
                cb = ckb[min(blk // 4, 4)]
                pa, pab = self.bank(bi % 4)
                bi += 1
                for c in range(2):
                    self.mm(pa[:, 0:ntok], wkv.ap[:, c, 0, hp * 128:(hp + 1) * 128], ck.ap[:, c, tok0:tok0 + ntok], c == 0, c == 1,
                            [wkv.buf, cb], [pab])
                kt = kt_r.next()
                self.cp(kt.ap[:, 0:ntok], pa[:, 0:ntok], [pab], [kt.buf], eng=("act" if blk % 2 else "dve"))
                self.store(KN[hp * 128:(hp + 1) * 128, tok0:tok0 + ntok], self.dbuf["KN1"], kt.ap[:, 0:ntok], kt.buf, q=("sp" if si % 2 == 0 else "act"))
                si += 1
        P.barrier()
        A.release(m0)

    def phase_mla(self):
        A, P = self.A, self.P
        KN, V1, MIX1 = self.din["KN1"], self.din["V1"], self.din["MIX1"]
        m0 = A.mark()
        wq = [A.alloc(nm + "_b", [128, 3, 768], BF16) for nm in ("mla_w_uq", "mla_w_uq_sw")]
        m1 = A.mark()
        for i, nm in enumerate(("mla_w_uq", "mla_w_uq_sw")):
            wst = A.alloc(nm + "_f", [128, 3, 768], F32)
            self.load(wst.ap, wst.buf, self.din[nm].rearrange("(c p) n -> p c n", p=128), self.dbuf[nm])
            for c in range(3):
                self.ts(wq[i].ap[:, c, :], wst.ap[:, c, :], self.vT.ap[:, c, 20:21], None, ALU.mult, None, [wst.buf, self.vT.buf], [wq[i].buf])
        P.barrier()
        A.release(m1)
        tab_r = Ring(A, "qtab", [96, 2, 512], F32, 2)
        kt_r = Ring(A, "KT1h", [96, 8, 1], BF16, 1)
        kt_r.items = [A.alloc_at(f"KT1h{i}", [96, NKEY], BF16, self.mixT.off + i * NKEY * 2) for i in range(2)]
        v_r = Ring(A, "V1h", [128, NKT, 65], BF16, 2)
        q_r = Ring(A, "Q1h", [96, TOWN], BF16, 2)
        e_r = Ring(A, "mE", [128, 1024], BF16, 4)
        t_r = Ring(A, "mT", [96, 512], F32, 4)
        oc_r = Ring(A, "mO", [64, 512], F32, 2)
        rz_r = Ring(A, "mZ", [1, 512], F32, 2)
        on_r = Ring(A, "mN", [64, 512], BF16, 3)
        scale = 96.0 ** -0.5
        heads = [(kt_r.items[h % 2], v_r.next(), q_r.next()) for h in range(8)]

        for i in range(2):
            KTb = kt_r.items[i]
            for rnk in range(4):
                pap, pbuf = self.lat_piece(rnk, 2)
                self.load(KTb.ap[64:96, rnk * TOWN:(rnk + 1) * TOWN], KTb.buf, pap, pbuf)
            self.load(KTb.ap[64:96, SEQ:NKEY], KTb.buf, self.din["LATC"][256:288, :], self.dbuf["LATC"])

        def prep_head(h):
            KT, VH, QH = heads[h]
            for part in range(3):
                c0, c1 = part * 2816, (part + 1) * 2816
                self.load(KT.ap[0:64, c0:c1], KT.buf, KN[h * 64:(h + 1) * 64, c0:c1], self.dbuf["KN1"], q="act")
                t0, t1_ = part * 22, (part + 1) * 22
                self.load(VH.ap[:, t0:t1_, :], VH.buf, V1[h, :, t0:t1_, :], self.dbuf["V1"])
            for qb in range(4):
                qs = slice(qb * 512, (qb + 1) * 512)
                pa, pab = self.bank(6)
                pb, pbb = self.bank(7)
                for c in range(3):
                    self.mm(pa[0:96, :], wq[0].ap[:, c, h * 96:(h + 1) * 96], self.cqn.ap[:, c, qs], c == 0, c == 2, [wq[0].buf, self.cqn.buf], [pab])
                for c in range(3):
                    self.mm(pb[0:96, :], wq[1].ap[:, c, h * 96:(h + 1) * 96], self.cqn.ap[:, c, qs], c == 0, c == 2, [wq[1].buf, self.cqn.buf], [pbb])
                t1, t2 = t_r.next(), t_r.next()
                tab = tab_r.next()
                self.load(tab.ap[:, 0, :], tab.buf, self.din["ropeq_c_own"][:, qs], self.dbuf["ropeq_c_own"])
                self.load(tab.ap[:, 1, :], tab.buf, self.din["ropeq_s_own"][:, qs], self.dbuf["ropeq_s_own"])
                self.tt(t1.ap, pa[0:96, :], tab.ap[:, 0, :], ALU.mult, [pab, tab.buf], [t1.buf])
                self.tt(t2.ap, pb[0:96, :], tab.ap[:, 1, :], ALU.mult, [pbb, tab.buf], [t2.buf])
                self.tt(QH.ap[:, qs], t1.ap, t2.ap, ALU.add, [t1.buf, t2.buf], [QH.buf])

        def post(h, qs, ob, obb):
            oc, rz, on = oc_r.next(), rz_r.next(), on_r.next()
            self.cp(oc.ap, ob[0:64, :], [obb], [oc.buf], eng="act")
            self.recip(rz.ap, ob[64:65, :], [obb], [rz.buf])
            zb, zbb = self.bank(6)
            self.mm(zb[0:64, :], self.ones_f.ap[0:1, 0:64], rz.ap[0:1, :], True, True, [self.ones_f.buf, rz.buf], [zbb])
            self.tt(on.ap, oc.ap, zb[0:64, :], ALU.mult, [oc.buf, zbb], [on.buf])
            pr, half = h // 2, h % 2
            self.store(MIX1[pr, 64 * half:64 * half + 64, qs], self.dbuf["MIX1"], on.ap, on.buf)

        steps = []
        sidx = [0]
        oidx = [0]
        prep_head(0)
        for h in range(8):
            KT, VH, QH = heads[h]
            for qb in range(4):
                qs = slice(qb * 512, (qb + 1) * 512)
                ob, obb = self.bank(4 + oidx[0] % 2)
                oidx[0] += 1
                for kp in range(NKT // 2):
                    def front(h=h, KT=KT, QH=QH, qs=qs, kp=kp, qb=qb):
                        if qb == 0 and kp == 2 and h + 1 < 8:
                            prep_head(h + 1)
                        sap, sbufs = self.pp[sidx[0] % 2]
                        sidx[0] += 1
                        for j in range(2):
                            kt = 2 * kp + j
                            self.mm(sap[:, j * 512:(j + 1) * 512], KT.ap[0:96, kt * 128:(kt + 1) * 128], QH.ap[0:96, qs], True, True,
                                    [KT.buf, QH.buf], [sbufs[j]])
                        e = e_r.next()
                        self.act(e.ap, sap, AF.Exp, sbufs, [e.buf], scale=scale)
                        return e

                    def back(e, h=h, VH=VH, qs=qs, kp=kp, ob=ob, obb=obb):
                        for j in range(2):
                            kt = 2 * kp + j
                            self.mm(ob[0:65, :], VH.ap[:, kt, :], e.ap[:, j * 512:(j + 1) * 512], (kp == 0 and j == 0),
                                    (kp == NKT // 2 - 1 and j == 1), [VH.buf, e.buf], [obb])
                        if kp == NKT // 2 - 1:
                            post(h, qs, ob, obb)
                    steps.append((front, back))
        prev = None
        for (fr, bk) in steps:
            e = fr()
            if prev is not None:
                prev[0](prev[1])
            prev = (bk, e)
        prev[0](prev[1])
        P.barrier()
        A.release(m0)

    def phase_load_h(self):
        A, P = self.A, self.P
        self.mixT = A.alloc("mixT", [128, KC, NOWN], BF16)
        self.hT = A.alloc("hT", [128, KC, NOWN], F32)
        m0 = A.mark()
        xin_r = Ring(A, "xin", [128, 4, D], F32, 2)
        for blk in range(5):
            ctx = blk == 4
            ntok = 256 if ctx else 512
            tok0 = blk * 512
            r = 1 if ctx else 0
            xin = xin_r.next()
            self.load_xT(self.din["h_own"][tok0:tok0 + ntok, :], self.dbuf["h_own"], ntok, xin,
                         V(self.mixT.ap[:, :, tok0:tok0 + ntok], self.mixT.buf), self.sc1[1], self.sh1[1], r,
                         hT=self.hT, h_off=tok0, pbanks=(0, 1))
        P.barrier()
        A.release(m0)


def build(mode):
    kb = KB(mode)
    kb.declare_inputs()
    evs = []
    if mode in ("A", "F"):
        kb.phase_const(); kb.phase_mod()
        kb.phase_k0()
        kb.phase_q("na"); kb.phase_kna(); kb.phase_na()
        kb.phase_q("diff"); kb.phase_diff()
        kb.phase_o(0, "ev_w_out", True)
        kb.phase_f(0, kb.G3, kb.B3)
    if mode == "A":
        kb.outp("out_h", [NOWN, D])
        evs += kb.phase_out("out_h", NOWN)
        kb.P.barrier()
        kb.phase_p1(lat_only=True)
        kb.outp("lat_out", [288, TOWN], BF16)
        for (nm, r0, r1) in (("LATA", 0, 128), ("LATB", 128, 256), ("LATK", 256, 288)):
            evs.append(kb.P.dma("sp", (lambda o, i: lambda e: e.dma_start(out=o, in_=i))(kb.din["lat_out"][r0:r1, :], kb.din[nm]),
                                reads=[kb.dbuf[nm]], writes=[kb.dbuf["lat_out"]]))
    if mode == "B":
        kb.inp("h_own", [NOWN, D])
        kb.phase_const(); kb.phase_mod()
        kb.phase_load_h()
    if mode in ("B", "F"):
        kb.phase_p1(mid=lambda: kb.exchange(mode))
        kb.phase_k1()
        kb.phase_mla()
        kb.phase_o(1, "od_w_out", False, nblk=4, mix_dram="MIX1")
        kb.phase_f(1, None, None, ntb=4)
        kb.outp("out", [TOWN, D])
        evs += kb.phase_out("out", TOWN)
    kb.P.emit(final_events=evs)
    return kb


def rope_tables():
    t = np.arange(SEQ); row = (t // 64).astype(np.float32); col = (t % 64).astype(np.float32)
    inv32 = np.power(np.float32(10000.0), -np.arange(0, 32, 2, dtype=np.float32) / np.float32(32)).astype(np.float32)
    cd = np.zeros((64, SEQ), np.float32); sd = np.zeros((64, SEQ), np.float32)
    for d in range(64):
        pos = row if d < 32 else col
        j = d % 16
        ang = (pos * inv32[j]).astype(np.float32)
        cd[d] = np.cos(ang); s = np.sin(ang)
        sd[d] = -s if (d % 32) < 16 else s
    cd = np.concatenate([cd, cd], 0); sd = np.concatenate([sd, sd], 0)
    inv16 = np.power(np.float32(10000.0), -np.arange(0, 16, 2, dtype=np.float32) / np.float32(16)).astype(np.float32)
    cm = np.zeros((32, SEQ), np.float32); sm = np.zeros((32, SEQ), np.float32)
    for d in range(32):
        pos = row if d < 16 else col
        j = d % 8
        ang = (pos * inv16[j]).astype(np.float32)
        cm[d] = np.cos(ang); s = np.sin(ang)
        sm[d] = -s if (d % 16) < 8 else s
    cq = np.concatenate([np.ones((64, SEQ), np.float32), cm], 0)
    sq = np.concatenate([np.zeros((64, SEQ), np.float32), sm], 0)
    return cd, sd, cq, sq

def swap_perm(n, blk):
    idx = np.arange(n); h = blk // 2
    return np.where((idx % blk) < h, idx + h, idx - h)

def na_tables(rpb, q0):
    ck = np.arange(64)[:, None]; cq = np.arange(64)[None, :]
    cs = np.clip(cq - 8, 0, 48)
    colvalid = (ck >= cs) & (ck < cs + 16)
    coff = np.clip(ck - cq + 15, 0, 30)
    tb = np.zeros((2, 64, 8, 14, 64), np.float32)
    for a in range(2):
        for s in range(14):
            dr = 6 - s + a
            for h in range(8):
                blk = rpb[h, dr + 7][coff]
                tb[a, :, h, s, :] = np.where(colvalid, blk, np.float32(NEG))
    tb = tb.reshape(128, 8 * 14 * 64)
    qm = np.zeros((64, TOWN), np.float32); km = np.zeros((64, NNA), np.float32)
    r0 = q0 // 64
    iq = np.arange(TOWN); rq = r0 + iq // 64
    rs = np.clip(rq - 4, 0, 120)
    for rho in range(44):
        rk = r0 - 6 + rho
        valid = (rk >= 0) & (rk < 128) & (rk >= rs) & (rk < rs + 8)
        qm[rho] = np.where(valid, 0.0, NEG)
        km[rho, rho * 64:(rho + 1) * 64] = 1.0
    return tb, qm.astype(ml_dtypes.bfloat16), km.astype(ml_dtypes.bfloat16)

def core_inputs(INP, core, tabs):
    b, q0 = core // 4, (core % 4) * TOWN
    cd, sd, cq, sq = tabs
    x = INP['x'][b]
    xna = np.zeros((NNA, D), np.float32)
    lo, hi = q0 - 384, q0 - 384 + NNA
    slo, shi = max(lo, 0), min(hi, SEQ)
    xna[slo - lo:shi - lo] = x[slo:shi]
    v = np.zeros((24, 1024), np.float32)
    for l in range(2):
        v[4*l+0] = INP['ln_mix_g'][l]; v[4*l+1] = INP['ln_mix_b'][l]; v[4*l+2] = INP['ln_ffn_g'][l]; v[4*l+3] = INP['ln_ffn_b'][l]
        v[8+6*l:14+6*l] = INP['mod_b'][l].reshape(6, 1024)
    v[20, :384] = INP['mla_q_norm_g'][0]; v[21, :256] = INP['mla_kv_norm_g'][0]
    v[22] = INP['c'][b]; v[23] = INP['c_ctx']
    ev = INP['ev_w_in'][0]
    p64 = swap_perm(1024, 32)
    tb, qm, km = na_tables(INP['na_rpb'][0], q0)
    d = {
        "xall": x, "ctxb": INP['ctx'][b], "xown": x[q0:q0 + TOWN], "xna": xna, "vecs": v,
        "ident": np.eye(128, dtype=np.float32), "mod_w": INP['mod_w'],
        "ev_w_in": ev, "ev_w_in_sw": np.ascontiguousarray(ev[:, :1024][:, p64]), "ev_w_out": INP['ev_w_out'][0],
        "ffn_w_in": INP['ffn_w_in'], "ffn_w_out": INP['ffn_w_out'],
        "diff_lambda": INP['diff_lambda'][0].reshape(256), "diff_subln_g": INP['diff_subln_g'][0],
        "rope_cd": cd, "rope_sd": sd, "rope_cd_own": np.ascontiguousarray(cd[:, q0:q0 + TOWN]),
        "rope_sd_own": np.ascontiguousarray(sd[:, q0:q0 + TOWN]),
        "na_tb": tb, "na_qm": qm, "na_km": km,
    }
    return d


def core_inputs_l1(INP, core, tabs):
    b, q0 = core // 4, (core % 4) * TOWN
    cd, sd, cq, sq = tabs
    od = INP['od_w_in'][0]
    p32 = swap_perm(32, 16)
    wuq = INP['mla_w_uq'][0]
    wuq_sw = np.zeros_like(wuq)
    for h in range(8):
        c0 = h * 96 + 64
        wuq_sw[:, c0:c0 + 32] = wuq[:, c0:c0 + 32][:, p32]
    return {
        "od_w_in": od, "od_w_in_krsw": np.ascontiguousarray(od[:, 640:672][:, p32]), "od_w_out": INP['od_w_out'][0],
        "mla_w_uq": wuq, "mla_w_uq_sw": wuq_sw, "mla_w_ukv": INP['mla_w_ukv'][0],
        "gmlp_ln_g": INP['gmlp_ln_g'][0], "gmlp_ln_b": INP['gmlp_ln_b'][0], "gmlp_ws": INP['gmlp_ws'][0],
        "gmlp_b": INP['gmlp_b'][0].reshape(512),
        "ropeq_c_own": np.ascontiguousarray(cq[:, q0:q0 + TOWN]), "ropeq_s_own": np.ascontiguousarray(sq[:, q0:q0 + TOWN]),
    }


MODE = "F"
_CACHE = {}


def _get(mode):
    if mode not in _CACHE:
        _CACHE[mode] = build(mode)
    return _CACHE[mode]


def kernel(**inputs):
    INP = {k: np.asarray(v) for k, v in inputs.items()}
    tabs = rope_tables()
    per_core = []
    for core in range(8):
        d = core_inputs(INP, core, tabs)
        d.update(core_inputs_l1(INP, core, tabs))
        per_core.append(d)
    out = np.zeros((2, SEQ, D), np.float32)
    if MODE == "F":
        kb = _get("F")
        maps = [{k: v for k, v in d.items() if k in kb.din} for d in per_core]
        res = run_bass_kernel_spmd(kb.nc, maps, core_ids=list(range(8)))
        outs = [r["out"] for r in res.results]
    else:
        ka = _get("A")
        maps = [{k: v for k, v in d.items() if k in ka.din} for d in per_core]
        ra = run_bass_kernel_spmd(ka.nc, maps, core_ids=list(range(8))).results
        kbb = _get("B")
        maps = []
        for core in range(8):
            g = core // 4
            d = {k: v for k, v in per_core[core].items() if k in kbb.din}
            d["h_own"] = np.asarray(ra[core]["out_h"])
            d["latg"] = np.concatenate([np.asarray(ra[4 * g + r]["lat_out"]) for r in range(4)], 0)
            maps.append(d)
        rb = run_bass_kernel_spmd(kbb.nc, maps, core_ids=list(range(8))).results
        outs = [r["out"] for r in rb]
    for core in range(8):
        b, q0 = core // 4, (core % 4) * TOWN
        out[b, q0:q0 + TOWN] = np.asarray(outs[core])
    return out
```

```python
import ml_dtypes
import contextlib
import numpy as np
import concourse.bass as bass
import concourse.mybir as mybir
from concourse.bass_utils import run_bass_kernel_spmd

F32 = mybir.dt.float32
BF16 = mybir.dt.bfloat16
AF = mybir.ActivationFunctionType
ALU = mybir.AluOpType
AX = mybir.AxisListType

SAME_ENGINE_SYNC = True


class Buf:
    _n = 0

    def __init__(self, name=""):
        Buf._n += 1
        self.name = f"{name}_{Buf._n}"
        self.last_w = None
        self.rd_eng = {}
        self.rd_dma = []
        self.sem = None
        self.cnt = 0
        self.excl = False


class Prog:
    ENGS = ("pe", "act", "dve", "pool", "sp")

    def __init__(self, nc):
        self.nc = nc
        self.q = {e: [] for e in self.ENGS}
        self.slots = []
        self.free_slots = []
        self.epoch_bufs = []
        self.last_ev = {e: None for e in self.ENGS}
        self.open_dma = []

    def _deps(self, reads, writes):
        deps = []
        for b in reads:
            if b.last_w is not None:
                deps.append(b.last_w)
            if b.excl:
                deps.extend(b.rd_eng.values())
        for b in writes:
            if b.last_w is not None:
                deps.append(b.last_w)
            deps.extend(b.rd_eng.values())
            deps.extend(b.rd_dma)
        return deps

    def _commit(self, ev, reads, writes):
        for b in reads:
            if ev[0] == "e":
                b.rd_eng[ev[1]] = ev
            else:
                b.rd_dma.append(ev)
        for b in writes:
            b.last_w = ev
            b.rd_eng = {}
            b.rd_dma = []

    def op(self, eng, fn, reads=(), writes=(), extra=()):
        deps = self._deps(reads, writes) + list(extra)
        idx = len(self.q[eng])
        ev = ("e", eng, idx)
        self.q[eng].append({"fn": fn, "deps": deps, "needed": False, "sem": None})
        self._commit(ev, reads, writes)
        if fn is not None:
            self.last_ev[eng] = ev
        return ev

    def dma(self, eng, fn, reads=(), writes=(), sembuf=None, inc=16, extra=(), dedicated=False, defer=False):
        deps = self._deps(reads, writes) + list(extra)
        sb = sembuf if sembuf is not None else writes[0]
        if dedicated:
            sb = Buf("dedicated")
            sb.sem = {"cnt": 0, "sem": None, "id": len(self.slots)}
            self.slots.append(sb.sem)
        if sb.sem is None:
            if self.free_slots:
                sb.sem = self.free_slots.pop()
            else:
                sb.sem = {"cnt": 0, "sem": None, "id": len(self.slots)}
                self.slots.append(sb.sem)
            self.epoch_bufs.append(sb)
        slot = sb.sem
        if not hasattr(self, "dma_eng"):
            self.dma_eng = {}
        drop = [b.last_w for b in writes if b.last_w is not None and b.last_w[0] == "d" and b.last_w[1] is slot
                and b is sb and self.dma_eng.get((slot["id"], b.last_w[2])) == eng]
        if drop:
            deps = [d for d in deps if not any(d is x for x in drop)]
        slot["cnt"] += inc
        ev = ("d", slot, slot["cnt"])
        self.dma_eng[(slot["id"], slot["cnt"])] = eng
        self.q[eng].append({"fn": fn, "deps": deps, "needed": False, "sem": slot, "inc": inc})
        self._commit(ev, reads, writes)
        if not defer:
            self.open_dma.append(ev)
        return ev

    def barrier(self):
        mx = {}
        for d in self.open_dma:
            k = d[1]["id"]
            if k not in mx or mx[k][2] < d[2]:
                mx[k] = d
        evs = [v for v in self.last_ev.values() if v is not None] + list(mx.values())
        self.open_dma = []
        for b in self.epoch_bufs:
            self.free_slots.append(b.sem)
            b.sem = None
        self.epoch_bufs = []
        for e in self.ENGS:
            self.op(e, None, extra=evs)

    def emit(self, final_events=()):
        nc = self.nc
        skip_same = lambda d, e: d[0] == "e" and d[1] == e and (e == "pe" or not SAME_ENGINE_SYNC)
        for e in self.ENGS:
            for o in self.q[e]:
                for d in o["deps"]:
                    if d[0] == "e" and not skip_same(d, e):
                        self.q[d[1]][d[2]]["needed"] = True
        for d in final_events:
            if d[0] == "e":
                self.q[d[1]][d[2]]["needed"] = True
        for e in self.ENGS:
            for i, o in enumerate(self.q[e]):
                if o["fn"] is None and o["needed"]:
                    raise RuntimeError("wait-only op used as dependency")
        for e in self.ENGS:
            c = 0
            for o in self.q[e]:
                if o["needed"]:
                    c += 1
                    o["val"] = c
        self.n_wait = 0
        self.n_ins = 0
        with contextlib.ExitStack() as st:
            esem = {e: st.enter_context(nc.semaphore(f"s_{e}")) for e in self.ENGS}
            for sl in self.slots:
                sl["sem"] = st.enter_context(nc.semaphore(f"d_slot{sl['id']}"))
            block = st.enter_context(nc.Block())

            def run(e, engobj, extra_final=()):
                waited = {}

                def wait_for(d):
                    if skip_same(d, e):
                        return
                    if d[0] == "e":
                        key = ("e", d[1])
                        sem = esem[d[1]]
                        val = self.q[d[1]][d[2]]["val"]
                    else:
                        key = ("d", d[1]["id"])
                        sem = d[1]["sem"]
                        val = d[2]
                    if waited.get(key, 0) >= val:
                        return
                    waited[key] = val
                    engobj.wait_ge(sem, val)
                    self.n_wait += 1

                for o in self.q[e]:
                    for d in o["deps"]:
                        wait_for(d)
                    if o["fn"] is None:
                        continue
                    ins = o["fn"](engobj)
                    self.n_ins += 1
                    if o["sem"] is not None:
                        if o["inc"] == 1:
                            ins.then_inc(o["sem"]["sem"])
                        else:
                            ins.then_inc(o["sem"]["sem"], o["inc"])
                    elif o["needed"]:
                        ins.then_inc(esem[e], 1)
                for d in extra_final:
                    wait_for(d)

            @block.tensor
            def _(t):
                run("pe", t)

            @block.scalar
            def _(s):
                run("act", s)

            @block.vector
            def _(v):
                run("dve", v)

            @block.gpsimd
            def _(g):
                run("pool", g)

            @block.sync
            def _(s):
                run("sp", s, final_events)


D = 1024
KC = 8
SEQ = 8192
LCTX = 256
TOWN = 2048
NOWN = TOWN + LCTX
NKEY = SEQ + LCTX
NKT = NKEY // 128
NNA = 2816
ALPHA = 4.0 ** 0.25
LN_EPS = 1e-5
RMS_EPS = 1e-6
NEG = -30000.0
FFN_H = 2816
HC = FFN_H // 128
NVEC = 24


def _prod(xs):
    r = 1
    for x in xs:
        r *= x
    return r


class Arena:
    def __init__(self, nc, base=16384, limit=16384 + 212000):
        self.nc, self.base, self.top, self.limit = nc, base, base, limit
        self.n = 0
        self.peak = base

    def alloc(self, name, shape, dtype):
        sz = 2 if dtype == BF16 else 4
        nbytes = _prod(shape[1:]) * sz
        off = (self.top + 31) // 32 * 32
        self.top = off + nbytes
        self.peak = max(self.peak, self.top)
        assert self.top <= self.limit, f"SBUF arena overflow at {name}: {self.top - self.base}"
        self.n += 1
        h = self.nc.alloc_sbuf_tensor_at(f"{name}_{self.n}", list(shape), dtype, offset=off)
        t = TT(h)
        t.off = off
        return t

    def alloc_at(self, name, shape, dtype, off):
        self.n += 1
        h = self.nc.alloc_sbuf_tensor_at(f"{name}_{self.n}", list(shape), dtype, offset=off)
        t = TT(h)
        t.off = off
        return t

    def mark(self):
        return self.top

    def release(self, m):
        self.top = m


class V:
    def __init__(self, ap, buf):
        self.ap, self.buf = ap, buf


class TT:
    def __init__(self, h, buf=None):
        self.h = h
        self.ap = h.ap()
        self.buf = buf if buf is not None else Buf(h.name)

    def __getitem__(self, k):
        return self.ap[k]


class Ring:
    def __init__(self, arena, name, shape, dtype, n):
        self.items = [arena.alloc(f"{name}{i}", shape, dtype) for i in range(n)]
        self.i = 0

    def next(self):
        t = self.items[self.i % len(self.items)]
        self.i += 1
        return t


class KB:
    def __init__(self, mode):
        self.mode = mode
        nc = bass.Bass("TRN2", target_bir_lowering=False)
        self.nc = nc
        self.P = Prog(nc)
        self.A = Arena(nc)
        self.din = {}
        self.dbuf = {}
        self.pp = []
        for i in range(4):
            h = nc.alloc_psum_tensor(f"ps{i}", [128, 1024], F32)
            self.pp.append((h.ap(), [Buf(f"ps{i}a"), Buf(f"ps{i}b")]))
            for b in self.pp[-1][1]:
                b.excl = True

    def bank(self, k):
        ap, bufs = self.pp[k // 2]
        j = k % 2
        return ap[:, j * 512:(j + 1) * 512], bufs[j]

    def inp(self, name, shape, dtype=F32):
        t = self.nc.dram_tensor(name, list(shape), dtype, kind="ExternalInput")
        self.din[name] = t.ap()
        self.dbuf[name] = Buf(name)
        return t.ap()

    def outp(self, name, shape, dtype=F32):
        t = self.nc.dram_tensor(name, list(shape), dtype, kind="ExternalOutput")
        self.din[name] = t.ap()
        self.dbuf[name] = Buf(name)
        return t.ap()

    def scratch(self, name, shape, dtype):
        t = self.nc.dram_tensor(name, list(shape), dtype)
        self.din[name] = t.ap()
        self.dbuf[name] = Buf(name)
        return t.ap()

    def mm(self, out, lhsT, rhs, start, stop, rd, wr):
        self.P.op("pe", lambda e: e.matmul(out, lhsT=lhsT, rhs=rhs, start=start, stop=stop),
                  reads=rd, writes=wr)

    def act(self, out, in_, func, rd, wr, bias=None, scale=None):
        kw = {}
        if bias is not None:
            kw["bias"] = bias
        if scale is not None:
            kw["scale"] = scale
        return self.P.op("act", lambda e: e.activation(out=out, in_=in_, func=func, **kw),
                         reads=rd, writes=wr)

    def tt(self, out, in0, in1, op, rd, wr, eng="dve"):
        return self.P.op(eng, lambda e: e.tensor_tensor(out=out, in0=in0, in1=in1, op=op),
                         reads=rd, writes=wr)

    def ts(self, out, in0, s1, s2, op0, op1, rd, wr, eng="dve"):
        if op1 is None:
            return self.P.op(eng, lambda e: e.tensor_scalar(out=out, in0=in0, scalar1=s1, scalar2=None, op0=op0),
                             reads=rd, writes=wr)
        return self.P.op(eng, lambda e: e.tensor_scalar(out=out, in0=in0, scalar1=s1, scalar2=s2, op0=op0, op1=op1),
                         reads=rd, writes=wr)

    def stt(self, out, in0, scalar, in1, op0, op1, rd, wr):
        return self.P.op("dve", lambda e: e.scalar_tensor_tensor(out=out, in0=in0, scalar=scalar, in1=in1, op0=op0, op1=op1),
                         reads=rd, writes=wr)

    def cp(self, out, in_, rd, wr, eng="dve"):
        if eng == "act":
            return self.P.op("act", lambda e: e.copy(out=out, in_=in_), reads=rd, writes=wr)
        return self.P.op(eng, lambda e: e.tensor_copy(out=out, in_=in_), reads=rd, writes=wr)

    def recip(self, out, in_, rd, wr):
        return self.P.op("dve", lambda e: e.reciprocal(out=out, in_=in_), reads=rd, writes=wr)

    def memset(self, t, val, eng="dve"):
        ap = t.ap
        return self.P.op(eng, lambda e: e.memset(ap, val), writes=[t.buf])

    def load(self, t_ap, t_buf, src_ap, src_buf, q="sp", slow=False):
        kw = {"allow_slow_non_contiguous": True} if slow else {}
        return self.P.dma(q, lambda e: e.dma_start(out=t_ap, in_=src_ap, **kw), reads=[src_buf], writes=[t_buf])

    def store(self, dst_ap, dst_buf, t_ap, t_buf, q="sp"):
        return self.P.dma(q, lambda e: e.dma_start(out=dst_ap, in_=t_ap), reads=[t_buf], writes=[dst_buf], sembuf=t_buf)

    def declare_inputs(self):
        I = self.inp
        mode = self.mode
        I("vecs", [NVEC, D]); I("ident", [128, 128]); I("mod_w", [2, D, 6 * D])
        I("ffn_w_in", [2, D, 2 * FFN_H]); I("ffn_w_out", [2, FFN_H, D])
        if mode in ("A", "F", "dbg"):
            I("ctxb", [LCTX, D]); I("xown", [TOWN, D]); I("xna", [NNA, D])
            I("ev_w_in", [D, 3072]); I("ev_w_in_sw", [D, 1024]); I("ev_w_out", [D, D])
            I("diff_lambda", [256]); I("diff_subln_g", [128])
            I("rope_cd_own", [128, TOWN]); I("rope_sd_own", [128, TOWN])
            I("na_tb", [128, 8 * 14 * 64]); I("na_qm", [64, TOWN], BF16); I("na_km", [64, NNA], BF16)
        I("od_w_in", [D, 1696]); I("od_w_in_krsw", [D, 32])
        I("ropeq_c_own", [96, TOWN]); I("ropeq_s_own", [96, TOWN])
        if mode in ("B", "F", "dbg"):
            I("od_w_out", [D, D])
            I("mla_w_uq", [384, 768]); I("mla_w_uq_sw", [384, 768]); I("mla_w_ukv", [256, 1024])
            I("gmlp_ln_g", [512]); I("gmlp_ln_b", [512]); I("gmlp_ws", [4, 128, 128]); I("gmlp_b", [512])

    def phase_const(self):
        A, P = self.A, self.P
        self.ident = A.alloc("ident", [128, 128], F32)
        self.load(self.ident.ap, self.ident.buf, self.din["ident"], self.dbuf["ident"])
        self.ones_f = A.alloc("ones_f", [128, 128], F32)
        self.memset(self.ones_f, 1.0)
        self.ones_b = A.alloc("ones_b", [128, 128], BF16)
        self.memset(self.ones_b, 1.0)
        self.eps_rms = A.alloc("eps_rms", [128, 1], F32)
        self.memset(self.eps_rms, RMS_EPS)
        self.eps_ln = A.alloc("eps_ln", [128, 1], F32)
        self.memset(self.eps_ln, LN_EPS / (ALPHA * ALPHA))
        self.vT = A.alloc("vT", [128, KC, NVEC], F32)
        self.sT = A.alloc("sT", [128, KC, 2], BF16)
        m0 = A.mark()
        vs = A.alloc("vecs_sb", [NVEC, D], F32)
        self.load(vs.ap, vs.buf, self.din["vecs"], self.dbuf["vecs"])
        bap, bb = self.bank(0)
        for kc in range(KC):
            self.mm(bap[:, kc * NVEC:(kc + 1) * NVEC], vs.ap[0:NVEC, kc * 128:(kc + 1) * 128],
                    self.ident.ap[0:NVEC, 0:NVEC], True, True, [vs.buf, self.ident.buf], [bb])
        self.cp(self.vT.ap, bap[:, 0:KC * NVEC].rearrange("p (k r) -> p k r", r=NVEC), [bb], [self.vT.buf])
        self.act(self.sT.ap, self.vT.ap[:, :, 22:24], AF.Silu, [self.vT.buf], [self.sT.buf])
        P.barrier()
        A.release(m0)

    def phase_mod(self):
        A, P = self.A, self.P
        self.m = [A.alloc(f"m{l}", [128, 6, KC, 2], F32) for l in range(2)]
        m0 = A.mark()
        ring = Ring(A, "modw", [128, KC, 1024], BF16, 2)
        for l in range(2):
            bap, bb = self.bank(l)
            for g in range(6):
                w = ring.next()
                src = self.din["mod_w"][l, :, g * 1024:(g + 1) * 1024].rearrange("(k p) n -> p k n", p=128)
                for hh in range(2):
                    self.load(w.ap[:, hh * 4:(hh + 1) * 4, :], w.buf, src[:, hh * 4:(hh + 1) * 4, :], self.dbuf["mod_w"], q="pool")
                for oc in range(8):
                    col = (g * 8 + oc) * 2
                    for kc in range(KC):
                        self.mm(bap[:, col:col + 2], w.ap[:, kc, oc * 128:(oc + 1) * 128], self.sT.ap[:, kc, :],
                                kc == 0, kc == KC - 1, [w.buf, self.sT.buf], [bb])
            pv = bap[:, 0:96].rearrange("p (g k r) -> p g k r", g=6, k=8)
            for g in range(6):
                for r in range(2):
                    self.tt(self.m[l].ap[:, g, :, r], pv[:, g, :, r], self.vT.ap[:, :, 8 + l * 6 + g], ALU.add,
                            [bb, self.vT.buf], [self.m[l].buf])
        A.release(m0)
        def newv(name):
            return A.alloc(name, [128, KC, 2], F32)
        self.sc1, self.gt1, self.sc2, self.gt2 = [], [], [], []
        self.sh1 = [V(self.m[l].ap[:, 0], self.m[l].buf) for l in range(2)]
        self.G2, self.B2, self.G3, self.B3 = [], [], [], []
        for l in range(2):
            m = self.m[l]
            sc1 = newv(f"sc1_{l}"); self.ts(sc1.ap, m.ap[:, 1], 1.0, None, ALU.add, None, [m.buf], [sc1.buf])
            gt1 = newv(f"gt1_{l}"); self.ts(gt1.ap, m.ap[:, 2], 1.0 / ALPHA, None, ALU.mult, None, [m.buf], [gt1.buf])
            sc2 = newv(f"sc2_{l}"); self.ts(sc2.ap, m.ap[:, 4], 1.0, None, ALU.add, None, [m.buf], [sc2.buf])
            gt2 = newv(f"gt2_{l}"); self.ts(gt2.ap, m.ap[:, 5], 1.0 / ALPHA, None, ALU.mult, None, [m.buf], [gt2.buf])
            self.sc1.append(sc1); self.gt1.append(gt1); self.sc2.append(sc2); self.gt2.append(gt2)
        for l in range(2):
            m = self.m[l]
            G2 = newv(f"G2_{l}"); B2 = newv(f"B2_{l}")
            for r in range(2):
                self.tt(G2.ap[:, :, r], self.sc2[l].ap[:, :, r], self.vT.ap[:, :, 4 * l + 0], ALU.mult,
                        [self.sc2[l].buf, self.vT.buf], [G2.buf])
                self.tt(B2.ap[:, :, r], self.sc2[l].ap[:, :, r], self.vT.ap[:, :, 4 * l + 1], ALU.mult,
                        [self.sc2[l].buf, self.vT.buf], [B2.buf])
            self.tt(B2.ap, B2.ap, m.ap[:, 3], ALU.add, [B2.buf, m.buf], [B2.buf])
            self.G2.append(G2); self.B2.append(B2)
        G3 = newv("G3"); B3 = newv("B3")
        for r in range(2):
            self.tt(G3.ap[:, :, r], self.sc1[1].ap[:, :, r], self.vT.ap[:, :, 2], ALU.mult, [self.sc1[1].buf, self.vT.buf], [G3.buf])
            self.tt(B3.ap[:, :, r], self.sc1[1].ap[:, :, r], self.vT.ap[:, :, 3], ALU.mult, [self.sc1[1].buf, self.vT.buf], [B3.buf])
        self.tt(B3.ap, B3.ap, self.m[1].ap[:, 0], ALU.add, [B3.buf, self.m[1].buf], [B3.buf])
        self.G3, self.B3 = G3, B3
        P.barrier()

    def load_xT(self, src_ap, src_buf, ntok, xin, uT, sc, sh, r, hT=None, h_off=0, pbanks=(0, 1)):
        nt = ntok // 128
        self.load(xin.ap[:, 0:nt, :], xin.buf, src_ap.rearrange("(t p) d -> p t d", p=128), src_buf)
        for kc in range(KC):
            bap, bb = self.bank(pbanks[kc % len(pbanks)])
            for t in range(nt):
                self.P.op("pe", (lambda o, i: lambda e: e.transpose(o, i, self.ident.ap))(
                    bap[:, t * 128:(t + 1) * 128], xin.ap[:, t, kc * 128:(kc + 1) * 128]),
                    reads=[xin.buf, self.ident.buf], writes=[bb])
            if uT is not None:
                self.act(uT.ap[:, kc, 0:ntok], bap[:, 0:ntok], AF.Identity, [bb, sc.buf, sh.buf], [uT.buf],
                         bias=sh.ap[:, kc, r:r + 1], scale=sc.ap[:, kc, r:r + 1])
            if hT is not None:
                self.cp(hT.ap[:, kc, h_off:h_off + ntok], bap[:, 0:ntok], [bb], [hT.buf])

    def phase_k0(self):
        A, P = self.A, self.P
        KTO = [self.scratch(f"KTO{h}", [128, TOWN], BF16) for h in range(4)]
        VO = [self.scratch(f"VO{h}", [128, TOWN], BF16) for h in range(4)]
        self.scratch("KTC", [4, 128, LCTX], BF16)
        self.scratch("VC", [4, 128, 2, 128], BF16)
        KTC, VC = self.din["KTC"], self.din["VC"]
        m0 = A.mark()
        wk = A.alloc("wk", [128, KC, 512], BF16)
        wks = A.alloc("wks", [128, KC, 512], BF16)
        wv = A.alloc("wv", [128, KC, 512], BF16)
        wsrc = self.din["ev_w_in"].rearrange("(k p) n -> p k n", p=128)
        wsw = self.din["ev_w_in_sw"].rearrange("(k p) n -> p k n", p=128)
        for hh in range(2):
            ks = slice(hh * 4, hh * 4 + 4)
            self.load(wk.ap[:, ks, :], wk.buf, wsrc[:, ks, 512:1024], self.dbuf["ev_w_in"], q="pool")
            self.load(wks.ap[:, ks, :], wks.buf, wsw[:, ks, 512:1024], self.dbuf["ev_w_in_sw"], q="pool")
            self.load(wv.ap[:, ks, :], wv.buf, wsrc[:, ks, 1024:1536], self.dbuf["ev_w_in"], q="pool")
        xin_r = Ring(A, "xin", [128, 4, D], F32, 2)
        uT_r = Ring(A, "uT", [128, KC, 512], BF16, 2)
        cd_r = Ring(A, "cd", [128, 512], F32, 2)
        sd_r = Ring(A, "sd", [128, 512], F32, 2)
        t1_r = Ring(A, "t1", [128, 512], F32, 2)
        t2_r = Ring(A, "t2", [128, 512], F32, 2)
        kt_r = Ring(A, "kt", [128, 512], BF16, 4)
        vt_r = Ring(A, "vt", [128, 512], BF16, 4)
        for blk in range(5):
            ctx = blk == 4
            ntok = 256 if ctx else 512
            tok0 = blk * 512
            r = 1 if ctx else 0
            xin, uT = xin_r.next(), uT_r.next()
            src = self.din["ctxb"] if ctx else self.din["xown"][tok0:tok0 + 512, :]
            sbuf = self.dbuf["ctxb"] if ctx else self.dbuf["xown"]
            self.load_xT(src, sbuf, ntok, xin, uT, self.sc1[0], self.sh1[0], r, pbanks=(0, 1))
            if not ctx:
                cd, sd = cd_r.next(), sd_r.next()
                self.load(cd.ap, cd.buf, self.din["rope_cd_own"][:, tok0:tok0 + 512], self.dbuf["rope_cd_own"])
                self.load(sd.ap, sd.buf, self.din["rope_sd_own"][:, tok0:tok0 + 512], self.dbuf["rope_sd_own"])
            for h in range(4):
                pa, pab = self.bank(2 + (h % 2) * 2)
                pb, pbb = self.bank(3 + (h % 2) * 2)
                for kc in range(KC):
                    self.mm(pa[:, 0:ntok], wk.ap[:, kc, h * 128:(h + 1) * 128], uT.ap[:, kc, 0:ntok], kc == 0, kc == KC - 1,
                            [wk.buf, uT.buf], [pab])
                kt = kt_r.next()
                if not ctx:
                    for kc in range(KC):
                        self.mm(pb[:, 0:ntok], wks.ap[:, kc, h * 128:(h + 1) * 128], uT.ap[:, kc, 0:ntok], kc == 0, kc == KC - 1,
                                [wks.buf, uT.buf], [pbb])
                    t1, t2 = t1_r.next(), t2_r.next()
                    self.tt(t1.ap, pa, cd.ap, ALU.mult, [pab, cd.buf], [t1.buf])
                    self.tt(t2.ap, pb, sd.ap, ALU.mult, [pbb, sd.buf], [t2.buf])
                    self.tt(kt.ap, t1.ap, t2.ap, ALU.add, [t1.buf, t2.buf], [kt.buf])
                    self.store(KTO[h][:, tok0:tok0 + ntok], self.dbuf[f"KTO{h}"], kt.ap[:, 0:ntok], kt.buf)
                else:
                    self.cp(kt.ap[:, 0:ntok], pa[:, 0:ntok], [pab], [kt.buf])
                    self.store(KTC[h, :, 0:ntok], self.dbuf["KTC"], kt.ap[:, 0:ntok], kt.buf)
            for t in range(ntok // 128):
                pv, pvb = self.bank(6 + (t % 2))
                for kc in range(KC):
                    self.mm(pv, uT.ap[:, kc, t * 128:(t + 1) * 128], wv.ap[:, kc, :], kc == 0, kc == KC - 1,
                            [uT.buf, wv.buf], [pvb])
                vt = vt_r.next()
                self.cp(vt.ap, pv, [pvb], [vt.buf], eng="act")
                T = blk * 4 + t
                for h in range(4):
                    if ctx:
                        self.store(VC[h, :, t, :], self.dbuf["VC"], vt.ap[:, h * 128:(h + 1) * 128], vt.buf)
                    else:
                        self.store(VO[h][:, T * 128:(T + 1) * 128], self.dbuf[f"VO{h}"], vt.ap[:, h * 128:(h + 1) * 128], vt.buf)
        P.barrier()
        A.release(m0)
        for h in range(4):
            for (sn, dn) in ((f"KTO{h}", f"KTG{h}"), (f"VO{h}", f"VG{h}")):
                self.scratch(dn, [4 * 128, TOWN], BF16)
                src, dst = self.din[sn], self.din[dn]
                self.P.dma("pool", (lambda s_, d_: lambda e: e.collective_compute(
                    "AllGather", ALU.bypass, replica_groups=[[0, 1, 2, 3], [4, 5, 6, 7]], ins=[s_.opt()], outs=[d_.opt()]))(src, dst),
                    reads=[self.dbuf[sn]], writes=[self.dbuf[dn]], inc=1, dedicated=True, defer=True)

    def phase_kna(self):
        A, P = self.A, self.P
        self.KTn = A.alloc("KTn", [128, 4, NNA], BF16)
        self.Vn = A.alloc("Vn", [128, NNA // 128, 8, 128], BF16)
        self.KTnc = A.alloc("KTnc", [128, 4, LCTX], BF16)
        self.Vnc = A.alloc("Vnc", [128, 2, 8, 128], BF16)
        self.memset(V(self.Vn.ap[:, :, :, 64:128], self.Vn.buf), 1.0)
        self.memset(V(self.Vnc.ap[:, :, :, 64:128], self.Vnc.buf), 1.0)
        m0 = A.mark()
        wbk = A.alloc("wbk", [128, KC, 512], BF16)
        wbv = A.alloc("wbv", [128, KC, 512], BF16)
        wsrc = self.din["ev_w_in"].rearrange("(k p) n -> p k n", p=128)
        for hh in range(2):
            ks = slice(hh * 4, hh * 4 + 4)
            self.load(wbk.ap[:, ks, :], wbk.buf, wsrc[:, ks, 2048:2560], self.dbuf["ev_w_in"], q="pool")
            self.load(wbv.ap[:, ks, :], wbv.buf, wsrc[:, ks, 2560:3072], self.dbuf["ev_w_in"], q="pool")
        xin_r = Ring(A, "xin", [128, 4, D], F32, 1)
        uT_r = Ring(A, "uT", [128, KC, 512], BF16, 2)
        blocks = [(False, i * 512, min(512, NNA - i * 512)) for i in range((NNA + 511) // 512)] + [(True, 0, LCTX)]
        for (ctx, tok0, ntok) in blocks:
            r = 1 if ctx else 0
            xin, uT = xin_r.next(), uT_r.next()
            src = self.din["ctxb"] if ctx else self.din["xna"][tok0:tok0 + ntok, :]
            sbuf = self.dbuf["ctxb"] if ctx else self.dbuf["xna"]
            self.load_xT(src, sbuf, ntok, xin, uT, self.sc1[0], self.sh1[0], r, pbanks=(0, 1))
            KT = self.KTnc if ctx else self.KTn
            VV = self.Vnc if ctx else self.Vn
            for pr in range(4):
                pa, pab = self.bank(2 + pr % 2)
                for kc in range(KC):
                    self.mm(pa[:, 0:ntok], wbk.ap[:, kc, pr * 128:(pr + 1) * 128], uT.ap[:, kc, 0:ntok], kc == 0, kc == KC - 1,
                            [wbk.buf, uT.buf], [pab])
                self.cp(KT.ap[:, pr, tok0:tok0 + ntok], pa[:, 0:ntok], [pab], [KT.buf])
            for t in range(ntok // 128):
                pv, pvb = self.bank(4 + (t % 2))
                for kc in range(KC):
                    self.mm(pv, uT.ap[:, kc, t * 128:(t + 1) * 128], wbv.ap[:, kc, :], kc == 0, kc == KC - 1,
                            [uT.buf, wbv.buf], [pvb])
                T = tok0 // 128 + t
                self.cp(VV.ap[:, T, :, 0:64], pv.rearrange("p (h d) -> p h d", h=8), [pvb], [VV.buf], eng="act")
        P.barrier()
        A.release(m0)

    def phase_q(self, which):
        A, P = self.A, self.P
        if which == "na":
            self.mixT = A.alloc("mixT", [128, KC, NOWN], BF16)
            self.m_att = A.mark()
            self.QTn = A.alloc("QTn", [128, 8, NOWN], BF16)
            self.memset(self.QTn, 0.0)
        else:
            self.QTd = A.alloc("QTd", [128, 4, 2, NOWN], BF16)
            self.memset(self.QTd, 0.0)
        m0 = A.mark()
        wsrc = self.din["ev_w_in"].rearrange("(k p) n -> p k n", p=128)
        wsw = self.din["ev_w_in_sw"].rearrange("(k p) n -> p k n", p=128)
        if which == "na":
            wbq = A.alloc("wbq", [128, KC, 512], BF16)
        else:
            wq = A.alloc("wq", [128, KC, 512], BF16)
            wqs = A.alloc("wqs", [128, KC, 512], BF16)
        for hh in range(2):
            ks = slice(hh * 4, hh * 4 + 4)
            if which == "na":
                self.load(wbq.ap[:, ks, :], wbq.buf, wsrc[:, ks, 1536:2048], self.dbuf["ev_w_in"], q="pool")
            else:
                self.load(wq.ap[:, ks, :], wq.buf, wsrc[:, ks, 0:512], self.dbuf["ev_w_in"], q="pool")
                self.load(wqs.ap[:, ks, :], wqs.buf, wsw[:, ks, 0:512], self.dbuf["ev_w_in_sw"], q="pool")
        xin_r = Ring(A, "xin", [128, 4, D], F32, 2)
        uT_r = Ring(A, "uT", [128, KC, 512], BF16, 2)
        if which != "na":
            cd_r = Ring(A, "cd", [128, 512], F32, 2)
            sd_r = Ring(A, "sd", [128, 512], F32, 2)
            t1_r = Ring(A, "t1", [128, 512], F32, 2)
            t2_r = Ring(A, "t2", [128, 512], F32, 2)
        for blk in range(5):
            ctx = blk == 4
            ntok = 256 if ctx else 512
            tok0 = blk * 512
            r = 1 if ctx else 0
            xin, uT = xin_r.next(), uT_r.next()
            src = self.din["ctxb"] if ctx else self.din["xown"][tok0:tok0 + 512, :]
            sbuf = self.dbuf["ctxb"] if ctx else self.dbuf["xown"]
            self.load_xT(src, sbuf, ntok, xin, uT, self.sc1[0], self.sh1[0], r, pbanks=(0, 1))
            if which == "na":
                for pr in range(4):
                    pa, pab = self.bank(2 + pr % 2)
                    for kc in range(KC):
                        self.mm(pa[:, 0:ntok], wbq.ap[:, kc, pr * 128:(pr + 1) * 128], uT.ap[:, kc, 0:ntok], kc == 0, kc == KC - 1,
                                [wbq.buf, uT.buf], [pab])
                    self.cp(self.QTn.ap[0:64, 2 * pr, tok0:tok0 + ntok], pa[0:64, 0:ntok], [pab], [self.QTn.buf], eng="act")
                    self.cp(self.QTn.ap[64:128, 2 * pr + 1, tok0:tok0 + ntok], pa[64:128, 0:ntok], [pab], [self.QTn.buf], eng="dve")
                continue
            if not ctx:
                cd, sd = cd_r.next(), sd_r.next()
                self.load(cd.ap, cd.buf, self.din["rope_cd_own"][:, tok0:tok0 + 512], self.dbuf["rope_cd_own"])
                self.load(sd.ap, sd.buf, self.din["rope_sd_own"][:, tok0:tok0 + 512], self.dbuf["rope_sd_own"])
            for h in range(4):
                pa, pab = self.bank(2 + (h % 2) * 2)
                pb, pbb = self.bank(3 + (h % 2) * 2)
                for kc in range(KC):
                    self.mm(pa[:, 0:ntok], wq.ap[:, kc, h * 128:(h + 1) * 128], uT.ap[:, kc, 0:ntok], kc == 0, kc == KC - 1,
                            [wq.buf, uT.buf], [pab])
                if not ctx:
                    for kc in range(KC):
                        self.mm(pb[:, 0:ntok], wqs.ap[:, kc, h * 128:(h + 1) * 128], uT.ap[:, kc, 0:ntok], kc == 0, kc == KC - 1,
                                [wqs.buf, uT.buf], [pbb])
                    t1, t2 = t1_r.next(), t2_r.next()
                    self.tt(t1.ap, pa, cd.ap, ALU.mult, [pab, cd.buf], [t1.buf])
                    self.tt(t2.ap, pb, sd.ap, ALU.mult, [pbb, sd.buf], [t2.buf])
                    for m in range(2):
                        rows = slice(64 * m, 64 * m + 64)
                        self.tt(self.QTd.ap[rows, h, m, tok0:tok0 + ntok], t1.ap[rows, :], t2.ap[rows, :], ALU.add,
                                [t1.buf, t2.buf], [self.QTd.buf])
                else:
                    for m in range(2):
                        rows = slice(64 * m, 64 * m + 64)
                        self.cp(self.QTd.ap[rows, h, m, tok0:tok0 + ntok], pa[rows, 0:ntok], [pab], [self.QTd.buf])
        P.barrier()
        A.release(m0)

    def phase_na(self):
        A, P = self.A, self.P
        tb = A.alloc("tb", [128, 8 * 14 * 64], F32)
        self.load(tb.ap, tb.buf, self.din["na_tb"], self.dbuf["na_tb"])
        self.ts(tb.ap, tb.ap, 8.0, None, ALU.mult, None, [tb.buf], [tb.buf])
        tbv = tb.ap.rearrange("p (h x) -> p h x", h=8)
        qm = A.alloc("qm", [128, TOWN], BF16)
        km = A.alloc("km", [128, NNA], BF16)
        self.memset(qm, 0.0)
        self.memset(km, 0.0)
        self.load(qm.ap[0:64, :], qm.buf, self.din["na_qm"], self.dbuf["na_qm"])
        self.load(km.ap[0:64, :], km.buf, self.din["na_km"], self.dbuf["na_km"])
        sb_r = Ring(A, "nasb", [128, 896], F32, 2)
        e_r = Ring(A, "nae", [128, 1152], BF16, 3)
        rz_r = Ring(A, "narz", [64, 128], F32, 2)
        steps = []
        it = 0
        for qt in range(NOWN // 128):
            ctxq = qt >= 16
            qs = slice(qt * 128, (qt + 1) * 128)
            for h in range(8):
                sw, swb = self.pp[it % 2]
                sc, scb = self.bank(4 + it % 2)
                ob, obb = self.bank(6 + it % 2)
                it += 1

                def front(qt=qt, h=h, ctxq=ctxq, qs=qs, sw=sw, swb=swb, sc=sc, scb=scb):
                    pr = h // 2
                    e = e_r.next()
                    if not ctxq:
                        for j in range(7):
                            kt = qt + 6 - j
                            cols = slice(j * 128, (j + 1) * 128)
                            bb_ = swb[j // 4]
                            self.mm(sw[:, cols], self.KTn.ap[:, pr, kt * 128:(kt + 1) * 128], self.QTn.ap[:, h, qs], True, False,
                                    [self.KTn.buf, self.QTn.buf], [bb_])
                            self.mm(sw[:, cols], km.ap[:, kt * 128:(kt + 1) * 128], qm.ap[:, qs], False, True, [km.buf, qm.buf], [bb_])
                        sb = sb_r.next()
                        self.tt(sb.ap, sw[:, 0:896], tbv[:, h, :], ALU.add, swb + [tb.buf], [sb.buf])
                        self.act(e.ap[:, 0:896], sb.ap, AF.Exp, [sb.buf], [e.buf], scale=0.125)
                    for c in range(2):
                        self.mm(sc[:, c * 128:(c + 1) * 128], self.KTnc.ap[:, pr, c * 128:(c + 1) * 128], self.QTn.ap[:, h, qs], True, True,
                                [self.KTnc.buf, self.QTn.buf], [scb])
                    self.act(e.ap[:, 896:1152], sc[:, 0:256], AF.Exp, [scb], [e.buf], scale=0.125)
                    return e

                def back(e, qt=qt, h=h, ctxq=ctxq, qs=qs, ob=ob, obb=obb):
                    pr, half = h // 2, h % 2
                    nk = 0 if ctxq else 7
                    for j in range(nk):
                        kt = qt + 6 - j
                        self.mm(ob[:, 0:128], self.Vn.ap[:, kt, h, :], e.ap[:, j * 128:(j + 1) * 128], j == 0, False,
                                [self.Vn.buf, e.buf], [obb])
                    for c in range(2):
                        self.mm(ob[:, 0:128], self.Vnc.ap[:, c, h, :], e.ap[:, 896 + c * 128:896 + (c + 1) * 128], (nk == 0 and c == 0), c == 1,
                                [self.Vnc.buf, e.buf], [obb])
                    rz = rz_r.next()
                    self.recip(rz.ap, ob[64:128, 0:128], [obb], [rz.buf])
                    self.tt(self.mixT.ap[64 * half:64 * half + 64, 4 + pr, qs], ob[0:64, 0:128], rz.ap, ALU.mult,
                            [obb, rz.buf], [self.mixT.buf])
                steps.append((front, back))
        prev = None
        for (fr, bk) in steps:
            e = fr()
            if prev is not None:
                prev[0](prev[1])
            prev = (bk, e)
        prev[0](prev[1])
        P.barrier()
        A.release(self.m_att)

    def phase_diff(self):
        A, P = self.A, self.P
        lam_in = A.alloc("lam_in", [128, 256], F32)
        self.load(lam_in.ap, lam_in.buf, self.din["diff_lambda"].partition_broadcast(128), self.dbuf["diff_lambda"])
        lp = A.alloc("lam_p", [128, 128], F32)
        self.tt(lp.ap[:, 0:64], lam_in.ap[:, 0:64], lam_in.ap[:, 64:128], ALU.mult, [lam_in.buf], [lp.buf])
        self.tt(lp.ap[:, 64:128], lam_in.ap[:, 128:192], lam_in.ap[:, 192:256], ALU.mult, [lam_in.buf], [lp.buf])
        ls = A.alloc("lam_s", [128, 4], F32)
        self.P.op("dve", lambda e: e.reduce_sum(out=ls.ap[:, 0:2], in_=lp.ap.rearrange("p (a d) -> p a d", a=2), axis=AX.X),
                  reads=[lp.buf], writes=[ls.buf])
        self.act(ls.ap[:, 2:4], ls.ap[:, 0:2], AF.Exp, [ls.buf], [ls.buf])
        nlam = A.alloc("nlam", [128, 1], F32)
        self.stt(nlam.ap, ls.ap[:, 3:4], -0.2, ls.ap[:, 2:3], ALU.add, ALU.subtract, [ls.buf], [nlam.buf])
        gsub = A.alloc("gsub", [128, 1], F32)
        self.load(gsub.ap, gsub.buf, self.din["diff_subln_g"].rearrange("(p o) -> p o", o=1), self.dbuf["diff_subln_g"], slow=True)
        self.ts(gsub.ap, gsub.ap, 0.8, None, ALU.mult, None, [gsub.buf], [gsub.buf])
        kt_r = Ring(A, "KTh", [128, NKEY], BF16, 2)
        v_r = Ring(A, "Vh", [128, NKT, 128], BF16, 2)
        e_r = Ring(A, "dE", [128, 2, 512], BF16, 4)
        f_r = Ring(A, "dF", [128, 512], F32, 6)
        za_r = Ring(A, "dZ", [128, 2, 512], F32, 2)
        qblocks = [(i * 512, 512, list(range(NKT))) for i in range(4)] + [(TOWN, LCTX, [NKT - 2, NKT - 1])]
        heads = []
        for h in range(4):
            KT, VH = kt_r.next(), v_r.next()
            heads.append((KT, VH))

        def load_head(h):
            KT, VH = heads[h]
            for rnk in range(4):
                self.load(KT.ap[:, rnk * TOWN:(rnk + 1) * TOWN], KT.buf, self.din[f"KTG{h}"][rnk * 128:(rnk + 1) * 128, :], self.dbuf[f"KTG{h}"])
                self.load(VH.ap[:, rnk * 16:(rnk + 1) * 16, :], VH.buf,
                          self.din[f"VG{h}"][rnk * 128:(rnk + 1) * 128, :].rearrange("p (t d) -> p t d", d=128), self.dbuf[f"VG{h}"])
            self.load(KT.ap[:, SEQ:NKEY], KT.buf, self.din["KTC"][h], self.dbuf["KTC"])
            self.load(VH.ap[:, 64:66, :], VH.buf, self.din["VC"][h], self.dbuf["VC"])

        def post(h, q0, nq, O, za):
            pz, pzb = self.bank(7)
            r0, r1, t0_, t1b, dd, sq = [f_r.next() for _ in range(6)]
            n = slice(0, nq)
            self.mm(pz[:, n], self.ones_f.ap, za.ap[:, 0, n], True, True, [self.ones_f.buf, za.buf], [pzb])
            self.recip(r0.ap[:, n], pz[:, n], [pzb], [r0.buf])
            self.mm(pz[:, n], self.ones_f.ap, za.ap[:, 1, n], True, True, [self.ones_f.buf, za.buf], [pzb])
            self.recip(r1.ap[:, n], pz[:, n], [pzb], [r1.buf])
            self.tt(t0_.ap[:, n], O[0][0][:, n], r0.ap[:, n], ALU.mult, [O[0][1], r0.buf], [t0_.buf])
            self.tt(t1b.ap[:, n], O[1][0][:, n], r1.ap[:, n], ALU.mult, [O[1][1], r1.buf], [t1b.buf])
            self.stt(dd.ap[:, n], t1b.ap[:, n], nlam.ap[:, 0:1], t0_.ap[:, n], ALU.mult, ALU.add,
                     [t1b.buf, t0_.buf, nlam.buf], [dd.buf])
            self.act(sq.ap[:, n], dd.ap[:, n], AF.Square, [dd.buf], [sq.buf])
            self.mm(pz[:, n], self.ones_f.ap, sq.ap[:, n], True, True, [self.ones_f.buf, sq.buf], [pzb])
            self.act(r0.ap[:, n], pz[:, n], AF.Sqrt, [pzb], [r0.buf], bias=self.eps_rms.ap[:, 0:1], scale=1.0 / 128)
            self.recip(r1.ap[:, n], r0.ap[:, n], [r0.buf], [r1.buf])
            self.stt(self.mixT.ap[:, h, q0:q0 + nq], dd.ap[:, n], gsub.ap[:, 0:1], r1.ap[:, n], ALU.mult, ALU.mult,
                     [dd.buf, gsub.buf, r1.buf], [self.mixT.buf])

        steps = []
        sidx = [0]
        oset = [0]
        load_head(0)
        for h in range(4):
            KT, VH = heads[h]
            for qi, (q0, nq, kts) in enumerate(qblocks):
                ob = 3 + 2 * (oset[0] % 2)
                oset[0] += 1
                O = [self.bank(ob), self.bank(ob + 1)]
                za = za_r.next()
                for ki, kt in enumerate(kts):
                    def front(h=h, KT=KT, q0=q0, nq=nq, kt=kt, ki=ki, za=za, qi=qi):
                        if qi == 0 and ki == 1 and h + 1 < 4:
                            load_head(h + 1)
                        e = e_r.next()
                        for m in range(2):
                            sap, sbuf_ = self.bank(sidx[0] % 3)
                            sidx[0] += 1
                            self.mm(sap[:, 0:nq], KT.ap[:, kt * 128:(kt + 1) * 128], self.QTd.ap[:, h, m, q0:q0 + nq], True, True,
                                    [KT.buf, self.QTd.buf], [sbuf_])
                            self.act(e.ap[:, m, 0:nq], sap[:, 0:nq], AF.Exp, [sbuf_], [e.buf], scale=0.125)
                        if ki == 0:
                            self.cp(za.ap[:, :, 0:nq], e.ap[:, :, 0:nq], [e.buf], [za.buf])
                        else:
                            self.tt(za.ap[:, :, 0:nq], za.ap[:, :, 0:nq], e.ap[:, :, 0:nq], ALU.add, [za.buf, e.buf], [za.buf])
                        return e

                    def back(es, h=h, VH=VH, q0=q0, nq=nq, kt=kt, ki=ki, nk=len(kts), O=O, za=za):
                        for m in range(2):
                            self.mm(O[m][0][:, 0:nq], VH.ap[:, kt, :], es.ap[:, m, 0:nq], ki == 0, ki == nk - 1,
                                    [VH.buf, es.buf], [O[m][1]])
                        if ki == nk - 1:
                            post(h, q0, nq, O, za)
                    steps.append((front, back))
        prev = None
        for (fr, bk) in steps:
            es = fr()
            if prev is not None:
                prev[0](prev[1])
            prev = (bk, es)
        prev[0](prev[1])
        P.barrier()
        A.release(self.m_att)

    def ln_block(self, hT, c0, ntok, gi, bi, G, B, r, uT, u0, st):
        sq, zb, mean, msq, var, rstd, nmr = st
        zs = hT.ap[:, :, c0:c0 + ntok]
        s1, s1b = self.bank(6)
        s2, s2b = self.bank(7)
        self.act(sq.ap[:, :, 0:ntok], zs, AF.Square, [hT.buf], [sq.buf])
        self.cp(zb.ap[:, :, 0:ntok], zs, [hT.buf], [zb.buf], eng="pool")
        for kc in range(KC):
            self.mm(s1[:, 0:ntok], self.ones_b.ap, zb.ap[:, kc, 0:ntok], kc == 0, kc == KC - 1, [self.ones_b.buf, zb.buf], [s1b])
        for kc in range(KC):
            self.mm(s2[:, 0:ntok], self.ones_b.ap, sq.ap[:, kc, 0:ntok], kc == 0, kc == KC - 1, [self.ones_b.buf, sq.buf], [s2b])
        n = slice(0, ntok)
        self.ts(mean.ap[:, n], s1[:, n], 1.0 / D, None, ALU.mult, None, [s1b], [mean.buf])
        self.tt(msq.ap[:, n], mean.ap[:, n], mean.ap[:, n], ALU.mult, [mean.buf], [msq.buf])
        self.stt(var.ap[:, n], s2[:, n], 1.0 / D, msq.ap[:, n], ALU.mult, ALU.subtract, [s2b, msq.buf], [var.buf])
        self.act(msq.ap[:, n], var.ap[:, n], AF.Sqrt, [var.buf], [msq.buf], bias=self.eps_ln.ap[:, 0:1], scale=1.0)
        self.recip(rstd.ap[:, n], msq.ap[:, n], [msq.buf], [rstd.buf])
        self.stt(nmr.ap[:, n], mean.ap[:, n], -1.0, rstd.ap[:, n], ALU.mult, ALU.mult, [mean.buf, rstd.buf], [nmr.buf])
        for kc in range(KC):
            z = hT.ap[:, kc, c0:c0 + ntok]
            self.tt(z, z, rstd.ap[:, n], ALU.mult, [hT.buf, rstd.buf], [hT.buf])
            self.tt(z, z, nmr.ap[:, n], ALU.add, [hT.buf, nmr.buf], [hT.buf])
            if uT is not None:
                self.act(uT.ap[:, kc, u0:u0 + ntok], z, AF.Identity, [hT.buf, G.buf, B.buf], [uT.buf],
                         bias=B.ap[:, kc, r:r + 1], scale=G.ap[:, kc, r:r + 1])
            self.ts(z, z, self.vT.ap[:, kc, gi:gi + 1], self.vT.ap[:, kc, bi:bi + 1], ALU.mult, ALU.add,
                    [hT.buf, self.vT.buf], [hT.buf])

    def alloc_ln_state(self):
        A = self.A
        sq = A.alloc("ln_sq", [128, KC, 512], BF16)
        zb = A.alloc("ln_zb", [128, KC, 512], BF16)
        rest = [A.alloc(f"ln_{n}", [128, 512], F32) for n in ("mean", "msq", "var", "rstd", "nmr")]
        return [sq, zb] + rest

    def phase_o(self, l, w_name, first, nblk=5, mix_dram=None):
        A, P = self.A, self.P
        if first:
            self.hT = A.alloc("hT", [128, KC, NOWN], F32)
        m0 = A.mark()
        mx_r = Ring(A, "mxblk", [128, KC, 512], BF16, 2) if mix_dram is not None else None
        wo = A.alloc("wo", [128, KC, D], BF16)
        wsrc = self.din[w_name].rearrange("(k p) n -> p k n", p=128)
        for hh in range(4):
            ks = slice(hh * 2, hh * 2 + 2)
            self.load(wo.ap[:, ks, :], wo.buf, wsrc[:, ks, :], self.dbuf[w_name], q="pool")
        st = self.alloc_ln_state()
        xin = A.alloc("xin_o", [128, 4, D], F32) if first else None
        pend_ln = []
        for blk in range(nblk):
            ctx = blk == 4
            ntok = 256 if ctx else 512
            tok0 = blk * 512
            r = 1 if ctx else 0
            if first:
                src = self.din["ctxb"] if ctx else self.din["xown"][tok0:tok0 + 512, :]
                sbuf = self.dbuf["ctxb"] if ctx else self.dbuf["xown"]
                self.load_xT(src, sbuf, ntok, xin, None, None, None, r, hT=self.hT, h_off=tok0, pbanks=(0, 1))
            if mix_dram is not None:
                mx = mx_r.next()
                for hh in range(2):
                    self.load(mx.ap[:, hh * 4:hh * 4 + 4, :], mx.buf,
                              self.din[mix_dram][hh * 4:hh * 4 + 4, :, tok0:tok0 + 512].rearrange("k p n -> p k n"), self.dbuf[mix_dram])
                mxa = lambda kc: mx.ap[:, kc, 0:ntok]
                mxb = mx.buf
            else:
                mxa = lambda kc: self.mixT.ap[:, kc, tok0:tok0 + ntok]
                mxb = self.mixT.buf
            for oc in range(KC):
                yb, ybb = self.bank(2 + oc % 4)
                for kc in range(KC):
                    self.mm(yb[:, 0:ntok], wo.ap[:, kc, oc * 128:(oc + 1) * 128], mxa(kc),
                            kc == 0, kc == KC - 1, [wo.buf, mxb], [ybb])
                z = self.hT.ap[:, oc, tok0:tok0 + ntok]
                self.stt(z, yb[:, 0:ntok], self.gt1[l].ap[:, oc, r:r + 1], z, ALU.mult, ALU.add,
                         [ybb, self.gt1[l].buf, self.hT.buf], [self.hT.buf])
            pend_ln.append((tok0, ntok, r))
            if len(pend_ln) > 1:
                t0_, n_, r_ = pend_ln.pop(0)
                self.ln_block(self.hT, t0_, n_, 4 * l + 0, 4 * l + 1, self.G2[l], self.B2[l], r_, self.mixT, t0_, st)
        for (t0_, n_, r_) in pend_ln:
            self.ln_block(self.hT, t0_, n_, 4 * l + 0, 4 * l + 1, self.G2[l], self.B2[l], r_, self.mixT, t0_, st)
        P.barrier()
        A.release(m0)

    def phase_f(self, l, Gn, Bn, ntb=5):
        A, P = self.A, self.P
        m0 = A.mark()
        st = self.alloc_ln_state()
        hf = A.alloc("hffn", [128, HC, 512], BF16)
        wg_r = Ring(A, "wg", [128, KC, 128], BF16, 3)
        wa_r = Ring(A, "wa", [128, KC, 128], BF16, 3)
        w2_r = Ring(A, "w2", [128, HC, 128], BF16, 2)
        sg_r = Ring(A, "sg", [128, 512], F32, 2)
        w1src = self.din["ffn_w_in"][l].rearrange("(k p) n -> p k n", p=128)
        w2src = self.din["ffn_w_out"][l].rearrange("(c p) n -> p c n", p=128)
        pidx = 0
        pend = None
        for blk in range(ntb):
            ctx = blk == 4
            ntok = 256 if ctx else 512
            tok0 = blk * 512
            r = 1 if ctx else 0
            for hc in range(HC):
                wg, wa = wg_r.next(), wa_r.next()
                self.load(wg.ap, wg.buf, w1src[:, :, hc * 128:(hc + 1) * 128], self.dbuf["ffn_w_in"], q="pool")
                self.load(wa.ap, wa.buf, w1src[:, :, FFN_H + hc * 128:FFN_H + (hc + 1) * 128], self.dbuf["ffn_w_in"], q="pool")
                gb, gbb = self.bank(pidx % 4)
                ab, abb = self.bank((pidx + 1) % 4)
                pidx += 2
                for kc in range(KC):
                    self.mm(gb[:, 0:ntok], wg.ap[:, kc, :], self.mixT.ap[:, kc, tok0:tok0 + ntok], kc == 0, kc == KC - 1,
                            [wg.buf, self.mixT.buf], [gbb])
                for kc in range(KC):
                    self.mm(ab[:, 0:ntok], wa.ap[:, kc, :], self.mixT.ap[:, kc, tok0:tok0 + ntok], kc == 0, kc == KC - 1,
                            [wa.buf, self.mixT.buf], [abb])
                sg = sg_r.next()
                self.act(sg.ap[:, 0:ntok], gb[:, 0:ntok], AF.Silu, [gbb], [sg.buf])
                self.tt(hf.ap[:, hc, 0:ntok], ab[:, 0:ntok], sg.ap[:, 0:ntok], ALU.mult, [abb, sg.buf], [hf.buf])
            if pend is not None:
                self.ln_block(self.hT, pend[0], pend[1], 4 * l + 2, 4 * l + 3, Gn, Bn, pend[2], self.mixT if Gn is not None else None, pend[0], st)
                pend = None
            for oc in range(KC):
                w2 = w2_r.next()
                self.load(w2.ap, w2.buf, w2src[:, :, oc * 128:(oc + 1) * 128], self.dbuf["ffn_w_out"], q="pool")
                yb, ybb = self.bank(4 + oc % 2)
                for hc in range(HC):
                    self.mm(yb[:, 0:ntok], w2.ap[:, hc, :], hf.ap[:, hc, 0:ntok], hc == 0, hc == HC - 1, [w2.buf, hf.buf], [ybb])
                z = self.hT.ap[:, oc, tok0:tok0 + ntok]
                self.stt(z, yb[:, 0:ntok], self.gt2[l].ap[:, oc, r:r + 1], z, ALU.mult, ALU.add,
                         [ybb, self.gt2[l].buf, self.hT.buf], [self.hT.buf])
            pend = (tok0, ntok, r)
        if pend is not None:
            self.ln_block(self.hT, pend[0], pend[1], 4 * l + 2, 4 * l + 3, Gn, Bn, pend[2], self.mixT if Gn is not None else None, pend[0], st)
        P.barrier()
        A.release(m0)

    def phase_out(self, dst_name, ntok_total):
        A, P = self.A, self.P
        m0 = A.mark()
        o_r = Ring(A, "orow", [128, D], F32, 3)
        evs = []
        for t in range(ntok_total // 128):
            o = o_r.next()
            for half in range(2):
                pb, pbb = self.bank((2 * t + half) % 4)
                for j in range(4):
                    kc = half * 4 + j
                    self.P.op("pe", (lambda oo, ii: lambda e: e.transpose(oo, ii, self.ident.ap))(
                        pb[:, j * 128:(j + 1) * 128], self.hT.ap[:, kc, t * 128:(t + 1) * 128]),
                        reads=[self.hT.buf, self.ident.buf], writes=[pbb])
                self.cp(o.ap[:, half * 512:(half + 1) * 512], pb, [pbb], [o.buf], eng=("act" if half else "dve"))
            evs.append(self.store(self.din[dst_name][t * 128:(t + 1) * 128, :], self.dbuf[dst_name], o.ap, o.buf))
        A.release(m0)
        return evs

    def phase_p1(self, lat_only=False, mid=None):
        A, P = self.A, self.P
        LATP = [self.scratch("LATA", [128, TOWN], BF16), self.scratch("LATB", [128, TOWN], BF16), self.scratch("LATK", [32, TOWN], BF16)]
        LATN = ["LATA", "LATB", "LATK"]
        LATC = self.scratch("LATC", [288, LCTX], BF16)
        if not lat_only:
            MIX1 = self.scratch("MIX1", [KC, 128, TOWN], BF16)
            self.cqn = A.alloc("cqn", [128, 3, TOWN], BF16)
        wsrc = self.din["od_w_in"].rearrange("(k p) n -> p k n", p=128)

        def proj_T(wt, c0, ncol, bank_i, tok0, ntok):
            bap, bbuf = self.bank(bank_i)
            for kc in range(KC):
                self.mm(bap[0:ncol, 0:ntok], wt.ap[:, kc, c0:c0 + ncol], self.mixT.ap[:, kc, tok0:tok0 + ntok], kc == 0, kc == KC - 1,
                        [wt.buf, self.mixT.buf], [bbuf])
            return bap, bbuf

        m0 = A.mark()
        w1 = A.alloc("w1a", [128, KC, 672], BF16)
        for hh in range(2):
            ks = slice(hh * 4, hh * 4 + 4)
            self.load(w1.ap[:, ks, :], w1.buf, wsrc[:, ks, 0:672], self.dbuf["od_w_in"], q="pool")
        wkrs = A.alloc("wkrs", [128, KC, 32], BF16)
        self.load(wkrs.ap, wkrs.buf, self.din["od_w_in_krsw"].rearrange("(k p) n -> p k n", p=128), self.dbuf["od_w_in_krsw"], q="pool")
        f_r = Ring(A, "p1f", [128, 3, 512], F32, 2)
        s_r = Ring(A, "p1s", [128, 3, 512], F32, 2)
        r_r = Ring(A, "p1r", [128, 512], F32, 4)
        lat_r = Ring(A, "latst", [128, 2, 512], BF16, 2)
        kr_r = Ring(A, "krst", [32, 512], BF16, 2)
        tq_r = Ring(A, "p1tab", [32, 2, 512], F32, 2)
        k1_r = Ring(A, "p1k", [32, 2, 512], F32, 1)

        def rms_norm_T(ps_list, nch, ntok, nfeat, outs):
            f, s_ = f_r.next(), s_r.next()
            for c in range(nch):
                self.cp(f.ap[:, c, 0:ntok], ps_list[c][0][:, 0:ntok], [ps_list[c][1]], [f.buf], eng="dve")
                self.act(s_.ap[:, c, 0:ntok], ps_list[c][0][:, 0:ntok], AF.Square, [ps_list[c][1]], [s_.buf])
            sb_, sbb_ = self.bank(7)
            for c in range(nch):
                self.mm(sb_[:, 0:ntok], self.ones_f.ap, s_.ap[:, c, 0:ntok], c == 0, c == nch - 1, [self.ones_f.buf, s_.buf], [sbb_])
            r0, r1 = r_r.next(), r_r.next()
            self.act(r0.ap[:, 0:ntok], sb_[:, 0:ntok], AF.Sqrt, [sbb_], [r0.buf], bias=self.eps_rms.ap[:, 0:1], scale=1.0 / nfeat)
            self.recip(r1.ap[:, 0:ntok], r0.ap[:, 0:ntok], [r0.buf], [r1.buf])
            for c in range(nch):
                oap, obuf = outs[c]
                self.tt(oap, f.ap[:, c, 0:ntok], r1.ap[:, 0:ntok], ALU.mult, [f.buf, r1.buf], [obuf])

        for blk in range(5):
            ctx = blk == 4
            ntok = 256 if ctx else 512
            tok0 = blk * 512
            ps = [proj_T(w1, 384 + c * 128, 128, c, tok0, ntok) for c in range(2)]
            lat = lat_r.next()
            rms_norm_T(ps, 2, ntok, 256, [(lat.ap[:, c, 0:ntok], lat.buf) for c in range(2)])
            if ctx:
                self.store(LATC[0:256, 0:ntok].rearrange("(c p) n -> p c n", p=128), self.dbuf["LATC"], lat.ap[:, :, 0:ntok], lat.buf)
            else:
                for c in range(2):
                    self.store(LATP[c][:, tok0:tok0 + ntok], self.dbuf[LATN[c]], lat.ap[:, c, 0:ntok], lat.buf)
            pa, pab = proj_T(w1, 640, 32, 2, tok0, ntok)
            krs = kr_r.next()
            if not ctx:
                pb, pbb = proj_T(wkrs, 0, 32, 3, tok0, ntok)
                tq = tq_r.next()
                self.load(tq.ap[:, 0, :], tq.buf, self.din["ropeq_c_own"][64:96, tok0:tok0 + 512], self.dbuf["ropeq_c_own"])
                self.load(tq.ap[:, 1, :], tq.buf, self.din["ropeq_s_own"][64:96, tok0:tok0 + 512], self.dbuf["ropeq_s_own"])
                k1 = k1_r.next()
                self.tt(k1.ap[:, 0, :], pa[0:32, :], tq.ap[:, 0, :], ALU.mult, [pab, tq.buf], [k1.buf])
                self.tt(k1.ap[:, 1, :], pb[0:32, :], tq.ap[:, 1, :], ALU.mult, [pbb, tq.buf], [k1.buf])
                self.tt(krs.ap, k1.ap[:, 0, :], k1.ap[:, 1, :], ALU.add, [k1.buf], [krs.buf])
            else:
                self.cp(krs.ap[:, 0:ntok], pa[0:32, 0:ntok], [pab], [krs.buf])
            if ctx:
                self.store(LATC[256:288, 0:ntok], self.dbuf["LATC"], krs.ap[:, 0:ntok], krs.buf)
            else:
                self.store(LATP[2][:, tok0:tok0 + ntok], self.dbuf["LATK"], krs.ap[:, 0:ntok], krs.buf)
            if ctx or lat_only:
                continue
            ps = [proj_T(w1, c * 128, 128, 4 + c, tok0, ntok) for c in range(3)]
            rms_norm_T(ps, 3, ntok, 384, [(self.cqn.ap[:, c, tok0:tok0 + ntok], self.cqn.buf) for c in range(3)])
        P.barrier()
        A.release(m0)
        if lat_only:
            return
        if mid is not None:
            mid()
        m0 = A.mark()
        w1 = A.alloc("w1b", [128, KC, 1024], BF16)
        for hh in range(4):
            ks = slice(hh * 2, hh * 2 + 2)
            self.load(w1.ap[:, ks, :], w1.buf, wsrc[:, ks, 672:1696], self.dbuf["od_w_in"], q="pool")
        gbc = A.alloc("gln_g", [128, 512], F32)
        bbc = A.alloc("gln_b", [128, 512], F32)
        gbb = A.alloc("gm_b", [128, 512], F32)
        self.load(gbc.ap, gbc.buf, self.din["gmlp_ln_g"].partition_broadcast(128), self.dbuf["gmlp_ln_g"])
        self.load(bbc.ap, bbc.buf, self.din["gmlp_ln_b"].partition_broadcast(128), self.dbuf["gmlp_ln_b"])
        self.load(gbb.ap, gbb.buf, self.din["gmlp_b"].partition_broadcast(128), self.dbuf["gmlp_b"])
        wsT = A.alloc("wsT", [128, 4, 128], BF16)
        wss = A.alloc("wss", [128, 4, 128], F32)
        self.load(wss.ap, wss.buf, self.din["gmlp_ws"].rearrange("g i j -> i g j"), self.dbuf["gmlp_ws"])
        tb_, tbb_ = self.bank(0)
        for g in range(4):
            self.P.op("pe", (lambda o, i: lambda e: e.transpose(o, i, self.ident.ap))(tb_[:, g * 128:(g + 1) * 128], wss.ap[:, g, :]),
                      reads=[wss.buf, self.ident.buf], writes=[tbb_])
        self.cp(wsT.ap.rearrange("p g i -> p (g i)"), tb_, [tbb_], [wsT.buf])
        gu_r = Ring(A, "gu", [128, 4, 512], BF16, 2)
        gvf_r = Ring(A, "gvf", [128, 512], F32, 2)
        vg_r = Ring(A, "vg", [128, 512], BF16, 2)
        gm_r = Ring(A, "gm", [128, 512], F32, 2)
        go_r = Ring(A, "go", [128, 4, 128], BF16, 3)
        st_r = Ring(A, "bnst", [128, 16], F32, 2)
        eps_g = A.alloc("eps_g", [128, 1], F32)
        self.memset(eps_g, LN_EPS)
        for blk in range(4):
            ntok = 512
            tok0 = blk * 512
            gu = gu_r.next()
            for g in range(4):
                bap, bbuf = proj_T(w1, g * 128, 128, g % 4, tok0, ntok)
                self.act(gu.ap[:, g, :], bap, AF.Gelu, [bbuf], [gu.buf])
            for t in range(4):
                pv, pvb = self.bank(4 + t % 2)
                for kc in range(KC):
                    self.mm(pv, self.mixT.ap[:, kc, tok0 + t * 128:tok0 + (t + 1) * 128], w1.ap[:, kc, 512:1024], kc == 0, kc == KC - 1,
                            [self.mixT.buf, w1.buf], [pvb])
                gvf = gvf_r.next()
                self.act(gvf.ap, pv, AF.Gelu, [pvb], [gvf.buf])
                stt_ = st_r.next()
                self.P.op("dve", (lambda o, i: lambda e: e.bn_stats(out=o, in_=i))(stt_.ap[:, 0:6], gvf.ap), reads=[gvf.buf], writes=[stt_.buf])
                self.P.op("dve", (lambda o, i: lambda e: e.bn_aggr(out=o, in_=i))(stt_.ap[:, 6:8], stt_.ap[:, 0:6]), reads=[stt_.buf], writes=[stt_.buf])
                self.act(stt_.ap[:, 8:9], stt_.ap[:, 7:8], AF.Sqrt, [stt_.buf], [stt_.buf], bias=eps_g.ap[:, 0:1], scale=1.0)
                self.recip(stt_.ap[:, 9:10], stt_.ap[:, 8:9], [stt_.buf], [stt_.buf])
                self.stt(stt_.ap[:, 10:11], stt_.ap[:, 6:7], -1.0, stt_.ap[:, 9:10], ALU.mult, ALU.mult, [stt_.buf], [stt_.buf])
                self.ts(gvf.ap, gvf.ap, stt_.ap[:, 9:10], stt_.ap[:, 10:11], ALU.mult, ALU.add, [gvf.buf, stt_.buf], [gvf.buf])
                self.tt(gvf.ap, gvf.ap, gbc.ap, ALU.mult, [gvf.buf, gbc.buf], [gvf.buf])
                vg = vg_r.next()
                self.tt(vg.ap, gvf.ap, bbc.ap, ALU.add, [gvf.buf, bbc.buf], [vg.buf])
                pm, pmb = self.bank(6)
                for g in range(4):
                    self.mm(pm[:, g * 128:(g + 1) * 128], vg.ap[:, g * 128:(g + 1) * 128], wsT.ap[:, g, :], True, True, [vg.buf, wsT.buf], [pmb])
                gm = gm_r.next()
                self.tt(gm.ap, pm, gbb.ap, ALU.add, [pmb, gbb.buf], [gm.buf])
                go = go_r.next()
                self.tt(go.ap, gm.ap.rearrange("p (g i) -> p g i", g=4), gu.ap[:, :, t * 128:(t + 1) * 128], ALU.mult, [gm.buf, gu.buf], [go.buf])
                c0 = tok0 + t * 128
                self.store(MIX1[4:8, :, c0:c0 + 128].rearrange("g p i -> p g i"), self.dbuf["MIX1"], go.ap, go.buf)
        P.barrier()
        A.release(m0)

    def exchange(self, mode):
        if mode == "B":
            self.inp("latg", [4 * 288, TOWN], BF16)
            lg, lb = self.din["latg"], self.dbuf["latg"]
            self.lat_piece = lambda r, p: (lg[r * 288 + (0, 128, 256)[p]:r * 288 + (128, 256, 288)[p], :], lb)
            return
        names = [("LATA", "LGA", 128), ("LATB", "LGB", 128), ("LATK", "LGK", 32)]
        for (sn, dn, rows) in names:
            self.scratch(dn, [4 * rows, TOWN], BF16)
            src, dst = self.din[sn], self.din[dn]
            self.P.dma("pool", (lambda s_, d_: lambda e: e.collective_compute(
                "AllGather", ALU.bypass, replica_groups=[[0, 1, 2, 3], [4, 5, 6, 7]], ins=[s_.opt()], outs=[d_.opt()]))(src, dst),
                reads=[self.dbuf[sn]], writes=[self.dbuf[dn]], inc=1, dedicated=True)
        self.lat_piece = lambda r, p: (self.din[names[p][1]][r * names[p][2]:(r + 1) * names[p][2], :], self.dbuf[names[p][1]])

    def phase_k1(self):
        A, P = self.A, self.P
        KN = self.scratch("KN1", [8 * 64, NKEY], BF16)
        V1 = self.scratch("V1", [8, 128, NKT, 65], BF16)
        LC, lcb = self.din["LATC"], self.dbuf["LATC"]
        m0 = A.mark()
        ck = A.alloc("ckall", [128, 2, NKEY], BF16)
        ckb = [Buf(f"ck{r}") for r in range(5)]
        for rnk in range(4):
            for c in range(2):
                pap, pbuf = self.lat_piece(rnk, c)
                self.load(ck.ap[:, c, rnk * TOWN:(rnk + 1) * TOWN], ckb[rnk], pap, pbuf, q=("sp" if c == 0 else "act"))
        for c in range(2):
            self.load(ck.ap[:, c, SEQ:NKEY], ckb[4], LC[c * 128:(c + 1) * 128, :], lcb)
        wst = A.alloc("wukv_f", [128, 2, 1024], F32)
        self.load(wst.ap, wst.buf, self.din["mla_w_ukv"].rearrange("(c p) n -> p c n", p=128), self.dbuf["mla_w_ukv"])
        wkv = A.alloc("wukv", [128, 2, 2, 512], BF16)
        for c in range(2):
            src4 = wst.ap[:, c, :].rearrange("p (h x) -> p h x", h=8)
            for kv in range(2):
                self.ts(wkv.ap[:, c, kv, :].rearrange("p (h d) -> p h d", h=8), src4[:, :, kv * 64:(kv + 1) * 64],
                        self.vT.ap[:, c, 21:22], None, ALU.mult, None, [wst.buf, self.vT.buf], [wkv.buf])
        kt_r = Ring(A, "k1kt", [128, 512], BF16, 6)
        vt_r = Ring(A, "k1vt", [128, 8, 8, 65], BF16, 2)
        for v in vt_r.items:
            self.memset(v, 1.0)
        bi = 0
        vt = None
        si = 0
        for T in range(NKT):
            if T % 8 == 0:
                vt = vt_r.next()
            cb = ckb[min(T // 16, 4)]
            pv, pvb = self.bank(4 + T % 4)
            for c in range(2):
                self.mm(pv, ck.ap[:, c, T * 128:(T + 1) * 128], wkv.ap[:, c, 1, :],
                        c == 0, c == 1, [cb, wkv.buf], [pvb])
            self.cp(vt.ap[:, :, T % 8, 0:64], pv.rearrange("p (h d) -> p h d", h=8), [pvb], [vt.buf], eng=("act" if T % 2 else "dve"))
            if T % 8 == 7 or T == NKT - 1:
                tg = (T // 8) * 8
                nt = T - tg + 1
                for h in range(8):
                    self.store(V1[h, :, tg:tg + nt, :], self.dbuf["V1"], vt.ap[:, h, 0:nt, :], vt.buf, q=("sp" if si % 2 == 0 else "act"))
                    si += 1
        for hp in range(4):
            for blk in range(17):
                ntok = 256 if blk == 16 else 512
                tok0 = blk * 512
                cb = ckb[min(blk // 4, 4)]
                pa, pab = self.bank(bi % 4)
                bi += 1
                for c in range(2):
                    self.mm(pa[:, 0:ntok], wkv.ap[:, c, 0, hp * 128:(hp + 1) * 128], ck.ap[:, c, tok0:tok0 + ntok], c == 0, c == 1,
                            [wkv.buf, cb], [pab])
                kt = kt_r.next()
                self.cp(kt.ap[:, 0:ntok], pa[:, 0:ntok], [pab], [kt.buf], eng=("act" if blk % 2 else "dve"))
                self.store(KN[hp * 128:(hp + 1) * 128, tok0:tok0 + ntok], self.dbuf["KN1"], kt.ap[:, 0:ntok], kt.buf, q=("sp" if si % 2 == 0 else "act"))
                si += 1
        P.barrier()
        A.release(m0)

    def phase_mla(self):
        A, P = self.A, self.P
        KN, V1, MIX1 = self.din["KN1"], self.din["V1"], self.din["MIX1"]
        m0 = A.mark()
        wq = [A.alloc(nm + "_b", [128, 3, 768], BF16) for nm in ("mla_w_uq", "mla_w_uq_sw")]
        m1 = A.mark()
        for i, nm in enumerate(("mla_w_uq", "mla_w_uq_sw")):
            wst = A.alloc(nm + "_f", [128, 3, 768], F32)
            self.load(wst.ap, wst.buf, self.din[nm].rearrange("(c p) n -> p c n", p=128), self.dbuf[nm])
            for c in range(3):
                self.ts(wq[i].ap[:, c, :], wst.ap[:, c, :], self.vT.ap[:, c, 20:21], None, ALU.mult, None, [wst.buf, self.vT.buf], [wq[i].buf])
        P.barrier()
        A.release(m1)
        tab_r = Ring(A, "qtab", [96, 2, 512], F32, 2)
        kt_r = Ring(A, "KT1h", [96, 8, 1], BF16, 1)
        kt_r.items = [A.alloc_at(f"KT1h{i}", [96, NKEY], BF16, self.mixT.off + i * NKEY * 2) for i in range(2)]
        v_r = Ring(A, "V1h", [128, NKT, 65], BF16, 2)
        q_r = Ring(A, "Q1h", [96, TOWN], BF16, 2)
        e_r = Ring(A, "mE", [128, 1024], BF16, 4)
        t_r = Ring(A, "mT", [96, 512], F32, 4)
        oc_r = Ring(A, "mO", [64, 512], F32, 2)
        rz_r = Ring(A, "mZ", [1, 512], F32, 2)
        on_r = Ring(A, "mN", [64, 512], BF16, 3)
        scale = 96.0 ** -0.5
        heads = [(kt_r.items[h % 2], v_r.next(), q_r.next()) for h in range(8)]

        for i in range(2):
            KTb = kt_r.items[i]
            for rnk in range(4):
                pap, pbuf = self.lat_piece(rnk, 2)
                self.load(KTb.ap[64:96, rnk * TOWN:(rnk + 1) * TOWN], KTb.buf, pap, pbuf)
            self.load(KTb.ap[64:96, SEQ:NKEY], KTb.buf, self.din["LATC"][256:288, :], self.dbuf["LATC"])

        def prep_head(h):
            KT, VH, QH = heads[h]
            for part in range(3):
                c0, c1 = part * 2816, (part + 1) * 2816
                self.load(KT.ap[0:64, c0:c1], KT.buf, KN[h * 64:(h + 1) * 64, c0:c1], self.dbuf["KN1"], q="act")
                t0, t1_ = part * 22, (part + 1) * 22
                self.load(VH.ap[:, t0:t1_, :], VH.buf, V1[h, :, t0:t1_, :], self.dbuf["V1"])
            for qb in range(4):
                qs = slice(qb * 512, (qb + 1) * 512)
                pa, pab = self.bank(6)
                pb, pbb = self.bank(7)
                for c in range(3):
                    self.mm(pa[0:96, :], wq[0].ap[:, c, h * 96:(h + 1) * 96], self.cqn.ap[:, c, qs], c == 0, c == 2, [wq[0].buf, self.cqn.buf], [pab])
                for c in range(3):
                    self.mm(pb[0:96, :], wq[1].ap[:, c, h * 96:(h + 1) * 96], self.cqn.ap[:, c, qs], c == 0, c == 2, [wq[1].buf, self.cqn.buf], [pbb])
                t1, t2 = t_r.next(), t_r.next()
                tab = tab_r.next()
                self.load(tab.ap[:, 0, :], tab.buf, self.din["ropeq_c_own"][:, qs], self.dbuf["ropeq_c_own"])
                self.load(tab.ap[:, 1, :], tab.buf, self.din["ropeq_s_own"][:, qs], self.dbuf["ropeq_s_own"])
                self.tt(t1.ap, pa[0:96, :], tab.ap[:, 0, :], ALU.mult, [pab, tab.buf], [t1.buf])
                self.tt(t2.ap, pb[0:96, :], tab.ap[:, 1, :], ALU.mult, [pbb, tab.buf], [t2.buf])
                self.tt(QH.ap[:, qs], t1.ap, t2.ap, ALU.add, [t1.buf, t2.buf], [QH.buf])

        def post(h, qs, ob, obb):
            oc, rz, on = oc_r.next(), rz_r.next(), on_r.next()
            self.cp(oc.ap, ob[0:64, :], [obb], [oc.buf], eng="act")
            self.recip(rz.ap, ob[64:65, :], [obb], [rz.buf])
            zb, zbb = self.bank(6)
            self.mm(zb[0:64, :], self.ones_f.ap[0:1, 0:64], rz.ap[0:1, :], True, True, [self.ones_f.buf, rz.buf], [zbb])
            self.tt(on.ap, oc.ap, zb[0:64, :], ALU.mult, [oc.buf, zbb], [on.buf])
            pr, half = h // 2, h % 2
            self.store(MIX1[pr, 64 * half:64 * half + 64, qs], self.dbuf["MIX1"], on.ap, on.buf)

        steps = []
        sidx = [0]
        oidx = [0]
        prep_head(0)
        for h in range(8):
            KT, VH, QH = heads[h]
            for qb in range(4):
                qs = slice(qb * 512, (qb + 1) * 512)
                ob, obb = self.bank(4 + oidx[0] % 2)
                oidx[0] += 1
                for kp in range(NKT // 2):
                    def front(h=h, KT=KT, QH=QH, qs=qs, kp=kp, qb=qb):
                        if qb == 0 and kp == 2 and h + 1 < 8:
                            prep_head(h + 1)
                        sap, sbufs = self.pp[sidx[0] % 2]
                        sidx[0] += 1
                        for j in range(2):
                            kt = 2 * kp + j
                            self.mm(sap[:, j * 512:(j + 1) * 512], KT.ap[0:96, kt * 128:(kt + 1) * 128], QH.ap[0:96, qs], True, True,
                                    [KT.buf, QH.buf], [sbufs[j]])
                        e = e_r.next()
                        self.act(e.ap, sap, AF.Exp, sbufs, [e.buf], scale=scale)
                        return e

                    def back(e, h=h, VH=VH, qs=qs, kp=kp, ob=ob, obb=obb):
                        for j in range(2):
                            kt = 2 * kp + j
                            self.mm(ob[0:65, :], VH.ap[:, kt, :], e.ap[:, j * 512:(j + 1) * 512], (kp == 0 and j == 0),
                                    (kp == NKT // 2 - 1 and j == 1), [VH.buf, e.buf], [obb])
                        if kp == NKT // 2 - 1:
                            post(h, qs, ob, obb)
                    steps.append((front, back))
        prev = None
        for (fr, bk) in steps:
            e = fr()
            if prev is not None:
                prev[0](prev[1])
            prev = (bk, e)
        prev[0](prev[1])
        P.barrier()
        A.release(m0)

    def phase_load_h(self):
        A, P = self.A, self.P
        self.mixT = A.alloc("mixT", [128, KC, NOWN], BF16)
        self.hT = A.alloc("hT", [128, KC, NOWN], F32)
        m0 = A.mark()
        xin_r = Ring(A, "xin", [128, 4, D], F32, 2)
        for blk in range(5):
            ctx = blk == 4
            ntok = 256 if ctx else 512
            tok0 = blk * 512
            r = 1 if ctx else 0
            xin = xin_r.next()
            self.load_xT(self.din["h_own"][tok0:tok0 + ntok, :], self.dbuf["h_own"], ntok, xin,
                         V(self.mixT.ap[:, :, tok0:tok0 + ntok], self.mixT.buf), self.sc1[1], self.sh1[1], r,
                         hT=self.hT, h_off=tok0, pbanks=(0, 1))
        P.barrier()
        A.release(m0)


def build(mode):
    kb = KB(mode)
    kb.declare_inputs()
    evs = []
    if mode in ("A", "F"):
        kb.phase_const(); kb.phase_mod()
        kb.phase_k0()
        kb.phase_q("na"); kb.phase_kna(); kb.phase_na()
        kb.phase_q("diff"); kb.phase_diff()
        kb.phase_o(0, "ev_w_out", True)
        kb.phase_f(0, kb.G3, kb.B3)
    if mode == "A":
        kb.outp("out_h", [NOWN, D])
        evs += kb.phase_out("out_h", NOWN)
        kb.P.barrier()
        kb.phase_p1(lat_only=True)
        kb.outp("lat_out", [288, TOWN], BF16)
        for (nm, r0, r1) in (("LATA", 0, 128), ("LATB", 128, 256), ("LATK", 256, 288)):
            evs.append(kb.P.dma("sp", (lambda o, i: lambda e: e.dma_start(out=o, in_=i))(kb.din["lat_out"][r0:r1, :], kb.din[nm]),
                                reads=[kb.dbuf[nm]], writes=[kb.dbuf["lat_out"]]))
    if mode == "B":
        kb.inp("h_own", [NOWN, D])
        kb.phase_const(); kb.phase_mod()
        kb.phase_load_h()
    if mode in ("B", "F"):
        kb.phase_p1(mid=lambda: kb.exchange(mode))
        kb.phase_k1()
        kb.phase_mla()
        kb.phase_o(1, "od_w_out", False, nblk=4, mix_dram="MIX1")
        kb.phase_f(1, None, None, ntb=4)
        kb.outp("out", [TOWN, D])
        evs += kb.phase_out("out", TOWN)
    kb.P.emit(final_events=evs)
    return kb


def rope_tables():
    t = np.arange(SEQ); row = (t // 64).astype(np.float32); col = (t % 64).astype(np.float32)
    inv32 = np.power(np.float32(10000.0), -np.arange(0, 32, 2, dtype=np.float32) / np.float32(32)).astype(np.float32)
    cd = np.zeros((64, SEQ), np.float32); sd = np.zeros((64, SEQ), np.float32)
    for d in range(64):
        pos = row if d < 32 else col
        j = d % 16
        ang = (pos * inv32[j]).astype(np.float32)
        cd[d] = np.cos(ang); s = np.sin(ang)
        sd[d] = -s if (d % 32) < 16 else s
    cd = np.concatenate([cd, cd], 0); sd = np.concatenate([sd, sd], 0)
    inv16 = np.power(np.float32(10000.0), -np.arange(0, 16, 2, dtype=np.float32) / np.float32(16)).astype(np.float32)
    cm = np.zeros((32, SEQ), np.float32); sm = np.zeros((32, SEQ), np.float32)
    for d in range(32):
        pos = row if d < 16 else col
        j = d % 8
        ang = (pos * inv16[j]).astype(np.float32)
        cm[d] = np.cos(ang); s = np.sin(ang)
        sm[d] = -s if (d % 16) < 8 else s
    cq = np.concatenate([np.ones((64, SEQ), np.float32), cm], 0)
    sq = np.concatenate([np.zeros((64, SEQ), np.float32), sm], 0)
    return cd, sd, cq, sq

def swap_perm(n, blk):
    idx = np.arange(n); h = blk // 2
    return np.where((idx % blk) < h, idx + h, idx - h)

def na_tables(rpb, q0):
    ck = np.arange(64)[:, None]; cq = np.arange(64)[None, :]
    cs = np.clip(cq - 8, 0, 48)
    colvalid = (ck >= cs) & (ck < cs + 16)
    coff = np.clip(ck - cq + 15, 0, 30)
    tb = np.zeros((2, 64, 8, 14, 64), np.float32)
    for a in range(2):
        for s in range(14):
            dr = 6 - s + a
            for h in range(8):
                blk = rpb[h, dr + 7][coff]
                tb[a, :, h, s, :] = np.where(colvalid, blk, np.float32(NEG))
    tb = tb.reshape(128, 8 * 14 * 64)
    qm = np.zeros((64, TOWN), np.float32); km = np.zeros((64, NNA), np.float32)
    r0 = q0 // 64
    iq = np.arange(TOWN); rq = r0 + iq // 64
    rs = np.clip(rq - 4, 0, 120)
    for rho in range(44):
        rk = r0 - 6 + rho
        valid = (rk >= 0) & (rk < 128) & (rk >= rs) & (rk < rs + 8)
        qm[rho] = np.where(valid, 0.0, NEG)
        km[rho, rho * 64:(rho + 1) * 64] = 1.0
    return tb, qm.astype(ml_dtypes.bfloat16), km.astype(ml_dtypes.bfloat16)

def core_inputs(INP, core, tabs):
    b, q0 = core // 4, (core % 4) * TOWN
    cd, sd, cq, sq = tabs
    x = INP['x'][b]
    xna = np.zeros((NNA, D), np.float32)
    lo, hi = q0 - 384, q0 - 384 + NNA
    slo, shi = max(lo, 0), min(hi, SEQ)
    xna[slo - lo:shi - lo] = x[slo:shi]
    v = np.zeros((24, 1024), np.float32)
    for l in range(2):
        v[4*l+0] = INP['ln_mix_g'][l]; v[4*l+1] = INP['ln_mix_b'][l]; v[4*l+2] = INP['ln_ffn_g'][l]; v[4*l+3] = INP['ln_ffn_b'][l]
        v[8+6*l:14+6*l] = INP['mod_b'][l].reshape(6, 1024)
    v[20, :384] = INP['mla_q_norm_g'][0]; v[21, :256] = INP['mla_kv_norm_g'][0]
    v[22] = INP['c'][b]; v[23] = INP['c_ctx']
    ev = INP['ev_w_in'][0]
    p64 = swap_perm(1024, 32)
    tb, qm, km = na_tables(INP['na_rpb'][0], q0)
    d = {
        "xall": x, "ctxb": INP['ctx'][b], "xown": x[q0:q0 + TOWN], "xna": xna, "vecs": v,
        "ident": np.eye(128, dtype=np.float32), "mod_w": INP['mod_w'],
        "ev_w_in": ev, "ev_w_in_sw": np.ascontiguousarray(ev[:, :1024][:, p64]), "ev_w_out": INP['ev_w_out'][0],
        "ffn_w_in": INP['ffn_w_in'], "ffn_w_out": INP['ffn_w_out'],
        "diff_lambda": INP['diff_lambda'][0].reshape(256), "diff_subln_g": INP['diff_subln_g'][0],
        "rope_cd": cd, "rope_sd": sd, "rope_cd_own": np.ascontiguousarray(cd[:, q0:q0 + TOWN]),
        "rope_sd_own": np.ascontiguousarray(sd[:, q0:q0 + TOWN]),
        "na_tb": tb, "na_qm": qm, "na_km": km,
    }
    return d


def core_inputs_l1(INP, core, tabs):
    b, q0 = core // 4, (core % 4) * TOWN
    cd, sd, cq, sq = tabs
    od = INP['od_w_in'][0]
    p32 = swap_perm(32, 16)
    wuq = INP['mla_w_uq'][0]
    wuq_sw = np.zeros_like(wuq)
    for h in range(8):
        c0 = h * 96 + 64
        wuq_sw[:, c0:c0 + 32] = wuq[:, c0:c0 + 32][:, p32]
    return {
        "od_w_in": od, "od_w_in_krsw": np.ascontiguousarray(od[:, 640:672][:, p32]), "od_w_out": INP['od_w_out'][0],
        "mla_w_uq": wuq, "mla_w_uq_sw": wuq_sw, "mla_w_ukv": INP['mla_w_ukv'][0],
        "gmlp_ln_g": INP['gmlp_ln_g'][0], "gmlp_ln_b": INP['gmlp_ln_b'][0], "gmlp_ws": INP['gmlp_ws'][0],
        "gmlp_b": INP['gmlp_b'][0].reshape(512),
        "ropeq_c_own": np.ascontiguousarray(cq[:, q0:q0 + TOWN]), "ropeq_s_own": np.ascontiguousarray(sq[:, q0:q0 + TOWN]),
    }


MODE = "F"
_CACHE = {}


def _get(mode):
    if mode not in _CACHE:
        _CACHE[mode] = build(mode)
    return _CACHE[mode]


def kernel(**inputs):
    INP = {k: np.asarray(v) for k, v in inputs.items()}
    tabs = rope_tables()
    per_core = []
    for core in range(8):
        d = core_inputs(INP, core, tabs)
        d.update(core_inputs_l1(INP, core, tabs))
        per_core.append(d)
    out = np.zeros((2, SEQ, D), np.float32)
    if MODE == "F":
        kb = _get("F")
        maps = [{k: v for k, v in d.items() if k in kb.din} for d in per_core]
        res = run_bass_kernel_spmd(kb.nc, maps, core_ids=list(range(8)))
        outs = [r["out"] for r in res.results]
    else:
        ka = _get("A")
        maps = [{k: v for k, v in d.items() if k in ka.din} for d in per_core]
        ra = run_bass_kernel_spmd(ka.nc, maps, core_ids=list(range(8))).results
        kbb = _get("B")
        maps = []
        for core in range(8):
            g = core // 4
            d = {k: v for k, v in per_core[core].items() if k in kbb.din}
            d["h_own"] = np.asarray(ra[core]["out_h"])
            d["latg"] = np.concatenate([np.asarray(ra[4 * g + r]["lat_out"]) for r in range(4)], 0)
            maps.append(d)
        rb = run_bass_kernel_spmd(kbb.nc, maps, core_ids=list(range(8))).results
        outs = [r["out"] for r in rb]
    for core in range(8):
        b, q0 = core // 4, (core % 4) * TOWN
        out[b, q0:q0 + TOWN] = np.asarray(outs[core])
    return out
```
